# Optimizing a Trainium2 kernel written in Bass

```python
import jax, jax.numpy as jnp
from jax import lax
import numpy as np

D_MODEL = 1024
BATCH = 4
SEQ = 4096
DEPTH = 4

CTX_LEN = 256
GRID_W = 64
HEAD_DIM = 64
N_HEADS_A = 8
N_KV_A = 2
N_HEADS_B = 8
N_KV_B = 2
WINDOW = 128
BLOCK = 128
D_FF = -(-8 * D_MODEL // (3 * 256)) * 256
ROPE_THETA = 10000.0
NORM_EPS = 1e-6
NEG_INF = -1e30
Q_A = N_HEADS_A * HEAD_DIM
KV_A = N_KV_A * HEAD_DIM
Q_B = N_HEADS_B * HEAD_DIM
KV_B = N_KV_B * HEAD_DIM
IN_SIZES = (Q_A, KV_A, KV_A, Q_B, KV_B, KV_B, D_MODEL, D_MODEL)
IN_COLS = Q_A + 2 * KV_A + Q_B + 2 * KV_B + 2 * D_MODEL

kernel_name = "hybrid_dit_window_dense_gqa_prefix"


def rms_norm(x, g):
    xf = x.astype(jnp.float32)
    y = xf * lax.rsqrt(jnp.mean(xf * xf, axis=-1, keepdims=True) + NORM_EPS)
    return (y * g.astype(jnp.float32)).astype(x.dtype)


def modulate(h, shift, scale):
    return h * (1 + scale) + shift


def axial_rope_tables(n_tokens):
    rows = n_tokens // GRID_W
    row = jnp.repeat(jnp.arange(rows, dtype=jnp.float32), GRID_W)
    col = jnp.tile(jnp.arange(GRID_W, dtype=jnp.float32), rows)
    axis_dims = HEAD_DIM // 2
    inv = ROPE_THETA ** (-jnp.arange(0, axis_dims, 2, dtype=jnp.float32) / axis_dims)
    ang = jnp.stack([row[:, None] * inv, col[:, None] * inv], axis=1)
    return jnp.cos(ang), jnp.sin(ang)


def apply_rope(x, cos, sin):
    xr = x.astype(jnp.float32).reshape(x.shape[:-1] + (2, 2, HEAD_DIM // 4))
    x1, x2 = xr[..., 0, :], xr[..., 1, :]
    cs, sn = cos[:, None], sin[:, None]
    out = jnp.stack([x1 * cs - x2 * sn, x2 * cs + x1 * sn], axis=-2)
    return out.reshape(x.shape).astype(x.dtype)


def split_in_proj(z):
    pts = np.cumsum(IN_SIZES)[:-1].tolist()
    qa, ka, va, qb, kb, vb, ga, gb = jnp.split(z, pts, axis=-1)
    lead = z.shape[:-1]
    heads = lambda t, n: t.reshape(lead + (n, HEAD_DIM))
    return (heads(qa, N_HEADS_A), heads(ka, N_KV_A), heads(va, N_KV_A),
            heads(qb, N_HEADS_B), heads(kb, N_KV_B), heads(vb, N_KV_B), ga, gb)


def softmax_with_sink(s, sink_hg):
    sk = jnp.broadcast_to(sink_hg.astype(jnp.float32)[:, :, None, None], s.shape[:-1] + (1,))
    p = jax.nn.softmax(jnp.concatenate([s, sk], axis=-1), axis=-1)
    return p[..., :-1]


def window_attention_latent(q, k, v, kc, vc, sink):
    B, S = q.shape[:2]
    C = kc.shape[1]
    nb = S // BLOCK
    G = N_HEADS_A // N_KV_A
    scale = HEAD_DIM ** -0.5
    qb = q.reshape(B, nb, BLOCK, N_KV_A, G, HEAD_DIM)
    pad = ((0, 0), (BLOCK, BLOCK), (0, 0), (0, 0))
    kp = jnp.pad(k, pad).reshape(B, nb + 2, BLOCK, N_KV_A, HEAD_DIM)
    vp = jnp.pad(v, pad).reshape(B, nb + 2, BLOCK, N_KV_A, HEAD_DIM)
    kw = jnp.concatenate([kp[:, :-2], kp[:, 1:-1], kp[:, 2:]], axis=2)
    vw = jnp.concatenate([vp[:, :-2], vp[:, 1:-1], vp[:, 2:]], axis=2)
    s_win = jnp.einsum('bnqhgd,bnkhd->bnhgqk', qb, kw).astype(jnp.float32) * scale
    qpos = jnp.arange(nb)[:, None] * BLOCK + jnp.arange(BLOCK)[None, :]
    kpos = jnp.arange(nb)[:, None] * BLOCK - BLOCK + jnp.arange(3 * BLOCK)[None, :]
    rel = kpos[:, None, :] - qpos[:, :, None]
    valid = (jnp.abs(rel) <= WINDOW) & (kpos[:, None, :] >= 0) & (kpos[:, None, :] < S)
    s_win = jnp.where(valid[None, :, None, None], s_win, NEG_INF)
    s_ctx = jnp.einsum('bnqhgd,bchd->bnhgqc', qb, kc).astype(jnp.float32) * scale
    p = softmax_with_sink(jnp.concatenate([s_win, s_ctx], axis=-1), sink.reshape(N_KV_A, G)).astype(v.dtype)
    out = (jnp.einsum('bnhgqk,bnkhd->bnqhgd', p[..., :3 * BLOCK], vw)
           + jnp.einsum('bnhgqc,bchd->bnqhgd', p[..., 3 * BLOCK:], vc))
    return out.reshape(B, S, Q_A)


def full_attention_latent(q, k, v, kc, vc):
    B, S = q.shape[:2]
    nb = S // BLOCK
    G = N_HEADS_B // N_KV_B
    scale = HEAD_DIM ** -0.5
    kall = jnp.concatenate([k, kc], axis=1)
    vall = jnp.concatenate([v, vc], axis=1)
    qb = q.reshape(B, nb, BLOCK, N_KV_B, G, HEAD_DIM).transpose(1, 0, 2, 3, 4, 5)

    def one_block(qblk):
        s = jnp.einsum('bqhgd,bkhd->bhgqk', qblk, kall).astype(jnp.float32) * scale
        p = jax.nn.softmax(s, axis=-1).astype(vall.dtype)
        return jnp.einsum('bhgqk,bkhd->bqhgd', p, vall)

    out = lax.map(one_block, qb)
    return out.transpose(1, 0, 2, 3, 4, 5).reshape(B, S, Q_B)


def context_attention(q, k, v, n_kv, sink):
    B, C, H, _ = q.shape
    G = H // n_kv
    qg = q.reshape(B, C, n_kv, G, HEAD_DIM)
    s = jnp.einsum('bqhgd,bkhd->bhgqk', qg, k).astype(jnp.float32) * (HEAD_DIM ** -0.5)
    if sink is None:
        p = jax.nn.softmax(s, axis=-1)
    else:
        p = softmax_with_sink(s, sink.reshape(n_kv, G))
    out = jnp.einsum('bhgqk,bkhd->bqhgd', p.astype(v.dtype), v)
    return out.reshape(B, C, H * HEAD_DIM)


def merge_branches(ya, yb, ga, gb, w_pa, w_pb, w_o):
    m = jax.nn.sigmoid(ga) * (ya @ w_pa) + jax.nn.sigmoid(gb) * (yb @ w_pb)
    return m @ w_o


def swiglu(h, w_gate, w_up, w_down):
    return (jax.nn.silu(h @ w_gate) * (h @ w_up)) @ w_down


def setup_inputs(seed: int = 0) -> dict:
    key = jax.random.key(seed)
    ks = jax.random.split(key, 20)
    f32 = jnp.float32
    nrm = lambda k, shape, s: jax.random.normal(k, shape, f32) * s
    return {
        "x": nrm(ks[0], (BATCH, SEQ, D_MODEL), 1.0),
        "c": nrm(ks[1], (BATCH, D_MODEL), 1.0),
        "ctx": nrm(ks[2], (BATCH, CTX_LEN, D_MODEL), 1.0),
        "c_ctx": nrm(ks[3], (D_MODEL,), 1.0),
        "w_ada": nrm(ks[4], (DEPTH, D_MODEL, 6 * D_MODEL), 0.5 * D_MODEL ** -0.5),
        "b_ada": nrm(ks[5], (DEPTH, 6 * D_MODEL), 0.01),
        "norm1_g": 1.0 + nrm(ks[6], (DEPTH, D_MODEL), 0.02),
        "norm2_g": 1.0 + nrm(ks[7], (DEPTH, D_MODEL), 0.02),
        "w_in": nrm(ks[8], (DEPTH, D_MODEL, IN_COLS), D_MODEL ** -0.5),
        "q_norm_g": 1.0 + nrm(ks[9], (DEPTH, HEAD_DIM), 0.02),
        "k_norm_g": 1.0 + nrm(ks[10], (DEPTH, HEAD_DIM), 0.02),
        "sink_a": nrm(ks[11], (DEPTH, N_HEADS_A), 0.5),
        "w_proj_a": nrm(ks[12], (DEPTH, Q_A, D_MODEL), Q_A ** -0.5),
        "w_proj_b": nrm(ks[13], (DEPTH, Q_B, D_MODEL), Q_B ** -0.5),
        "w_out": nrm(ks[14], (DEPTH, D_MODEL, D_MODEL), D_MODEL ** -0.5),
        "w_ffn_gate": nrm(ks[15], (DEPTH, D_MODEL, D_FF), D_MODEL ** -0.5),
        "w_ffn_up": nrm(ks[16], (DEPTH, D_MODEL, D_FF), D_MODEL ** -0.5),
        "w_ffn_down": nrm(ks[17], (DEPTH, D_FF, D_MODEL), D_FF ** -0.5),
        "final_norm_g": 1.0 + nrm(ks[18], (D_MODEL,), 0.02),
    }


def reference(x, c, ctx, c_ctx, w_ada, b_ada, norm1_g, norm2_g, w_in, q_norm_g, k_norm_g, sink_a,
              w_proj_a, w_proj_b, w_out, w_ffn_gate, w_ffn_up, w_ffn_down, final_norm_g):
    cos, sin = axial_rope_tables(x.shape[1])
    silu_c = jax.nn.silu(c)
    silu_cc = jax.nn.silu(c_ctx)
    xc = ctx
    for l in range(DEPTH):
        last = l == DEPTH - 1
        mod = silu_c @ w_ada[l] + b_ada[l]
        mod_c = silu_cc @ w_ada[l] + b_ada[l]
        sh1, sc1, gt1, sh2, sc2, gt2 = jnp.split(mod[:, None, :], 6, axis=-1)
        csh1, csc1, cgt1, csh2, csc2, cgt2 = jnp.split(mod_c, 6, axis=-1)

        h = modulate(rms_norm(x, norm1_g[l]), sh1, sc1)
        hc = modulate(rms_norm(xc, norm1_g[l]), csh1, csc1)
        qa, ka, va, qb, kb, vb, ga, gb = split_in_proj(h @ w_in[l])
        qac, kac, vac, qbc, kbc, vbc, gac, gbc = split_in_proj(hc @ w_in[l])
        qb = rms_norm(qb, q_norm_g[l])
        kb = rms_norm(kb, k_norm_g[l])
        kbc = rms_norm(kbc, k_norm_g[l])
        ya = window_attention_latent(apply_rope(qa, cos, sin), apply_rope(ka, cos, sin), va, kac, vac, sink_a[l])
        yb = full_attention_latent(apply_rope(qb, cos, sin), apply_rope(kb, cos, sin), vb, kbc, vbc)
        x = x + gt1 * merge_branches(ya, yb, ga, gb, w_proj_a[l], w_proj_b[l], w_out[l])

        h2 = modulate(rms_norm(x, norm2_g[l]), sh2, sc2)
        x = x + gt2 * swiglu(h2, w_ffn_gate[l], w_ffn_up[l], w_ffn_down[l])

        if not last:
            qbc = rms_norm(qbc, q_norm_g[l])
            yac = context_attention(qac, kac, vac, N_KV_A, sink_a[l])
            ybc = context_attention(qbc, kbc, vbc, N_KV_B, None)
            xc = xc + cgt1 * merge_branches(yac, ybc, gac, gbc, w_proj_a[l], w_proj_b[l], w_out[l])
            h2c = modulate(rms_norm(xc, norm2_g[l]), csh2, csc2)
            xc = xc + cgt2 * swiglu(h2c, w_ffn_gate[l], w_ffn_up[l], w_ffn_down[l])
    return rms_norm(x, final_norm_g)
```

```python
import os
import numpy as np
from contextlib import ExitStack
DBGA = int(os.environ.get('DBGA', '99'))
import concourse.bass as bass
import concourse.mybir as mybir
from concourse.bass_utils import run_bass_kernel_spmd

F32 = mybir.dt.float32
BF16 = mybir.dt.bfloat16
ALU = mybir.AluOpType
AF = mybir.ActivationFunctionType

D = 1024
T = 4096
C = 256
NT = T + C
L = 4
HD = 64
DFF = 2816
P = 128
EPS = 1e-6
GRID_W = 64
NCORES = 8
BLOCKS = [(i * 512, 512, 0) for i in range(8)] + [(T, C, 1)]

QA0, QB0, KA0, KB0, V0, GA0, GB0, INW = 0, 512, 1024, 1152, 1280, 1536, 2560, 3584


class Res:
    __slots__ = ("name", "w", "r", "excl")

    def __init__(self, name):
        self.name = name
        self.w = None
        self.r = {}
        self.excl = False


class Tile:
    def __init__(self, t, name):
        self.t = t
        self.res = Res(name)
        self.dsem = None

    def __getitem__(self, idx):
        return self.t[idx]


class KB:
    def __init__(self, nc, es):
        self.nc = nc
        self.eng = {"pe": nc.tensor, "act": nc.scalar, "dve": nc.vector, "pool": nc.gpsimd, "sp": nc.sync}
        self.es = es
        self.sems = {k: [] for k in self.eng}
        self.cnt = {k: 0 for k in self.eng}
        self.seen = {k: {} for k in self.eng}
        self.dpool = [[es.enter_context(nc.semaphore(f"D{i}")), 0, i] for i in range(48)]
        self.dfree = list(range(48))
        self.dram_res = {}
        self.uid = 0

    def tile(self, es, name, shape, dtype):
        self.uid += 1
        t = es.enter_context(self.nc.sbuf_tensor(f"{name}_{self.uid}", list(shape), dtype))
        tl = Tile(t, name)
        es.callback(self._release, tl)
        return tl

    def _release(self, tl):
        if tl.dsem is not None:
            self.dfree.append(tl.dsem[2])
            tl.dsem = None

    def _dsem(self, tl):
        if tl.dsem is None:
            tl.dsem = self.dpool[self.dfree.pop(0)]
        return tl.dsem

    def dres(self, name, b):
        key = (name, b)
        if key not in self.dram_res:
            self.dram_res[key] = Res(f"{name}[{b}]")
        return self.dram_res[key]

    EPOCH = 1500

    def _etok(self, e):
        cnt = self.cnt[e]
        ep = (cnt - 1) // self.EPOCH
        while len(self.sems[e]) <= ep:
            self.sems[e].append(self.es.enter_context(self.nc.semaphore(f"S_{e}_{len(self.sems[e])}")))
        return (e, self.sems[e][ep], (cnt - ep * self.EPOCH, cnt))

    def _wait(self, e, tok):
        if tok is None:
            return
        key, sem, val = tok
        if key == e and e == "pe":
            return
        if isinstance(val, list):
            lval = gval = val[1]
        else:
            lval, gval = val
        if self.seen[e].get(key, 0) >= gval:
            return
        self.eng[e].wait_ge(sem, lval)
        self.seen[e][key] = gval

    def _deps(self, e, reads, writes):
        for r in reads:
            self._wait(e, r.w)
        for w in writes:
            self._wait(e, w.w)
            for t in list(w.r.values()):
                self._wait(e, t)

    def _mark(self, tok, reads, writes):
        for r in reads:
            r.r[tok[0]] = tok
        for w in writes:
            w.w = tok
            w.r = {}

    @staticmethod
    def _res(xs):
        return [x.res if isinstance(x, Tile) else x for x in xs]

    def op(self, e, fn, reads=(), writes=()):
        reads = self._res(reads)
        writes = self._res(writes)
        writes = writes + [r for r in reads if r.excl and r not in writes]
        self._deps(e, reads, writes)
        ins = fn(self.eng[e])
        self.cnt[e] += 1
        tok = self._etok(e)
        ins.then_inc(tok[1], 1)
        self._mark(tok, reads, writes)

    def dma(self, q, out, in_, sb, reads=(), writes=(), **kw):
        reads = self._res(reads)
        writes = self._res(writes)
        self._deps(q, reads, writes)
        ds = self._dsem(sb)
        ins = self.eng[q].dma_start(out=out, in_=in_, **kw)
        ds[1] += 16
        ins.then_inc(ds[0], 16)
        self._mark(("d%d" % ds[2], ds[0], ds), reads, writes)

    def barrier(self):
        toks = [self._etok(k) for k in self.eng if self.cnt[k] > 0]
        toks += [("d%d" % d[2], d[0], d) for d in self.dpool if d[1] > 0]
        for e in self.eng:
            for t in toks:
                self._wait(e, t)


def build(layers, final_norm, debug=False, stop_after=None):
    nc = bass.Bass("TRN2", target_bir_lowering=False)
    es = ExitStack()

    def din(name, shape, dt=F32):
        return nc.dram_tensor(name, list(shape), dt, kind="ExternalInput").ap()

    scr_kind = "ExternalOutput" if debug else "Internal"

    def dscr(name, shape, dt):
        return nc.dram_tensor(name, list(shape), dt, kind=scr_kind).ap()

    x_d = din("x", [NT, D])
    cc_d = din("cc", [2, D])
    wada_d = din("w_ada", [L, D, 6 * D])
    bada_d = din("b_ada", [L, 6 * D])
    small_d = din("small", [L, P, 32])
    sink_d = din("sink", [L, 8])
    win_d = din("w_in", [L, D, INW])
    wpa_d = din("w_pa", [L, 512, D])
    wpb_d = din("w_pb", [L, 512, D])
    wo_d = din("w_o", [L, D, D])
    wg_d = din("w_g", [L, D, DFF])
    wu_d = din("w_u", [L, D, DFF])
    wd_d = din("w_d", [L, DFF, D])
    gfin_d = din("gfin", [1, D])
    consts_d = din("consts", [P, 640])
    sel_d = din("sel", [2, 256])
    cos_d = din("cosT", [P, T])
    sin_d = din("sinT", [P, T])

    if final_norm:
        out_d = nc.dram_tensor("out", [T, D], F32, kind="ExternalOutput").ap()
        xres_d = dscr("xres", [NT, D], F32)
    else:
        xres_d = nc.dram_tensor("out", [NT, D], F32, kind="ExternalOutput").ap()
    qscr = dscr("qscr", [8, P, NT], BF16)
    kscr = dscr("kscr", [2, P, NT], BF16)
    vscr = dscr("vscr", [NT, 768], BF16)
    gscr = dscr("gscr", [16, P, NT], BF16)
    yscr = dscr("yscr", [8, P, NT], BF16)
    hscr = dscr("hscr", [8, P, NT], BF16)

    k = KB(nc, es)
    ps = []
    for i in range(8):
        t = es.enter_context(nc.psum_tensor(f"ps{i}", [P, 512], F32))
        ps.append(Tile(t, f"ps{i}"))
        ps[-1].res.excl = True

    cst = k.tile(es, "cst", [P, 640], F32)
    ident = cst.t[:, 0:128]
    bones = cst.t[:, 256:384]
    permb = k.tile(es, "permb", [P, P], BF16)
    maskb = k.tile(es, "maskb", [P, 2, P], BF16)
    sel = k.tile(es, "sel", [2, 256], F32)
    ones1 = k.tile(es, "ones1", [1, P], F32)
    silu_row = k.tile(es, "silu_row", [2, D], F32)
    siluT = k.tile(es, "siluT", [P, 8, 2], F32)
    modT = k.tile(es, "modT", [P, 48, 2], F32)
    gm = k.tile(es, "gm", [P, 16, 2], F32)
    small = k.tile(es, "small", [P, 32], F32)
    gtb = [[k.tile(es, f"gtb{g}{s}", [P, D], F32) for s in range(2)] for g in range(2)]
    sinkrow = k.tile(es, "sinkrow", [1, 8], F32)
    sinkexp = k.tile(es, "sinkexp", [P, 8], F32)
    setup_sem_tile = cst

    k.dma("sp", cst.t[:, :], consts_d[:, :], cst, writes=[cst])
    k.dma("sp", sel.t[:, :], sel_d[:, :], sel, writes=[sel])
    k.op("dve", lambda e: e.tensor_copy(out=permb.t[:, :], in_=cst.t[:, 128:256]), reads=[cst], writes=[permb])
    k.op("dve", lambda e: e.tensor_copy(out=maskb.t[:, :, :], in_=cst.t[:, 384:640].rearrange("p (a b) -> p a b", b=P)),
         reads=[cst], writes=[maskb])
    k.op("dve", lambda e: e.memset(ones1.t[:, :], 1.0), writes=[ones1])
    k.dma("sp", silu_row.t[:, :], cc_d[:, :], silu_row, writes=[silu_row])
    k.op("act", lambda e: e.activation(out=silu_row.t[:, :], in_=silu_row.t[:, :], func=AF.Silu),
         reads=[silu_row], writes=[silu_row])

    def _tr_silu(e):
        ins = None
        for kc in range(8):
            ins = e.transpose(out=ps[0].t[:, kc * 2:kc * 2 + 2], in_=silu_row.t[0:2, kc * P:(kc + 1) * P],
                              identity=cst.t[0:2, 0:2])
        return ins
    k.op("pe", _tr_silu, reads=[silu_row, cst], writes=[ps[0]])
    k.op("dve", lambda e: e.tensor_copy(out=siluT.t[:, :, :], in_=ps[0].t[:, 0:16].rearrange("p (a b) -> p a b", b=2)),
         reads=[ps[0]], writes=[siluT])

    for bi, (r0, n, s) in enumerate(BLOCKS):
        k.dma("sp", xres_d[r0:r0 + n, :], x_d[r0:r0 + n, :], setup_sem_tile, writes=[k.dres("xres", bi)])

    def wload(tl, dst_fn, src2d, ncols, maxc=2048):
        c0 = 0
        while c0 < ncols:
            c1 = min(ncols, c0 + maxc)
            k.dma("pool", dst_fn(c0, c1), src2d[:, c0:c1], tl, writes=[tl])
            c0 = c1

    def rstd_from_ss(ss, lnv, rstd, nt, inv_n):
        k.op("act", lambda e: e.activation(out=lnv.t[:, 0:nt], in_=ss.t[:, 0:nt], func=AF.Ln, scale=inv_n, bias=EPS),
             reads=[ss], writes=[lnv])
        k.op("act", lambda e: e.activation(out=rstd.t[:, 0:nt], in_=lnv.t[:, 0:nt], func=AF.Exp, scale=-0.5),
             reads=[lnv], writes=[rstd])

    def norm_to_hT(xs_tile, hT, nt, s, gmoff, shoff, psA, psB):
        for c in range(8):
            pst = psA if c % 2 == 0 else psB

            def _tr(e, c=c, pst=pst):
                ins = None
                for t in range(nt):
                    ins = e.transpose(out=pst.t[:, t * P:(t + 1) * P], in_=xs_tile.t[:, t, c * P:(c + 1) * P],
                                      identity=ident)
                return ins
            k.op("pe", _tr, reads=[xs_tile, cst], writes=[pst])
            k.op("act", lambda e, c=c, pst=pst: e.activation(
                out=hT.t[:, c, 0:nt * P], in_=pst.t[:, 0:nt * P], func=AF.Identity,
                scale=gm.t[:, gmoff + c, s:s + 1], bias=modT.t[:, shoff + c, s:s + 1]),
                reads=[pst, gm, modT], writes=[hT])

    def mm(pst, out_ap, pairs, reads, start=True, stop=True):
        def _f(e):
            ins = None
            n_ = len(pairs)
            for i, (l_, r_) in enumerate(pairs):
                ins = e.matmul(out_ap, lhsT=l_, rhs=r_, start=(start and i == 0), stop=(stop and i == n_ - 1))
            return ins
        k.op("pe", _f, reads=reads, writes=[pst])

    def phase_mod(l):
        pes = ExitStack()
        wa = [k.tile(pes, f"wada{i}", [P, 8, 512], F32) for i in range(2)]
        modrow = k.tile(pes, "modrow", [2, 6 * D], F32)
        bada2 = k.tile(pes, "bada2", [2, 6 * D], F32)
        k.dma("sp", bada2.t[0:1, :], bada_d[l:l + 1, :], bada2, writes=[bada2])
        k.dma("sp", bada2.t[1:2, :], bada_d[l:l + 1, :], bada2, writes=[bada2])
        k.dma("sp", small.t[:, :], small_d[l, :, :], small, writes=[small])
        k.dma("sp", sinkrow.t[:, :], sink_d[l:l + 1, :], sinkrow, writes=[sinkrow])
        wsrc = wada_d[l, :, :].rearrange("(kc p) n -> p kc n", p=P)

        def ld(cb):
            k.dma("sp", wa[cb % 2].t[:, :, :], wsrc[:, :, cb * 512:(cb + 1) * 512], wa[cb % 2], writes=[wa[cb % 2]])
        ld(0)
        for cb in range(12):
            if cb + 1 < 12:
                ld(cb + 1)
            w_ = wa[cb % 2]
            pst = ps[cb % 2]
            mm(pst, pst.t[0:2, :], [(siluT.t[:, kc, :], w_.t[:, kc, :]) for kc in range(8)], reads=[siluT, w_])
            k.op("dve", lambda e, cb=cb, pst=pst: e.tensor_tensor(
                out=modrow.t[:, cb * 512:(cb + 1) * 512], in0=pst.t[0:2, :], in1=bada2.t[:, cb * 512:(cb + 1) * 512],
                op=ALU.add), reads=[pst, bada2], writes=[modrow])
        for off in (D, 4 * D):
            k.op("dve", lambda e, off=off: e.tensor_scalar(out=modrow.t[:, off:off + D], in0=modrow.t[:, off:off + D],
                                                           scalar1=1.0, scalar2=None, op0=ALU.add),
                 reads=[modrow], writes=[modrow])

        def _tr(e):
            ins = None
            for j in range(48):
                ins = e.transpose(out=ps[2].t[:, j * 2:j * 2 + 2], in_=modrow.t[0:2, j * P:(j + 1) * P],
                                  identity=cst.t[0:2, 0:2])
            return ins
        k.op("pe", _tr, reads=[modrow, cst], writes=[ps[2]])
        k.op("dve", lambda e: e.tensor_copy(out=modT.t[:, :, :], in_=ps[2].t[:, 0:96].rearrange("p (a b) -> p a b", b=2)),
             reads=[ps[2]], writes=[modT])
        for s in range(2):
            k.op("dve", lambda e, s=s: e.tensor_tensor(out=gm.t[:, 0:8, s], in0=modT.t[:, 8:16, s], in1=small.t[:, 0:8],
                                                       op=ALU.mult), reads=[modT, small], writes=[gm])
            k.op("dve", lambda e, s=s: e.tensor_tensor(out=gm.t[:, 8:16, s], in0=modT.t[:, 32:40, s], in1=small.t[:, 8:16],
                                                       op=ALU.mult), reads=[modT, small], writes=[gm])
        i = 0
        for g, off in enumerate((2 * D, 5 * D)):
            for s in range(2):
                for hf in range(2):
                    pst = ps[3 + (i % 2)]
                    i += 1
                    mm(pst, pst.t[:, :], [(sel.t[0:2, s * P:(s + 1) * P], modrow.t[0:2, off + hf * 512:off + (hf + 1) * 512])],
                       reads=[sel, modrow])
                    k.op("dve", lambda e, g=g, s=s, hf=hf, pst=pst: e.tensor_copy(
                        out=gtb[g][s].t[:, hf * 512:(hf + 1) * 512], in_=pst.t[:, :]), reads=[pst], writes=[gtb[g][s]])
        mm(ps[5], ps[5].t[:, 0:8], [(ones1.t[0:1, :], sinkrow.t[0:1, :])], reads=[ones1, sinkrow])
        k.op("act", lambda e: e.activation(out=sinkexp.t[:, :], in_=ps[5].t[:, 0:8], func=AF.Exp),
             reads=[ps[5]], writes=[sinkexp])
        k.barrier()
        pes.close()

    def phase_A(l):
        pes = ExitStack()
        w = k.tile(pes, "w_in", [P, 8, INW], BF16)
        for kc in range(8):
            wload(w, lambda c0, c1, kc=kc: w.t[:, kc, c0:c1], win_d[l, kc * P:(kc + 1) * P, :], INW)
        xt = [k.tile(pes, f"xt{i}", [P, 4, D], F32) for i in range(2)]
        cs = [k.tile(pes, f"cs{i}", [P, 2, 512], F32) for i in range(2)]
        hT = k.tile(pes, "hT", [P, 8, 512], BF16)
        junk = k.tile(pes, "junk", [P, D], F32)
        ss = k.tile(pes, "ss", [P, 4], F32)
        lnv = k.tile(pes, "lnv", [P, 4], F32)
        rstd = k.tile(pes, "rstd", [P, 4], F32)
        zb = k.tile(pes, "zb", [P, 512], BF16)
        sq = k.tile(pes, "sq", [P, 512], F32)
        lnq = k.tile(pes, "lnq", [P, 512], F32)
        rq = k.tile(pes, "rq", [P, 512], F32)
        t1 = k.tile(pes, "t1", [P, 512], F32)
        t2 = k.tile(pes, "t2", [P, 512], F32)
        qout = k.tile(pes, "qout", [P, 8, 512], BF16)
        kout = k.tile(pes, "kout", [P, 2, 512], BF16)
        vout = k.tile(pes, "vout", [P, 4, 4, 192], BF16)
        gout = k.tile(pes, "gout", [P, 16, 512], BF16)
        k.op("pool", lambda e: e.memset(vout.t[:, :, :, :], 1.0), writes=[vout])

        def load(bi):
            r0, n, s = BLOCKS[bi]
            nt = n // P
            b = bi % 2
            k.dma("sp", xt[b].t[:, 0:nt, :], xres_d[r0:r0 + n, :].rearrange("(t p) d -> p t d", p=P), xt[b],
                  reads=[k.dres("xres", bi)], writes=[xt[b]])
            if s == 0:
                k.dma("sp", cs[b].t[:, 0, :], cos_d[:, r0:r0 + n], cs[b], writes=[cs[b]])
                k.dma("sp", cs[b].t[:, 1, :], sin_d[:, r0:r0 + n], cs[b], writes=[cs[b]])

        load(0)
        for bi, (r0, n, s) in enumerate(BLOCKS):
            if bi + 1 < len(BLOCKS):
                load(bi + 1)
            nt = n // P
            X = xt[bi % 2]
            CS = cs[bi % 2]
            rope = (s == 0)
            for t in range(nt):
                k.op("act", lambda e, t=t: e.activation(out=junk.t[:, :], in_=X.t[:, t, :], func=AF.Square),
                     reads=[X], writes=[junk])
                k.op("dve", lambda e, t=t: e.tensor_reduce(out=ss.t[:, t:t + 1], in_=junk.t[:, :], axis=mybir.AxisListType.X,
                                                           op=ALU.add), reads=[junk], writes=[ss])
            rstd_from_ss(ss, lnv, rstd, nt, 1.0 / D)
            for t in range(nt):
                k.op("dve", lambda e, t=t: e.tensor_scalar(out=X.t[:, t, :], in0=X.t[:, t, :], scalar1=rstd.t[:, t:t + 1],
                                                           scalar2=None, op0=ALU.mult), reads=[X, rstd], writes=[X])
            if DBGA < 2:
                break
            norm_to_hT(X, hT, nt, s, 0, 0, ps[0], ps[1])
            if DBGA < 3:
                break

            chunks = [(QA0 + j * P, qout, j, False, None) for j in range(4)]
            chunks += [(QB0 + j * P, qout, 4 + j, True, 16) for j in range(4)]
            chunks += [(KA0, kout, 0, False, None), (KB0, kout, 1, True, 18)]
            for ci, (col, dst, dj, isB, gcol) in enumerate(chunks):
                pz = ps[2 + (ci % 2)]
                mm(pz, pz.t[:, 0:n], [(w.t[:, kc, col:col + P], hT.t[:, kc, 0:n]) for kc in range(8)], reads=[w, hT])
                if isB:
                    k.op("act", lambda e, pz=pz: e.activation(out=sq.t[:, 0:n], in_=pz.t[:, 0:n], func=AF.Square),
                         reads=[pz], writes=[sq])
                    mm(ps[4], ps[4].t[:, 0:n], [(bones, sq.t[:, 0:n])], reads=[cst, sq])
                    k.op("act", lambda e: e.activation(out=lnq.t[:, 0:n], in_=ps[4].t[:, 0:n], func=AF.Ln,
                                                       scale=1.0 / HD, bias=EPS), reads=[ps[4]], writes=[lnq])
                    k.op("act", lambda e: e.activation(out=rq.t[:, 0:n], in_=lnq.t[:, 0:n], func=AF.Exp, scale=-0.5),
                         reads=[lnq], writes=[rq])
                g0 = small.t[:, gcol:gcol + 1] if isB else 1.0
                g1 = small.t[:, gcol + 1:gcol + 2] if isB else 1.0
                if rope:
                    k.op("act", lambda e, pz=pz: e.activation(out=zb.t[:, 0:n], in_=pz.t[:, 0:n], func=AF.Copy),
                         reads=[pz], writes=[zb])
                    mm(ps[5], ps[5].t[:, 0:n], [(permb.t[:, :], zb.t[:, 0:n])], reads=[permb, zb])
                    k.op("dve", lambda e, pz=pz, g0=g0: e.scalar_tensor_tensor(
                        out=t1.t[:, 0:n], in0=pz.t[:, 0:n], scalar=g0, in1=CS.t[:, 0, 0:n], op0=ALU.mult, op1=ALU.mult),
                        reads=[pz, CS, small], writes=[t1])
                    k.op("dve", lambda e, g1=g1: e.scalar_tensor_tensor(
                        out=t2.t[:, 0:n], in0=ps[5].t[:, 0:n], scalar=g1, in1=CS.t[:, 1, 0:n], op0=ALU.mult, op1=ALU.mult),
                        reads=[ps[5], CS, small], writes=[t2])
                    if isB:
                        k.op("pool", lambda e: e.tensor_tensor(out=t1.t[:, 0:n], in0=t1.t[:, 0:n], in1=t2.t[:, 0:n],
                                                               op=ALU.add), reads=[t1, t2], writes=[t1])
                        k.op("dve", lambda e, dst=dst, dj=dj: e.tensor_tensor(
                            out=dst.t[:, dj, 0:n], in0=t1.t[:, 0:n], in1=rq.t[:, 0:n], op=ALU.mult),
                            reads=[t1, rq], writes=[dst])
                    else:
                        k.op("pool", lambda e, dst=dst, dj=dj: e.tensor_tensor(
                            out=dst.t[:, dj, 0:n], in0=t1.t[:, 0:n], in1=t2.t[:, 0:n], op=ALU.add),
                            reads=[t1, t2], writes=[dst])
                else:
                    if isB:
                        k.op("dve", lambda e, pz=pz, g0=g0, dst=dst, dj=dj: e.scalar_tensor_tensor(
                            out=dst.t[:, dj, 0:n], in0=pz.t[:, 0:n], scalar=g0, in1=rq.t[:, 0:n], op0=ALU.mult,
                            op1=ALU.mult), reads=[pz, rq, small], writes=[dst])
                    else:
                        k.op("act", lambda e, pz=pz, dst=dst, dj=dj: e.activation(
                            out=dst.t[:, dj, 0:n], in_=pz.t[:, 0:n], func=AF.Copy), reads=[pz], writes=[dst])
            if DBGA < 4:
                break
            for t in range(nt):
                pv = ps[6 + (t % 2)]
                mm(pv, pv.t[:, 0:256], [(hT.t[:, kc, t * P:(t + 1) * P], w.t[:, kc, V0:V0 + 256]) for kc in range(8)],
                   reads=[w, hT])
                k.op("dve", lambda e, t=t, pv=pv: e.tensor_copy(
                    out=vout.t[:, t, :, 64:128], in_=pv.t[:, 0:256].rearrange("p (a b) -> p a b", b=64)),
                    reads=[pv], writes=[vout])
            if DBGA < 5:
                break
            for j in range(16):
                pg = ps[2 + (j % 2)]
                col = GA0 + j * P
                mm(pg, pg.t[:, 0:n], [(w.t[:, kc, col:col + P], hT.t[:, kc, 0:n]) for kc in range(8)], reads=[w, hT])
                k.op("act", lambda e, j=j, pg=pg: e.activation(out=gout.t[:, j, 0:n], in_=pg.t[:, 0:n], func=AF.Sigmoid),
                     reads=[pg], writes=[gout])
            if DBGA < 6:
                break
            k.dma("sp", qscr[:, :, r0:r0 + n].rearrange("j p n -> p j n"), qout.t[:, :, 0:n], qout,
                  reads=[qout], writes=[k.dres("q", bi)])
            k.dma("sp", kscr[:, :, r0:r0 + n].rearrange("j p n -> p j n"), kout.t[:, :, 0:n], kout,
                  reads=[kout], writes=[k.dres("k", bi)])
            k.dma("sp", vscr[r0:r0 + n, :].rearrange("(t p) f -> p t f", p=P),
                  vout.t[:, 0:nt, :, :].rearrange("p t a b -> p t (a b)"), vout, reads=[vout], writes=[k.dres("v", bi)])
            k.dma("sp", gscr[:, :, r0:r0 + n].rearrange("j p n -> p j n"), gout.t[:, :, 0:n], gout,
                  reads=[gout], writes=[k.dres("g", bi)])
        k.barrier()
        pes.close()

    def phase_attn(l, mixer, last):
        pes = ExitStack()
        Kp = k.tile(pes, "Kp", [P, 2, 2, NT], BF16)
        Va = k.tile(pes, "Va", [P, 34, 2, 192], BF16)
        qt = [k.tile(pes, f"qt{i}", [P, 4, 512], BF16) for i in range(2)]
        ptl = [k.tile(pes, f"pt{i}", [P, 512], BF16) for i in range(4)]
        yout = k.tile(pes, "yout", [P, 4, 512], BF16)
        den = k.tile(pes, "den", [P, 512], F32)
        rec = k.tile(pes, "rec", [P, 512], F32)
        k.op("pool", lambda e: e.memset(Kp.t[:, :, :, :], 0.0), writes=[Kp])
        kreads = [k.dres("k", bi) for bi in range(len(BLOCKS))]
        vreads = [k.dres("v", bi) for bi in range(len(BLOCKS))]
        for kvh in range(2):
            for r in range(2):
                k.dma("sp", Kp.t[r * 64:(r + 1) * 64, r, kvh, :], kscr[mixer, kvh * 64:(kvh + 1) * 64, :], Kp,
                      reads=kreads, writes=[Kp])
        vsrc = vscr[:, mixer * 384:(mixer + 1) * 384].rearrange("(c p) f -> p c f", p=P)
        for c0 in range(0, 34, 9):
            c1 = min(34, c0 + 9)
            k.dma("sp", Va.t[:, c0:c1, :, :].rearrange("p c a b -> p c (a b)"), vsrc[:, c0:c1, :], Va,
                  reads=vreads, writes=[Va])

        blocks = list(range(len(BLOCKS)))

        def load(bi):
            r0, n, s = BLOCKS[bi]
            b = bi % 2
            k.dma("sp", qt[b].t[:, :, 0:n], qscr[mixer * 4:(mixer + 1) * 4, :, r0:r0 + n].rearrange("j p n -> p j n"),
                  qt[b], reads=[k.dres("q", bi)], writes=[qt[b]])
        load(0)
        for bi in blocks:
            r0, n, s = BLOCKS[bi]
            if bi + 1 < len(BLOCKS):
                load(bi + 1)
            Q = qt[bi % 2]
            sched = []
            if s == 1:
                sched = [(32, 0, n, []), (33, 0, n, [])]
            elif mixer == 1:
                sched = [(kc, 0, n, []) for kc in range(34)]
            else:
                sched = [(32, 0, n, []), (33, 0, n, [])]
                for kc in range(4 * bi - 1, 4 * bi + 5):
                    if kc < 0 or kc > 31:
                        continue
                    qlo = max(kc - 1, 4 * bi)
                    qhi = min(kc + 1, 4 * bi + 3)
                    masks = []
                    for qtile in range(qlo, qhi + 1):
                        if kc == qtile - 1:
                            masks.append((0, (qtile - 4 * bi) * P))
                        elif kc == qtile + 1:
                            masks.append((1, (qtile - 4 * bi) * P))
                    sched.append((kc, (qlo - 4 * bi) * P, (qhi + 1 - 4 * bi) * P, masks))
            for h in range(8):
                j, r, kvh = h // 2, h % 2, h // 4
                acc = ps[4 + (h % 2)]
                vcols = slice(64, 192) if r == 0 else slice(0, 128)
                nsch = len(sched)
                SK = 2

                def s_mm(i):
                    kc, q0, q1, _ = sched[i]
                    pst = ps[i % 4]
                    mm(pst, pst.t[:, q0:q1], [(Kp.t[:, r, kvh, kc * P:(kc + 1) * P], Q.t[:, j, q0:q1])], reads=[Kp, Q])
                for i in range(min(SK, nsch)):
                    s_mm(i)
                for i, (kc, q0, q1, masks) in enumerate(sched):
                    pst = ps[i % 4]
                    pt = ptl[i % 4]
                    k.op("act", lambda e, pst=pst, pt=pt, q0=q0, q1=q1: e.activation(
                        out=pt.t[:, q0:q1], in_=pst.t[:, q0:q1], func=AF.Exp, scale=HD ** -0.5), reads=[pst], writes=[pt])
                    for (mi, c0) in masks:
                        k.op("pool", lambda e, pt=pt, mi=mi, c0=c0: e.tensor_tensor(
                            out=pt.t[:, c0:c0 + P], in0=pt.t[:, c0:c0 + P], in1=maskb.t[:, mi, :], op=ALU.mult),
                            reads=[pt, maskb], writes=[pt])
                    if i + SK < nsch:
                        s_mm(i + SK)
                    mm(acc, acc.t[:, q0:q1], [(Va.t[:, kc, kvh, vcols], pt.t[:, q0:q1])], reads=[Va, pt],
                       start=(i == 0), stop=(i == nsch - 1))
                drows = slice(64, 128) if r == 0 else slice(0, 64)
                nrows = slice(0, 64) if r == 0 else slice(64, 128)
                if mixer == 0:
                    k.op("dve", lambda e, acc=acc, drows=drows, h=h: e.tensor_scalar(
                        out=den.t[drows, 0:n], in0=acc.t[drows, 0:n], scalar1=sinkexp.t[drows, h:h + 1], scalar2=None,
                        op0=ALU.add), reads=[acc, sinkexp], writes=[den])
                    k.op("act", lambda e, drows=drows: e.activation(out=den.t[drows, 0:n], in_=den.t[drows, 0:n], func=AF.Ln),
                         reads=[den], writes=[den])
                else:
                    k.op("act", lambda e, acc=acc, drows=drows: e.activation(out=den.t[drows, 0:n], in_=acc.t[drows, 0:n],
                                                                             func=AF.Ln), reads=[acc], writes=[den])
                k.op("act", lambda e, drows=drows: e.activation(out=rec.t[drows, 0:n], in_=den.t[drows, 0:n], func=AF.Exp,
                                                                scale=-1.0), reads=[den], writes=[rec])
                k.op("dve", lambda e, acc=acc, drows=drows, nrows=nrows, j=j: e.tensor_tensor(
                    out=yout.t[nrows, j, 0:n], in0=acc.t[nrows, 0:n], in1=rec.t[drows, 0:n], op=ALU.mult),
                    reads=[acc, rec], writes=[yout])
            k.dma("sp", yscr[mixer * 4:(mixer + 1) * 4, :, r0:r0 + n].rearrange("j p n -> p j n"), yout.t[:, :, 0:n], yout,
                  reads=[yout], writes=[k.dres(f"y{mixer}", bi)])
        k.barrier()
        pes.close()

    def alloc_merge_w(es_, l):
        wpa = k.tile(es_, "wpa", [P, 4, D], BF16)
        wpb = k.tile(es_, "wpb", [P, 4, D], BF16)
        wo = k.tile(es_, "wo", [P, 8, D], BF16)
        for kc in range(4):
            wload(wpa, lambda c0, c1, kc=kc: wpa.t[:, kc, c0:c1], wpa_d[l, kc * P:(kc + 1) * P, :], D)
            wload(wpb, lambda c0, c1, kc=kc: wpb.t[:, kc, c0:c1], wpb_d[l, kc * P:(kc + 1) * P, :], D)
        for kc in range(8):
            wload(wo, lambda c0, c1, kc=kc: wo.t[:, kc, c0:c1], wo_d[l, kc * P:(kc + 1) * P, :], D)
        return wpa, wpb, wo

    def phase_merge(l, last, mw):
        pes = ExitStack()
        wpa, wpb, wo = mw
        yt = [k.tile(pes, f"yt{i}", [P, 8, 512], BF16) for i in range(2)]
        gt = [k.tile(pes, "gt0", [P, 16, 512], BF16)] * 2
        xt = [k.tile(pes, f"xt{i}", [P, 4, D], F32) for i in range(2)]
        xs = k.tile(pes, "xs", [P, 4, D], F32)
        mT = k.tile(pes, "mT", [P, 8, 512], BF16)
        hT = k.tile(pes, "hT", [P, 8, 512], BF16)
        ta = k.tile(pes, "ta", [P, 512], F32)
        tb = k.tile(pes, "tb", [P, 512], F32)
        tmp = [k.tile(pes, f"tmp{i}", [P, 512], F32) for i in range(2)]
        junk = k.tile(pes, "junk", [P, D], F32)
        ss = k.tile(pes, "ss", [P, 4], F32)
        lnv = k.tile(pes, "lnv", [P, 4], F32)
        rstd = k.tile(pes, "rstd", [P, 4], F32)
        nb = len(BLOCKS) - (1 if last else 0)

        def load(bi):
            r0, n, s = BLOCKS[bi]
            nt = n // P
            b = bi % 2
            k.dma("sp", yt[b].t[:, :, 0:n], yscr[:, :, r0:r0 + n].rearrange("j p n -> p j n"), yt[b],
                  reads=[k.dres("y0", bi), k.dres("y1", bi)], writes=[yt[b]])
            k.dma("sp", xt[b].t[:, 0:nt, :], xres_d[r0:r0 + n, :].rearrange("(t p) d -> p t d", p=P), xt[b],
                  reads=[k.dres("xres", bi)], writes=[xt[b]])
        load(0)
        for bi in range(nb):
            r0, n, s = BLOCKS[bi]
            nt = n // P
            if bi + 1 < nb:
                load(bi + 1)
            Y, G, X = yt[bi % 2], gt[0], xt[bi % 2]
            k.dma("sp", G.t[:, :, 0:n], gscr[:, :, r0:r0 + n].rearrange("j p n -> p j n"), G,
                  reads=[k.dres("g", bi)], writes=[G])
            for oc in range(8):
                pa, pb = ps[(oc % 2) * 2], ps[(oc % 2) * 2 + 1]
                mm(pa, pa.t[:, 0:n], [(wpa.t[:, kc, oc * P:(oc + 1) * P], Y.t[:, kc, 0:n]) for kc in range(4)], reads=[wpa, Y])
                mm(pb, pb.t[:, 0:n], [(wpb.t[:, kc, oc * P:(oc + 1) * P], Y.t[:, 4 + kc, 0:n]) for kc in range(4)],
                   reads=[wpb, Y])
                k.op("dve", lambda e, pa=pa, oc=oc: e.tensor_tensor(out=ta.t[:, 0:n], in0=pa.t[:, 0:n], in1=G.t[:, oc, 0:n],
                                                                    op=ALU.mult), reads=[pa, G], writes=[ta])
                k.op("dve", lambda e, pb=pb, oc=oc: e.tensor_tensor(out=tb.t[:, 0:n], in0=pb.t[:, 0:n], in1=G.t[:, 8 + oc, 0:n],
                                                                    op=ALU.mult), reads=[pb, G], writes=[tb])
                k.op("pool", lambda e, oc=oc: e.tensor_tensor(out=mT.t[:, oc, 0:n], in0=ta.t[:, 0:n], in1=tb.t[:, 0:n],
                                                              op=ALU.add), reads=[ta, tb], writes=[mT])
            i = 0
            for t in range(nt):
                for hf in range(2):
                    po = ps[4 + (i % 2)]
                    tm = tmp[i % 2]
                    i += 1
                    mm(po, po.t[:, :], [(mT.t[:, kc, t * P:(t + 1) * P], wo.t[:, kc, hf * 512:(hf + 1) * 512]) for kc in range(8)],
                       reads=[mT, wo])
                    k.op("dve", lambda e, po=po, tm=tm, hf=hf: e.tensor_tensor(
                        out=tm.t[:, :], in0=po.t[:, :], in1=gtb[0][s].t[:, hf * 512:(hf + 1) * 512], op=ALU.mult),
                        reads=[po, gtb[0][s]], writes=[tm])
                    k.op("pool", lambda e, tm=tm, t=t, hf=hf: e.tensor_tensor(
                        out=X.t[:, t, hf * 512:(hf + 1) * 512], in0=X.t[:, t, hf * 512:(hf + 1) * 512], in1=tm.t[:, :],
                        op=ALU.add), reads=[tm, X], writes=[X])
            k.dma("sp", xres_d[r0:r0 + n, :].rearrange("(t p) d -> p t d", p=P), X.t[:, 0:nt, :], X,
                  reads=[X], writes=[k.dres("xres", bi)])
            for t in range(nt):
                k.op("act", lambda e, t=t: e.activation(out=junk.t[:, :], in_=X.t[:, t, :], func=AF.Square),
                     reads=[X], writes=[junk])
                k.op("dve", lambda e, t=t: e.tensor_reduce(out=ss.t[:, t:t + 1], in_=junk.t[:, :], axis=mybir.AxisListType.X,
                                                           op=ALU.add), reads=[junk], writes=[ss])
            rstd_from_ss(ss, lnv, rstd, nt, 1.0 / D)
            for t in range(nt):
                k.op("dve", lambda e, t=t: e.tensor_scalar(out=xs.t[:, t, :], in0=X.t[:, t, :], scalar1=rstd.t[:, t:t + 1],
                                                           scalar2=None, op0=ALU.mult), reads=[X, rstd], writes=[xs])
            norm_to_hT(xs, hT, nt, s, 8, 24, ps[6], ps[7])
            k.dma("sp", hscr[:, :, r0:r0 + n].rearrange("j p n -> p j n"), hT.t[:, :, 0:n], hT,
                  reads=[hT], writes=[k.dres("h2", bi)])
        k.barrier()
        pes.close()

    SPL = [(0, 6), (6, 5), (11, 6), (17, 5)]

    def ffn_slot(es_):
        return (k.tile(es_, "wg", [P, 8, 6 * P], BF16), k.tile(es_, "wu", [P, 8, 6 * P], BF16),
                k.tile(es_, "wd", [P, 6, D], BF16))

    def ffn_wload(l, qi, slot):
        c0_, nch = SPL[qi]
        f0, FH = c0_ * P, nch * P
        wg, wu, wd = slot
        for kc in range(8):
            wload(wg, lambda a, b, kc=kc: wg.t[:, kc, a:b], wg_d[l, kc * P:(kc + 1) * P, f0:f0 + FH], FH)
            wload(wu, lambda a, b, kc=kc: wu.t[:, kc, a:b], wu_d[l, kc * P:(kc + 1) * P, f0:f0 + FH], FH)
        for fc in range(nch):
            wload(wd, lambda a, b, fc=fc: wd.t[:, fc, a:b], wd_d[l, f0 + fc * P:f0 + (fc + 1) * P, :], D)

    def phase_ffn(l, last, slot0):
        pes = ExitStack()
        slots = [slot0, ffn_slot(pes)]
        ffn_wload(l, 1, slots[1])
        ht = [k.tile(pes, f"ht{i}", [P, 8, 512], BF16) for i in range(2)]
        xt = [k.tile(pes, f"xt{i}", [P, 4, D], F32) for i in range(2)]
        aT = k.tile(pes, "aT", [P, 6, 512], BF16)
        sg = [k.tile(pes, f"sg{i}", [P, 512], F32) for i in range(2)]
        tmp = [k.tile(pes, f"tmp{i}", [P, 512], F32) for i in range(2)]
        nb = len(BLOCKS) - (1 if last else 0)
        iters = [(qi, bi) for qi in range(4) for bi in range(nb)]

        def load(it):
            qi, bi = iters[it]
            r0, n, s = BLOCKS[bi]
            nt = n // P
            b = it % 2
            k.dma("sp", ht[b].t[:, :, 0:n], hscr[:, :, r0:r0 + n].rearrange("j p n -> p j n"), ht[b],
                  reads=[k.dres("h2", bi)], writes=[ht[b]])
            k.dma("sp", xt[b].t[:, 0:nt, :], xres_d[r0:r0 + n, :].rearrange("(t p) d -> p t d", p=P), xt[b],
                  reads=[k.dres("xres", bi)], writes=[xt[b]])
        load(0)
        for it, (qi, bi) in enumerate(iters):
            r0, n, s = BLOCKS[bi]
            nt = n // P
            if it + 1 < len(iters):
                load(it + 1)
            wg, wu, wd = slots[qi % 2]
            NF = SPL[qi][1]
            H, X = ht[it % 2], xt[it % 2]
            for fc in range(NF):
                pg, pu = ps[(fc % 2) * 2], ps[(fc % 2) * 2 + 1]
                sgt = sg[fc % 2]
                mm(pg, pg.t[:, 0:n], [(wg.t[:, kc, fc * P:(fc + 1) * P], H.t[:, kc, 0:n]) for kc in range(8)], reads=[wg, H])
                mm(pu, pu.t[:, 0:n], [(wu.t[:, kc, fc * P:(fc + 1) * P], H.t[:, kc, 0:n]) for kc in range(8)], reads=[wu, H])
                k.op("act", lambda e, pg=pg, sgt=sgt: e.activation(out=sgt.t[:, 0:n], in_=pg.t[:, 0:n], func=AF.Silu),
                     reads=[pg], writes=[sgt])
                k.op("dve", lambda e, pu=pu, sgt=sgt, fc=fc: e.tensor_tensor(
                    out=aT.t[:, fc, 0:n], in0=pu.t[:, 0:n], in1=sgt.t[:, 0:n], op=ALU.mult), reads=[pu, sgt], writes=[aT])
            i = 0
            for t in range(nt):
                for hf in range(2):
                    po = ps[4 + (i % 2)]
                    tm = tmp[i % 2]
                    i += 1
                    mm(po, po.t[:, :], [(aT.t[:, fc, t * P:(t + 1) * P], wd.t[:, fc, hf * 512:(hf + 1) * 512]) for fc in range(NF)],
                       reads=[aT, wd])
                    k.op("dve", lambda e, po=po, tm=tm, hf=hf: e.tensor_tensor(
                        out=tm.t[:, :], in0=po.t[:, :], in1=gtb[1][s].t[:, hf * 512:(hf + 1) * 512], op=ALU.mult),
                        reads=[po, gtb[1][s]], writes=[tm])
                    k.op("pool", lambda e, tm=tm, t=t, hf=hf, X=X: e.tensor_tensor(
                        out=X.t[:, t, hf * 512:(hf + 1) * 512], in0=X.t[:, t, hf * 512:(hf + 1) * 512], in1=tm.t[:, :],
                        op=ALU.add), reads=[tm, X], writes=[X])
            k.dma("sp", xres_d[r0:r0 + n, :].rearrange("(t p) d -> p t d", p=P), X.t[:, 0:nt, :], X,
                  reads=[X], writes=[k.dres("xres", bi)])
            if bi == nb - 1 and qi + 2 < 4:
                ffn_wload(l, qi + 2, slots[qi % 2])
        k.barrier()
        pes.close()

    def phase_final():
        pes = ExitStack()
        grow = k.tile(pes, "grow", [1, D], F32)
        gb = k.tile(pes, "gb", [P, D], F32)
        xt = [k.tile(pes, f"xt{i}", [P, 4, D], F32) for i in range(2)]
        junk = k.tile(pes, "junk", [P, D], F32)
        ss = k.tile(pes, "ss", [P, 4], F32)
        lnv = k.tile(pes, "lnv", [P, 4], F32)
        rstd = k.tile(pes, "rstd", [P, 4], F32)
        k.dma("sp", grow.t[:, :], gfin_d[:, :], grow, writes=[grow])
        for hf in range(2):
            mm(ps[hf], ps[hf].t[:, :], [(ones1.t[0:1, :], grow.t[0:1, hf * 512:(hf + 1) * 512])], reads=[ones1, grow])
            k.op("dve", lambda e, hf=hf: e.tensor_copy(out=gb.t[:, hf * 512:(hf + 1) * 512], in_=ps[hf].t[:, :]),
                 reads=[ps[hf]], writes=[gb])

        def load(bi):
            r0, n, s = BLOCKS[bi]
            k.dma("sp", xt[bi % 2].t[:, :, :], xres_d[r0:r0 + n, :].rearrange("(t p) d -> p t d", p=P), xt[bi % 2],
                  reads=[k.dres("xres", bi)], writes=[xt[bi % 2]])
        load(0)
        for bi in range(8):
            r0, n, s = BLOCKS[bi]
            if bi + 1 < 8:
                load(bi + 1)
            X = xt[bi % 2]
            for t in range(4):
                k.op("act", lambda e, t=t: e.activation(out=junk.t[:, :], in_=X.t[:, t, :], func=AF.Square),
                     reads=[X], writes=[junk])
                k.op("dve", lambda e, t=t: e.tensor_reduce(out=ss.t[:, t:t + 1], in_=junk.t[:, :], axis=mybir.AxisListType.X,
                                                           op=ALU.add), reads=[junk], writes=[ss])
            rstd_from_ss(ss, lnv, rstd, 4, 1.0 / D)
            for t in range(4):
                k.op("dve", lambda e, t=t: e.scalar_tensor_tensor(
                    out=X.t[:, t, :], in0=X.t[:, t, :], scalar=rstd.t[:, t:t + 1], in1=gb.t[:, :], op0=ALU.mult, op1=ALU.mult),
                    reads=[X, rstd, gb], writes=[X])
            k.dma("sp", out_d[r0:r0 + n, :].rearrange("(t p) d -> p t d", p=P), X.t[:, :, :], X,
                  reads=[X], writes=[k.dres("out", bi)])
        k.barrier()
        pes.close()

    k.barrier()
    for li, l in enumerate(layers):
        last = (l == L - 1)
        phase_mod(l)
        phase_A(l)
        les = ExitStack()
        mw = alloc_merge_w(les, l)
        slot0 = ffn_slot(les)
        ffn_wload(l, 0, slot0)
        phase_attn(l, 0, last)
        phase_attn(l, 1, last)
        phase_merge(l, last, mw)
        phase_ffn(l, last, slot0)
        les.close()
    if final_norm and stop_after is None:
        phase_final()
    k.barrier()
    es.close()
    return nc


def _rope_tables():
    t = np.arange(T)
    row = (t // GRID_W).astype(np.float32)
    col = (t % GRID_W).astype(np.float32)
    inv = (10000.0 ** (-np.arange(0, 32, 2, dtype=np.float32) / 32.0)).astype(np.float32)
    cosT = np.zeros((P, T), np.float32)
    sinT = np.zeros((P, T), np.float32)
    for p in range(P):
        j = p % 64
        axis, half, i = j // 32, (j % 32) // 16, j % 16
        ang = ((row if axis == 0 else col) * inv[i]).astype(np.float32)
        cosT[p] = np.cos(ang)
        sinT[p] = np.sin(ang) * (-1.0 if half == 0 else 1.0)
    return cosT, sinT


def _partner(j):
    return j + 16 if (j % 32) < 16 else j - 16


def _consts():
    c = np.zeros((P, 640), np.float32)
    c[:, 0:128] = np.eye(P, dtype=np.float32)
    for m in range(P):
        c[(m // 64) * 64 + _partner(m % 64), 128 + m] = 1.0
    c[0:64, 256:320] = 1.0
    c[64:128, 320:384] = 1.0
    kp = np.arange(P)[:, None]
    qp = np.arange(P)[None, :]
    c[:, 384:512] = (kp >= qp)
    c[:, 512:640] = (kp <= qp)
    sel = np.zeros((2, 256), np.float32)
    sel[0, 0:128] = 1.0
    sel[1, 128:256] = 1.0
    return c, sel


def _prep_inputs(inp):
    f = lambda a: np.ascontiguousarray(np.asarray(a, dtype=np.float32))
    w_in = f(inp["w_in"])
    perm = np.concatenate([np.arange(0, 512), np.arange(768, 1280), np.arange(512, 640), np.arange(1280, 1408),
                           np.arange(640, 768), np.arange(1408, 1536), np.arange(1536, 3584)])
    w_in_p = np.ascontiguousarray(w_in[:, :, perm])
    small = np.zeros((L, P, 32), np.float32)
    part = np.array([_partner(j) for j in range(64)])
    for l in range(L):
        small[l, :, 0:8] = f(inp["norm1_g"])[l].reshape(8, P).T
        small[l, :, 8:16] = f(inp["norm2_g"])[l].reshape(8, P).T
        qg = f(inp["q_norm_g"])[l]
        kg = f(inp["k_norm_g"])[l]
        small[l, :, 16] = np.tile(qg, 2)
        small[l, :, 17] = np.tile(qg[part], 2)
        small[l, :, 18] = np.tile(kg, 2)
        small[l, :, 19] = np.tile(kg[part], 2)
    cosT, sinT = _rope_tables()
    consts, sel = _consts()
    shared = {
        "w_ada": f(inp["w_ada"]), "b_ada": f(inp["b_ada"]), "small": small, "sink": f(inp["sink_a"]),
        "w_in": w_in_p, "w_pa": f(inp["w_proj_a"]), "w_pb": f(inp["w_proj_b"]), "w_o": f(inp["w_out"]),
        "w_g": f(inp["w_ffn_gate"]), "w_u": f(inp["w_ffn_up"]), "w_d": f(inp["w_ffn_down"]),
        "gfin": f(inp["final_norm_g"]).reshape(1, D), "consts": consts, "sel": sel, "cosT": cosT, "sinT": sinT,
    }
    return shared


_NC_CACHE = {}


def _get_nc(layers, final_norm):
    key = (tuple(layers), final_norm)
    if key not in _NC_CACHE:
        _NC_CACHE[key] = build(list(layers), final_norm)
    return _NC_CACHE[key]


FUSED = True


def kernel(**inp):
    shared = _prep_inputs(inp)
    x = np.asarray(inp["x"], dtype=np.float32)
    ctx = np.asarray(inp["ctx"], dtype=np.float32)
    c = np.asarray(inp["c"], dtype=np.float32)
    c_ctx = np.asarray(inp["c_ctx"], dtype=np.float32)
    B = x.shape[0]
    xs = [np.concatenate([x[b], ctx[b]], axis=0) for b in range(B)]
    ccs = [np.stack([c[b], c_ctx], axis=0) for b in range(B)]
    groups = [list(range(L))] if FUSED else [[l] for l in range(L)]
    out = None
    for gi, layers in enumerate(groups):
        fin = (layers[-1] == L - 1)
        nc = _get_nc(layers, fin)
        in_maps = []
        for core in range(NCORES):
            b = core % B
            m = dict(shared)
            m["x"] = np.ascontiguousarray(xs[b])
            m["cc"] = np.ascontiguousarray(ccs[b])
            in_maps.append(m)
        res = run_bass_kernel_spmd(nc, in_maps, core_ids=list(range(NCORES)))
        if fin:
            out = np.stack([res.results[b]["out"] for b in range(B)], axis=0)
        else:
            xs = [res.results[b]["out"] for b in range(B)]
    return out.astype(np.float32)
```

```python
import os
import numpy as np
from contextlib import ExitStack
DBGA = int(os.environ.get('DBGA', '99'))
import concourse.bass as bass
import concourse.mybir as mybir
from concourse.bass_utils import run_bass_kernel_spmd

F32 = mybir.dt.float32
BF16 = mybir.dt.bfloat16
ALU = mybir.AluOpType
AF = mybir.ActivationFunctionType

D = 1024
T = 4096
C = 256
NT = T + C
L = 4
HD = 64
DFF = 2816
P = 128
EPS = 1e-6
GRID_W = 64
NCORES = 8
BLOCKS = [(i * 512, 512, 0) for i in range(8)] + [(T, C, 1)]

QA0, QB0, KA0, KB0, V0, GA0, GB0, INW = 0, 512, 1024, 1152, 1280, 1536, 2560, 3584


class Res:
    __slots__ = ("name", "w", "r", "excl")

    def __init__(self, name):
        self.name = name
        self.w = None
        self.r = {}
        self.excl = False


class Tile:
    def __init__(self, t, name):
        self.t = t
        self.res = Res(name)
        self.dsem = None

    def __getitem__(self, idx):
        return self.t[idx]


class KB:
    def __init__(self, nc, es):
        self.nc = nc
        self.eng = {"pe": nc.tensor, "act": nc.scalar, "dve": nc.vector, "pool": nc.gpsimd, "sp": nc.sync}
        self.es = es
        self.sems = {k: [] for k in self.eng}
        self.cnt = {k: 0 for k in self.eng}
        self.seen = {k: {} for k in self.eng}
        self.dpool = [[es.enter_context(nc.semaphore(f"D{i}")), 0, i] for i in range(48)]
        self.dfree = list(range(48))
        self.dram_res = {}
        self.uid = 0

    def tile(self, es, name, shape, dtype):
        self.uid += 1
        t = es.enter_context(self.nc.sbuf_tensor(f"{name}_{self.uid}", list(shape), dtype))
        tl = Tile(t, name)
        es.callback(self._release, tl)
        return tl

    def _release(self, tl):
        if tl.dsem is not None:
            self.dfree.append(tl.dsem[2])
            tl.dsem = None

    def _dsem(self, tl):
        if tl.dsem is None:
            tl.dsem = self.dpool[self.dfree.pop(0)]
        return tl.dsem

    def dres(self, name, b):
        key = (name, b)
        if key not in self.dram_res:
            self.dram_res[key] = Res(f"{name}[{b}]")
        return self.dram_res[key]

    EPOCH = 1500

    def _etok(self, e):
        cnt = self.cnt[e]
        ep = (cnt - 1) // self.EPOCH
        while len(self.sems[e]) <= ep:
            self.sems[e].append(self.es.enter_context(self.nc.semaphore(f"S_{e}_{len(self.sems[e])}")))
        return (e, self.sems[e][ep], (cnt - ep * self.EPOCH, cnt))

    def _wait(self, e, tok):
        if tok is None:
            return
        key, sem, val = tok
        if key == e and e == "pe":
            return
        if isinstance(val, list):
            lval = gval = val[1]
        else:
            lval, gval = val
        if self.seen[e].get(key, 0) >= gval:
            return
        self.eng[e].wait_ge(sem, lval)
        self.seen[e][key] = gval

    def _deps(self, e, reads, writes):
        for r in reads:
            self._wait(e, r.w)
        for w in writes:
            self._wait(e, w.w)
            for t in list(w.r.values()):
                self._wait(e, t)

    def _mark(self, tok, reads, writes):
        for r in reads:
            r.r[tok[0]] = tok
        for w in writes:
            w.w = tok
            w.r = {}

    @staticmethod
    def _res(xs):
        return [x.res if isinstance(x, Tile) else x for x in xs]

    def op(self, e, fn, reads=(), writes=()):
        reads = self._res(reads)
        writes = self._res(writes)
        writes = writes + [r for r in reads if r.excl and r not in writes]
        self._deps(e, reads, writes)
        ins = fn(self.eng[e])
        self.cnt[e] += 1
        tok = self._etok(e)
        ins.then_inc(tok[1], 1)
        self._mark(tok, reads, writes)

    def dma(self, q, out, in_, sb, reads=(), writes=(), **kw):
        reads = self._res(reads)
        writes = self._res(writes)
        self._deps(q, reads, writes)
        ds = self._dsem(sb)
        ins = self.eng[q].dma_start(out=out, in_=in_, **kw)
        ds[1] += 16
        ins.then_inc(ds[0], 16)
        self._mark(("d%d" % ds[2], ds[0], ds), reads, writes)

    def barrier(self):
        toks = [self._etok(k) for k in self.eng if self.cnt[k] > 0]
        toks += [("d%d" % d[2], d[0], d) for d in self.dpool if d[1] > 0]
        for e in self.eng:
            for t in toks:
                self._wait(e, t)


def build(layers, final_norm, debug=False, stop_after=None):
    nc = bass.Bass("TRN2", target_bir_lowering=False)
    es = ExitStack()

    def din(name, shape, dt=F32):
        return nc.dram_tensor(name, list(shape), dt, kind="ExternalInput").ap()

    scr_kind = "ExternalOutput" if debug else "Internal"

    def dscr(name, shape, dt):
        return nc.dram_tensor(name, list(shape), dt, kind=scr_kind).ap()

    x_d = din("x", [NT, D])
    cc_d = din("cc", [2, D])
    wada_d = din("w_ada", [L, D, 6 * D])
    bada_d = din("b_ada", [L, 6 * D])
    small_d = din("small", [L, P, 32])
    sink_d = din("sink", [L, 8])
    win_d = din("w_in", [L, D, INW])
    wpa_d = din("w_pa", [L, 512, D])
    wpb_d = din("w_pb", [L, 512, D])
    wo_d = din("w_o", [L, D, D])
    wg_d = din("w_g", [L, D, DFF])
    wu_d = din("w_u", [L, D, DFF])
    wd_d = din("w_d", [L, DFF, D])
    gfin_d = din("gfin", [1, D])
    consts_d = din("consts", [P, 640])
    sel_d = din("sel", [2, 256])
    cos_d = din("cosT", [P, T])
    sin_d = din("sinT", [P, T])

    if final_norm:
        out_d = nc.dram_tensor("out", [T, D], F32, kind="ExternalOutput").ap()
        xres_d = dscr("xres", [NT, D], F32)
    else:
        xres_d = nc.dram_tensor("out", [NT, D], F32, kind="ExternalOutput").ap()
    qscr = dscr("qscr", [8, P, NT], BF16)
    kscr = dscr("kscr", [2, P, NT], BF16)
    vscr = dscr("vscr", [NT, 768], BF16)
    gscr = dscr("gscr", [16, P, NT], BF16)
    yscr = dscr("yscr", [8, P, NT], BF16)
    hscr = dscr("hscr", [8, P, NT], BF16)

    k = KB(nc, es)
    ps = []
    psd = []
    for i in range(4):
        t = es.enter_context(nc.psum_tensor(f"psd{i}", [P, 1024], F32))
        psd.append(Tile(t, f"psd{i}"))
        psd[-1].res.excl = True
        for hh in range(2):
            ps.append(Tile(t[:, hh * 512:(hh + 1) * 512], f"ps{2 * i + hh}"))
            ps[-1].res.excl = True

    cst = k.tile(es, "cst", [P, 640], F32)
    ident = cst.t[:, 0:128]
    bones = cst.t[:, 256:384]
    permb = k.tile(es, "permb", [P, P], BF16)
    maskb = k.tile(es, "maskb", [P, 2, P], BF16)
    sel = k.tile(es, "sel", [2, 256], F32)
    ones1 = k.tile(es, "ones1", [1, P], F32)
    silu_row = k.tile(es, "silu_row", [2, D], F32)
    siluT = k.tile(es, "siluT", [P, 8, 2], F32)
    modT = k.tile(es, "modT", [P, 48, 2], F32)
    gm = k.tile(es, "gm", [P, 16, 2], F32)
    small = k.tile(es, "small", [P, 32], F32)
    gtb = [[k.tile(es, f"gtb{g}{s}", [P, D], F32) for s in range(2)] for g in range(2)]
    sinkrow = k.tile(es, "sinkrow", [1, 8], F32)
    sinkexp = k.tile(es, "sinkexp", [P, 8], F32)
    setup_sem_tile = cst

    k.dma("sp", cst.t[:, :], consts_d[:, :], cst, writes=[cst])
    k.dma("sp", sel.t[:, :], sel_d[:, :], sel, writes=[sel])
    k.op("dve", lambda e: e.tensor_copy(out=permb.t[:, :], in_=cst.t[:, 128:256]), reads=[cst], writes=[permb])
    k.op("dve", lambda e: e.tensor_copy(out=maskb.t[:, :, :], in_=cst.t[:, 384:640].rearrange("p (a b) -> p a b", b=P)),
         reads=[cst], writes=[maskb])
    k.op("dve", lambda e: e.memset(ones1.t[:, :], 1.0), writes=[ones1])
    k.dma("sp", silu_row.t[:, :], cc_d[:, :], silu_row, writes=[silu_row])
    k.op("act", lambda e: e.activation(out=silu_row.t[:, :], in_=silu_row.t[:, :], func=AF.Silu),
         reads=[silu_row], writes=[silu_row])

    def _tr_silu(e):
        ins = None
        for kc in range(8):
            ins = e.transpose(out=ps[0].t[:, kc * 2:kc * 2 + 2], in_=silu_row.t[0:2, kc * P:(kc + 1) * P],
                              identity=cst.t[0:2, 0:2])
        return ins
    k.op("pe", _tr_silu, reads=[silu_row, cst], writes=[ps[0]])
    k.op("dve", lambda e: e.tensor_copy(out=siluT.t[:, :, :], in_=ps[0].t[:, 0:16].rearrange("p (a b) -> p a b", b=2)),
         reads=[ps[0]], writes=[siluT])

    for bi, (r0, n, s) in enumerate(BLOCKS):
        k.dma("sp", xres_d[r0:r0 + n, :], x_d[r0:r0 + n, :], setup_sem_tile, writes=[k.dres("xres", bi)])

    def wload(tl, dst_fn, src2d, ncols, maxc=2048):
        c0 = 0
        while c0 < ncols:
            c1 = min(ncols, c0 + maxc)
            k.dma("pool", dst_fn(c0, c1), src2d[:, c0:c1], tl, writes=[tl])
            c0 = c1

    def rstd_from_ss(ss, lnv, rstd, nt, inv_n):
        k.op("act", lambda e: e.activation(out=lnv.t[:, 0:nt], in_=ss.t[:, 0:nt], func=AF.Ln, scale=inv_n, bias=EPS),
             reads=[ss], writes=[lnv])
        k.op("act", lambda e: e.activation(out=rstd.t[:, 0:nt], in_=lnv.t[:, 0:nt], func=AF.Exp, scale=-0.5),
             reads=[lnv], writes=[rstd])

    def norm_to_hT(xs_tile, hT, nt, s, gmoff, shoff, psA, psB):
        for c in range(8):
            pst = psA if c % 2 == 0 else psB

            def _tr(e, c=c, pst=pst):
                ins = None
                for t in range(nt):
                    ins = e.transpose(out=pst.t[:, t * P:(t + 1) * P], in_=xs_tile.t[:, t, c * P:(c + 1) * P],
                                      identity=ident)
                return ins
            k.op("pe", _tr, reads=[xs_tile, cst], writes=[pst])
            k.op("act", lambda e, c=c, pst=pst: e.activation(
                out=hT.t[:, c, 0:nt * P], in_=pst.t[:, 0:nt * P], func=AF.Identity,
                scale=gm.t[:, gmoff + c, s:s + 1], bias=modT.t[:, shoff + c, s:s + 1]),
                reads=[pst, gm, modT], writes=[hT])

    def mm(pst, out_ap, pairs, reads, start=True, stop=True):
        def _f(e):
            ins = None
            n_ = len(pairs)
            for i, (l_, r_) in enumerate(pairs):
                ins = e.matmul(out_ap, lhsT=l_, rhs=r_, start=(start and i == 0), stop=(stop and i == n_ - 1))
            return ins
        k.op("pe", _f, reads=reads, writes=[pst])

    def phase_mod(l):
        pes = ExitStack()
        wa = [k.tile(pes, f"wada{i}", [P, 8, 512], F32) for i in range(2)]
        modrow = k.tile(pes, "modrow", [2, 6 * D], F32)
        bada2 = k.tile(pes, "bada2", [2, 6 * D], F32)
        k.dma("sp", bada2.t[0:1, :], bada_d[l:l + 1, :], bada2, writes=[bada2])
        k.dma("sp", bada2.t[1:2, :], bada_d[l:l + 1, :], bada2, writes=[bada2])
        k.dma("sp", small.t[:, :], small_d[l, :, :], small, writes=[small])
        k.dma("sp", sinkrow.t[:, :], sink_d[l:l + 1, :], sinkrow, writes=[sinkrow])
        wsrc = wada_d[l, :, :].rearrange("(kc p) n -> p kc n", p=P)

        def ld(cb):
            k.dma("sp", wa[cb % 2].t[:, :, :], wsrc[:, :, cb * 512:(cb + 1) * 512], wa[cb % 2], writes=[wa[cb % 2]])
        ld(0)
        for cb in range(12):
            if cb + 1 < 12:
                ld(cb + 1)
            w_ = wa[cb % 2]
            pst = ps[cb % 2]
            mm(pst, pst.t[0:2, :], [(siluT.t[:, kc, :], w_.t[:, kc, :]) for kc in range(8)], reads=[siluT, w_])
            k.op("dve", lambda e, cb=cb, pst=pst: e.tensor_tensor(
                out=modrow.t[:, cb * 512:(cb + 1) * 512], in0=pst.t[0:2, :], in1=bada2.t[:, cb * 512:(cb + 1) * 512],
                op=ALU.add), reads=[pst, bada2], writes=[modrow])
        for off in (D, 4 * D):
            k.op("dve", lambda e, off=off: e.tensor_scalar(out=modrow.t[:, off:off + D], in0=modrow.t[:, off:off + D],
                                                           scalar1=1.0, scalar2=None, op0=ALU.add),
                 reads=[modrow], writes=[modrow])

        def _tr(e):
            ins = None
            for j in range(48):
                ins = e.transpose(out=ps[2].t[:, j * 2:j * 2 + 2], in_=modrow.t[0:2, j * P:(j + 1) * P],
                                  identity=cst.t[0:2, 0:2])
            return ins
        k.op("pe", _tr, reads=[modrow, cst], writes=[ps[2]])
        k.op("dve", lambda e: e.tensor_copy(out=modT.t[:, :, :], in_=ps[2].t[:, 0:96].rearrange("p (a b) -> p a b", b=2)),
             reads=[ps[2]], writes=[modT])
        for s in range(2):
            k.op("dve", lambda e, s=s: e.tensor_tensor(out=gm.t[:, 0:8, s], in0=modT.t[:, 8:16, s], in1=small.t[:, 0:8],
                                                       op=ALU.mult), reads=[modT, small], writes=[gm])
            k.op("dve", lambda e, s=s: e.tensor_tensor(out=gm.t[:, 8:16, s], in0=modT.t[:, 32:40, s], in1=small.t[:, 8:16],
                                                       op=ALU.mult), reads=[modT, small], writes=[gm])
        i = 0
        for g, off in enumerate((2 * D, 5 * D)):
            for s in range(2):
                for hf in range(2):
                    pst = ps[3 + (i % 2)]
                    i += 1
                    mm(pst, pst.t[:, :], [(sel.t[0:2, s * P:(s + 1) * P], modrow.t[0:2, off + hf * 512:off + (hf + 1) * 512])],
                       reads=[sel, modrow])
                    k.op("dve", lambda e, g=g, s=s, hf=hf, pst=pst: e.tensor_copy(
                        out=gtb[g][s].t[:, hf * 512:(hf + 1) * 512], in_=pst.t[:, :]), reads=[pst], writes=[gtb[g][s]])
        mm(ps[5], ps[5].t[:, 0:8], [(ones1.t[0:1, :], sinkrow.t[0:1, :])], reads=[ones1, sinkrow])
        k.op("act", lambda e: e.activation(out=sinkexp.t[:, :], in_=ps[5].t[:, 0:8], func=AF.Exp),
             reads=[ps[5]], writes=[sinkexp])
        k.barrier()
        pes.close()

    def phase_A(l):
        pes = ExitStack()
        w = k.tile(pes, "w_in", [P, 8, INW], BF16)
        for kc in range(8):
            wload(w, lambda c0, c1, kc=kc: w.t[:, kc, c0:c1], win_d[l, kc * P:(kc + 1) * P, :], INW)
        xt = [k.tile(pes, f"xt{i}", [P, 4, D], F32) for i in range(2)]
        cs = [k.tile(pes, f"cs{i}", [P, 2, 512], F32) for i in range(2)]
        hT = k.tile(pes, "hT", [P, 8, 512], BF16)
        junk = k.tile(pes, "junk", [P, D], F32)
        ss = k.tile(pes, "ss", [P, 4], F32)
        lnv = k.tile(pes, "lnv", [P, 4], F32)
        rstd = k.tile(pes, "rstd", [P, 4], F32)
        zb = k.tile(pes, "zb", [P, 512], BF16)
        sq = k.tile(pes, "sq", [P, 512], F32)
        lnq = k.tile(pes, "lnq", [P, 512], F32)
        rq = k.tile(pes, "rq", [P, 512], F32)
        t1 = k.tile(pes, "t1", [P, 512], F32)
        t2 = k.tile(pes, "t2", [P, 512], F32)
        qout = k.tile(pes, "qout", [P, 8, 512], BF16)
        kout = k.tile(pes, "kout", [P, 2, 512], BF16)
        vout = k.tile(pes, "vout", [P, 4, 4, 192], BF16)
        gout = k.tile(pes, "gout", [P, 16, 512], BF16)
        k.op("pool", lambda e: e.memset(vout.t[:, :, :, :], 1.0), writes=[vout])

        def load(bi):
            r0, n, s = BLOCKS[bi]
            nt = n // P
            b = bi % 2
            k.dma("sp", xt[b].t[:, 0:nt, :], xres_d[r0:r0 + n, :].rearrange("(t p) d -> p t d", p=P), xt[b],
                  reads=[k.dres("xres", bi)], writes=[xt[b]])
            if s == 0:
                k.dma("sp", cs[b].t[:, 0, :], cos_d[:, r0:r0 + n], cs[b], writes=[cs[b]])
                k.dma("sp", cs[b].t[:, 1, :], sin_d[:, r0:r0 + n], cs[b], writes=[cs[b]])

        load(0)
        for bi, (r0, n, s) in enumerate(BLOCKS):
            if bi + 1 < len(BLOCKS):
                load(bi + 1)
            nt = n // P
            X = xt[bi % 2]
            CS = cs[bi % 2]
            rope = (s == 0)
            for t in range(nt):
                k.op("act", lambda e, t=t: e.activation(out=junk.t[:, :], in_=X.t[:, t, :], func=AF.Square),
                     reads=[X], writes=[junk])
                k.op("dve", lambda e, t=t: e.tensor_reduce(out=ss.t[:, t:t + 1], in_=junk.t[:, :], axis=mybir.AxisListType.X,
                                                           op=ALU.add), reads=[junk], writes=[ss])
            rstd_from_ss(ss, lnv, rstd, nt, 1.0 / D)
            for t in range(nt):
                k.op("dve", lambda e, t=t: e.tensor_scalar(out=X.t[:, t, :], in0=X.t[:, t, :], scalar1=rstd.t[:, t:t + 1],
                                                           scalar2=None, op0=ALU.mult), reads=[X, rstd], writes=[X])
            if DBGA < 2:
                break
            norm_to_hT(X, hT, nt, s, 0, 0, ps[0], ps[1])
            if DBGA < 3:
                break

            chunks = [(QA0 + j * P, qout, j, False, None) for j in range(4)]
            chunks += [(QB0 + j * P, qout, 4 + j, True, 16) for j in range(4)]
            chunks += [(KA0, kout, 0, False, None), (KB0, kout, 1, True, 18)]
            for ci, (col, dst, dj, isB, gcol) in enumerate(chunks):
                pz = ps[2 + (ci % 2)]
                mm(pz, pz.t[:, 0:n], [(w.t[:, kc, col:col + P], hT.t[:, kc, 0:n]) for kc in range(8)], reads=[w, hT])
                if isB:
                    k.op("act", lambda e, pz=pz: e.activation(out=sq.t[:, 0:n], in_=pz.t[:, 0:n], func=AF.Square),
                         reads=[pz], writes=[sq])
                    mm(ps[4], ps[4].t[:, 0:n], [(bones, sq.t[:, 0:n])], reads=[cst, sq])
                    k.op("act", lambda e: e.activation(out=lnq.t[:, 0:n], in_=ps[4].t[:, 0:n], func=AF.Ln,
                                                       scale=1.0 / HD, bias=EPS), reads=[ps[4]], writes=[lnq])
                    k.op("act", lambda e: e.activation(out=rq.t[:, 0:n], in_=lnq.t[:, 0:n], func=AF.Exp, scale=-0.5),
                         reads=[lnq], writes=[rq])
                g0 = small.t[:, gcol:gcol + 1] if isB else 1.0
                g1 = small.t[:, gcol + 1:gcol + 2] if isB else 1.0
                if rope:
                    k.op("act", lambda e, pz=pz: e.activation(out=zb.t[:, 0:n], in_=pz.t[:, 0:n], func=AF.Copy),
                         reads=[pz], writes=[zb])
                    mm(ps[5], ps[5].t[:, 0:n], [(permb.t[:, :], zb.t[:, 0:n])], reads=[permb, zb])
                    k.op("dve", lambda e, pz=pz, g0=g0: e.scalar_tensor_tensor(
                        out=t1.t[:, 0:n], in0=pz.t[:, 0:n], scalar=g0, in1=CS.t[:, 0, 0:n], op0=ALU.mult, op1=ALU.mult),
                        reads=[pz, CS, small], writes=[t1])
                    k.op("dve", lambda e, g1=g1: e.scalar_tensor_tensor(
                        out=t2.t[:, 0:n], in0=ps[5].t[:, 0:n], scalar=g1, in1=CS.t[:, 1, 0:n], op0=ALU.mult, op1=ALU.mult),
                        reads=[ps[5], CS, small], writes=[t2])
                    if isB:
                        k.op("pool", lambda e: e.tensor_tensor(out=t1.t[:, 0:n], in0=t1.t[:, 0:n], in1=t2.t[:, 0:n],
                                                               op=ALU.add), reads=[t1, t2], writes=[t1])
                        k.op("dve", lambda e, dst=dst, dj=dj: e.tensor_tensor(
                            out=dst.t[:, dj, 0:n], in0=t1.t[:, 0:n], in1=rq.t[:, 0:n], op=ALU.mult),
                            reads=[t1, rq], writes=[dst])
                    else:
                        k.op("pool", lambda e, dst=dst, dj=dj: e.tensor_tensor(
                            out=dst.t[:, dj, 0:n], in0=t1.t[:, 0:n], in1=t2.t[:, 0:n], op=ALU.add),
                            reads=[t1, t2], writes=[dst])
                else:
                    if isB:
                        k.op("dve", lambda e, pz=pz, g0=g0, dst=dst, dj=dj: e.scalar_tensor_tensor(
                            out=dst.t[:, dj, 0:n], in0=pz.t[:, 0:n], scalar=g0, in1=rq.t[:, 0:n], op0=ALU.mult,
                            op1=ALU.mult), reads=[pz, rq, small], writes=[dst])
                    else:
                        k.op("act", lambda e, pz=pz, dst=dst, dj=dj: e.activation(
                            out=dst.t[:, dj, 0:n], in_=pz.t[:, 0:n], func=AF.Copy), reads=[pz], writes=[dst])
            if DBGA < 4:
                break
            for t in range(nt):
                pv = ps[6 + (t % 2)]
                mm(pv, pv.t[:, 0:256], [(hT.t[:, kc, t * P:(t + 1) * P], w.t[:, kc, V0:V0 + 256]) for kc in range(8)],
                   reads=[w, hT])
                k.op("dve", lambda e, t=t, pv=pv: e.tensor_copy(
                    out=vout.t[:, t, :, 64:128], in_=pv.t[:, 0:256].rearrange("p (a b) -> p a b", b=64)),
                    reads=[pv], writes=[vout])
            if DBGA < 5:
                break
            for j in range(16):
                pg = ps[2 + (j % 2)]
                col = GA0 + j * P
                mm(pg, pg.t[:, 0:n], [(w.t[:, kc, col:col + P], hT.t[:, kc, 0:n]) for kc in range(8)], reads=[w, hT])
                k.op("act", lambda e, j=j, pg=pg: e.activation(out=gout.t[:, j, 0:n], in_=pg.t[:, 0:n], func=AF.Sigmoid),
                     reads=[pg], writes=[gout])
            if DBGA < 6:
                break
            k.dma("sp", qscr[:, :, r0:r0 + n].rearrange("j p n -> p j n"), qout.t[:, :, 0:n], qout,
                  reads=[qout], writes=[k.dres("q", bi)])
            k.dma("sp", kscr[:, :, r0:r0 + n].rearrange("j p n -> p j n"), kout.t[:, :, 0:n], kout,
                  reads=[kout], writes=[k.dres("k", bi)])
            k.dma("sp", vscr[r0:r0 + n, :].rearrange("(t p) f -> p t f", p=P),
                  vout.t[:, 0:nt, :, :].rearrange("p t a b -> p t (a b)"), vout, reads=[vout], writes=[k.dres("v", bi)])
            k.dma("sp", gscr[:, :, r0:r0 + n].rearrange("j p n -> p j n"), gout.t[:, :, 0:n], gout,
                  reads=[gout], writes=[k.dres("g", bi)])
        k.barrier()
        pes.close()

    def phase_attn(l, mixer, last):
        pes = ExitStack()
        Kp = k.tile(pes, "Kp", [P, 2, 2, NT], BF16)
        Va = k.tile(pes, "Va", [P, 34, 2, 192], BF16)
        qt = [k.tile(pes, f"qt{i}", [P, 4, 512], BF16) for i in range(2)]
        ptl = [k.tile(pes, f"pt{i}", [P, 2, 512], BF16) for i in range(3)]
        yout = k.tile(pes, "yout", [P, 4, 512], BF16)
        den = k.tile(pes, "den", [P, 512], F32)
        rec = k.tile(pes, "rec", [P, 512], F32)
        k.op("pool", lambda e: e.memset(Kp.t[:, :, :, :], 0.0), writes=[Kp])
        kreads = [k.dres("k", bi) for bi in range(len(BLOCKS))]
        vreads = [k.dres("v", bi) for bi in range(len(BLOCKS))]
        for kvh in range(2):
            for r in range(2):
                k.dma("sp", Kp.t[r * 64:(r + 1) * 64, r, kvh, :], kscr[mixer, kvh * 64:(kvh + 1) * 64, :], Kp,
                      reads=kreads, writes=[Kp])
        vsrc = vscr[:, mixer * 384:(mixer + 1) * 384].rearrange("(c p) f -> p c f", p=P)
        for c0 in range(0, 34, 9):
            c1 = min(34, c0 + 9)
            k.dma("sp", Va.t[:, c0:c1, :, :].rearrange("p c a b -> p c (a b)"), vsrc[:, c0:c1, :], Va,
                  reads=vreads, writes=[Va])

        blocks = list(range(len(BLOCKS)))

        def load(bi):
            r0, n, s = BLOCKS[bi]
            b = bi % 2
            k.dma("sp", qt[b].t[:, :, 0:n], qscr[mixer * 4:(mixer + 1) * 4, :, r0:r0 + n].rearrange("j p n -> p j n"),
                  qt[b], reads=[k.dres("q", bi)], writes=[qt[b]])
        load(0)
        for bi in blocks:
            r0, n, s = BLOCKS[bi]
            if bi + 1 < len(BLOCKS):
                load(bi + 1)
            Q = qt[bi % 2]
            sched = []
            if s == 1:
                sched = [(32, 0, n, []), (33, 0, n, [])]
            elif mixer == 1:
                sched = [(kc, 0, n, []) for kc in range(34)]
            else:
                sched = [(32, 0, n, []), (33, 0, n, [])]
                for kc in range(4 * bi - 1, 4 * bi + 5):
                    if kc < 0 or kc > 31:
                        continue
                    qlo = max(kc - 1, 4 * bi)
                    qhi = min(kc + 1, 4 * bi + 3)
                    masks = []
                    for qtile in range(qlo, qhi + 1):
                        if kc == qtile - 1:
                            masks.append((0, (qtile - 4 * bi) * P))
                        elif kc == qtile + 1:
                            masks.append((1, (qtile - 4 * bi) * P))
                    sched.append((kc, (qlo - 4 * bi) * P, (qhi + 1 - 4 * bi) * P, masks))
            items = []
            ii = 0
            while ii < len(sched):
                a = sched[ii]
                if ii + 1 < len(sched) and not a[3] and not sched[ii + 1][3] and a[1:3] == sched[ii + 1][1:3]:
                    items.append([a, sched[ii + 1]])
                    ii += 2
                else:
                    items.append([a])
                    ii += 1
            nit = len(items)
            for h in range(8):
                j, r, kvh = h // 2, h % 2, h // 4
                acc = ps[6 + (h % 2)]
                vcols = slice(64, 192) if r == 0 else slice(0, 128)
                SK = 2

                def s_mm(ii):
                    Dt = psd[ii % 3]

                    def _f(e, ii=ii, Dt=Dt):
                        ins = None
                        for jj, (kc, q0, q1, _) in enumerate(items[ii]):
                            ins = e.matmul(Dt.t[:, jj * 512 + q0:jj * 512 + q1], lhsT=Kp.t[:, r, kvh, kc * P:(kc + 1) * P],
                                           rhs=Q.t[:, j, q0:q1], start=True, stop=True)
                        return ins
                    k.op("pe", _f, reads=[Kp, Q], writes=[Dt])
                for ii in range(min(SK, nit)):
                    s_mm(ii)
                for ii, it in enumerate(items):
                    Dt = psd[ii % 3]
                    pt = ptl[ii % 3]
                    q0, q1 = it[0][1], it[0][2]
                    if len(it) == 2:
                        k.op("act", lambda e, Dt=Dt, pt=pt, q0=q0, q1=q1: e.activation(
                            out=pt.t[:, :, q0:q1], in_=Dt.t[:, :].rearrange("p (a b) -> p a b", b=512)[:, :, q0:q1],
                            func=AF.Exp, scale=HD ** -0.5), reads=[Dt], writes=[pt])
                    else:
                        k.op("act", lambda e, Dt=Dt, pt=pt, q0=q0, q1=q1: e.activation(
                            out=pt.t[:, 0, q0:q1], in_=Dt.t[:, q0:q1], func=AF.Exp, scale=HD ** -0.5),
                            reads=[Dt], writes=[pt])
                        for (mi, c0) in it[0][3]:
                            k.op("pool", lambda e, pt=pt, mi=mi, c0=c0: e.tensor_tensor(
                                out=pt.t[:, 0, c0:c0 + P], in0=pt.t[:, 0, c0:c0 + P], in1=maskb.t[:, mi, :], op=ALU.mult),
                                reads=[pt, maskb], writes=[pt])
                    if ii + SK < nit:
                        s_mm(ii + SK)

                    def _pv(e, ii=ii, it=it, pt=pt):
                        ins = None
                        for jj, (kc, q0_, q1_, _) in enumerate(it):
                            ins = e.matmul(acc.t[:, q0_:q1_], lhsT=Va.t[:, kc, kvh, vcols], rhs=pt.t[:, jj, q0_:q1_],
                                           start=(ii == 0 and jj == 0), stop=(ii == nit - 1 and jj == len(it) - 1))
                        return ins
                    k.op("pe", _pv, reads=[Va, pt], writes=[acc])
                drows = slice(64, 128) if r == 0 else slice(0, 64)
                nrows = slice(0, 64) if r == 0 else slice(64, 128)
                if mixer == 0:
                    k.op("dve", lambda e, acc=acc, drows=drows, h=h: e.tensor_scalar(
                        out=den.t[drows, 0:n], in0=acc.t[drows, 0:n], scalar1=sinkexp.t[drows, h:h + 1], scalar2=None,
                        op0=ALU.add), reads=[acc, sinkexp], writes=[den])
                    k.op("act", lambda e, drows=drows: e.activation(out=den.t[drows, 0:n], in_=den.t[drows, 0:n], func=AF.Ln),
                         reads=[den], writes=[den])
                else:
                    k.op("act", lambda e, acc=acc, drows=drows: e.activation(out=den.t[drows, 0:n], in_=acc.t[drows, 0:n],
                                                                             func=AF.Ln), reads=[acc], writes=[den])
                k.op("act", lambda e, drows=drows: e.activation(out=rec.t[drows, 0:n], in_=den.t[drows, 0:n], func=AF.Exp,
                                                                scale=-1.0), reads=[den], writes=[rec])
                k.op("dve", lambda e, acc=acc, drows=drows, nrows=nrows, j=j: e.tensor_tensor(
                    out=yout.t[nrows, j, 0:n], in0=acc.t[nrows, 0:n], in1=rec.t[drows, 0:n], op=ALU.mult),
                    reads=[acc, rec], writes=[yout])
            k.dma("sp", yscr[mixer * 4:(mixer + 1) * 4, :, r0:r0 + n].rearrange("j p n -> p j n"), yout.t[:, :, 0:n], yout,
                  reads=[yout], writes=[k.dres(f"y{mixer}", bi)])
        k.barrier()
        pes.close()

    def phase_merge(l, last):
        pes = ExitStack()
        wpa = k.tile(pes, "wpa", [P, 4, D], BF16)
        wpb = k.tile(pes, "wpb", [P, 4, D], BF16)
        wo = k.tile(pes, "wo", [P, 8, D], BF16)
        for kc in range(4):
            wload(wpa, lambda c0, c1, kc=kc: wpa.t[:, kc, c0:c1], wpa_d[l, kc * P:(kc + 1) * P, :], D)
            wload(wpb, lambda c0, c1, kc=kc: wpb.t[:, kc, c0:c1], wpb_d[l, kc * P:(kc + 1) * P, :], D)
        for kc in range(8):
            wload(wo, lambda c0, c1, kc=kc: wo.t[:, kc, c0:c1], wo_d[l, kc * P:(kc + 1) * P, :], D)
        yt = [k.tile(pes, f"yt{i}", [P, 8, 512], BF16) for i in range(2)]
        gt = [k.tile(pes, "gt0", [P, 16, 512], BF16)] * 2
        xt = [k.tile(pes, f"xt{i}", [P, 4, D], F32) for i in range(2)]
        xs = k.tile(pes, "xs", [P, 4, D], F32)
        mT = k.tile(pes, "mT", [P, 8, 512], BF16)
        hT = k.tile(pes, "hT", [P, 8, 512], BF16)
        ta = k.tile(pes, "ta", [P, 512], F32)
        tb = k.tile(pes, "tb", [P, 512], F32)
        tmp = [k.tile(pes, f"tmp{i}", [P, 512], F32) for i in range(2)]
        junk = k.tile(pes, "junk", [P, D], F32)
        ss = k.tile(pes, "ss", [P, 4], F32)
        lnv = k.tile(pes, "lnv", [P, 4], F32)
        rstd = k.tile(pes, "rstd", [P, 4], F32)
        nb = len(BLOCKS) - (1 if last else 0)

        def load(bi):
            r0, n, s = BLOCKS[bi]
            nt = n // P
            b = bi % 2
            k.dma("sp", yt[b].t[:, :, 0:n], yscr[:, :, r0:r0 + n].rearrange("j p n -> p j n"), yt[b],
                  reads=[k.dres("y0", bi), k.dres("y1", bi)], writes=[yt[b]])
            k.dma("sp", xt[b].t[:, 0:nt, :], xres_d[r0:r0 + n, :].rearrange("(t p) d -> p t d", p=P), xt[b],
                  reads=[k.dres("xres", bi)], writes=[xt[b]])
        load(0)
        for bi in range(nb):
            r0, n, s = BLOCKS[bi]
            nt = n // P
            if bi + 1 < nb:
                load(bi + 1)
            Y, G, X = yt[bi % 2], gt[0], xt[bi % 2]
            k.dma("sp", G.t[:, :, 0:n], gscr[:, :, r0:r0 + n].rearrange("j p n -> p j n"), G,
                  reads=[k.dres("g", bi)], writes=[G])
            for oc in range(8):
                pa, pb = ps[(oc % 2) * 2], ps[(oc % 2) * 2 + 1]
                mm(pa, pa.t[:, 0:n], [(wpa.t[:, kc, oc * P:(oc + 1) * P], Y.t[:, kc, 0:n]) for kc in range(4)], reads=[wpa, Y])
                mm(pb, pb.t[:, 0:n], [(wpb.t[:, kc, oc * P:(oc + 1) * P], Y.t[:, 4 + kc, 0:n]) for kc in range(4)],
                   reads=[wpb, Y])
                k.op("dve", lambda e, pa=pa, oc=oc: e.tensor_tensor(out=ta.t[:, 0:n], in0=pa.t[:, 0:n], in1=G.t[:, oc, 0:n],
                                                                    op=ALU.mult), reads=[pa, G], writes=[ta])
                k.op("dve", lambda e, pb=pb, oc=oc: e.tensor_tensor(out=tb.t[:, 0:n], in0=pb.t[:, 0:n], in1=G.t[:, 8 + oc, 0:n],
                                                                    op=ALU.mult), reads=[pb, G], writes=[tb])
                k.op("pool", lambda e, oc=oc: e.tensor_tensor(out=mT.t[:, oc, 0:n], in0=ta.t[:, 0:n], in1=tb.t[:, 0:n],
                                                              op=ALU.add), reads=[ta, tb], writes=[mT])
            i = 0
            for t in range(nt):
                for hf in range(2):
                    po = ps[4 + (i % 2)]
                    tm = tmp[i % 2]
                    i += 1
                    mm(po, po.t[:, :], [(mT.t[:, kc, t * P:(t + 1) * P], wo.t[:, kc, hf * 512:(hf + 1) * 512]) for kc in range(8)],
                       reads=[mT, wo])
                    k.op("dve", lambda e, po=po, tm=tm, hf=hf: e.tensor_tensor(
                        out=tm.t[:, :], in0=po.t[:, :], in1=gtb[0][s].t[:, hf * 512:(hf + 1) * 512], op=ALU.mult),
                        reads=[po, gtb[0][s]], writes=[tm])
                    k.op("pool", lambda e, tm=tm, t=t, hf=hf: e.tensor_tensor(
                        out=X.t[:, t, hf * 512:(hf + 1) * 512], in0=X.t[:, t, hf * 512:(hf + 1) * 512], in1=tm.t[:, :],
                        op=ALU.add), reads=[tm, X], writes=[X])
            k.dma("sp", xres_d[r0:r0 + n, :].rearrange("(t p) d -> p t d", p=P), X.t[:, 0:nt, :], X,
                  reads=[X], writes=[k.dres("xres", bi)])
            for t in range(nt):
                k.op("act", lambda e, t=t: e.activation(out=junk.t[:, :], in_=X.t[:, t, :], func=AF.Square),
                     reads=[X], writes=[junk])
                k.op("dve", lambda e, t=t: e.tensor_reduce(out=ss.t[:, t:t + 1], in_=junk.t[:, :], axis=mybir.AxisListType.X,
                                                           op=ALU.add), reads=[junk], writes=[ss])
            rstd_from_ss(ss, lnv, rstd, nt, 1.0 / D)
            for t in range(nt):
                k.op("dve", lambda e, t=t: e.tensor_scalar(out=xs.t[:, t, :], in0=X.t[:, t, :], scalar1=rstd.t[:, t:t + 1],
                                                           scalar2=None, op0=ALU.mult), reads=[X, rstd], writes=[xs])
            norm_to_hT(xs, hT, nt, s, 8, 24, ps[6], ps[7])
            k.dma("sp", hscr[:, :, r0:r0 + n].rearrange("j p n -> p j n"), hT.t[:, :, 0:n], hT,
                  reads=[hT], writes=[k.dres("h2", bi)])
        k.barrier()
        pes.close()

    def phase_ffn(l, half, last):
        pes = ExitStack()
        FH = DFF // 2
        NF = FH // P
        f0 = half * FH
        wg = k.tile(pes, "wg", [P, 8, FH], BF16)
        wu = k.tile(pes, "wu", [P, 8, FH], BF16)
        wd = k.tile(pes, "wd", [P, NF, D], BF16)
        for kc in range(8):
            wload(wg, lambda c0, c1, kc=kc: wg.t[:, kc, c0:c1], wg_d[l, kc * P:(kc + 1) * P, f0:f0 + FH], FH)
            wload(wu, lambda c0, c1, kc=kc: wu.t[:, kc, c0:c1], wu_d[l, kc * P:(kc + 1) * P, f0:f0 + FH], FH)
        for fc in range(NF):
            wload(wd, lambda c0, c1, fc=fc: wd.t[:, fc, c0:c1], wd_d[l, f0 + fc * P:f0 + (fc + 1) * P, :], D)
        ht = [k.tile(pes, f"ht{i}", [P, 8, 512], BF16) for i in range(2)]
        xt = [k.tile(pes, f"xt{i}", [P, 4, D], F32) for i in range(2)]
        aT = k.tile(pes, "aT", [P, NF, 512], BF16)
        sg = [k.tile(pes, f"sg{i}", [P, 512], F32) for i in range(2)]
        tmp = [k.tile(pes, f"tmp{i}", [P, 512], F32) for i in range(2)]
        nb = len(BLOCKS) - (1 if last else 0)

        def load(bi):
            r0, n, s = BLOCKS[bi]
            nt = n // P
            b = bi % 2
            k.dma("sp", ht[b].t[:, :, 0:n], hscr[:, :, r0:r0 + n].rearrange("j p n -> p j n"), ht[b],
                  reads=[k.dres("h2", bi)], writes=[ht[b]])
            k.dma("sp", xt[b].t[:, 0:nt, :], xres_d[r0:r0 + n, :].rearrange("(t p) d -> p t d", p=P), xt[b],
                  reads=[k.dres("xres", bi)], writes=[xt[b]])
        load(0)
        for bi in range(nb):
            r0, n, s = BLOCKS[bi]
            nt = n // P
            if bi + 1 < nb:
                load(bi + 1)
            H, X = ht[bi % 2], xt[bi % 2]
            for fc in range(NF):
                pg, pu = ps[(fc % 2) * 2], ps[(fc % 2) * 2 + 1]
                sgt = sg[fc % 2]
                mm(pg, pg.t[:, 0:n], [(wg.t[:, kc, fc * P:(fc + 1) * P], H.t[:, kc, 0:n]) for kc in range(8)], reads=[wg, H])
                mm(pu, pu.t[:, 0:n], [(wu.t[:, kc, fc * P:(fc + 1) * P], H.t[:, kc, 0:n]) for kc in range(8)], reads=[wu, H])
                k.op("act", lambda e, pg=pg, sgt=sgt: e.activation(out=sgt.t[:, 0:n], in_=pg.t[:, 0:n], func=AF.Silu),
                     reads=[pg], writes=[sgt])
                k.op("dve", lambda e, pu=pu, sgt=sgt, fc=fc: e.tensor_tensor(
                    out=aT.t[:, fc, 0:n], in0=pu.t[:, 0:n], in1=sgt.t[:, 0:n], op=ALU.mult), reads=[pu, sgt], writes=[aT])
            i = 0
            for t in range(nt):
                for hf in range(2):
                    po = ps[4 + (i % 2)]
                    tm = tmp[i % 2]
                    i += 1
                    mm(po, po.t[:, :], [(aT.t[:, fc, t * P:(t + 1) * P], wd.t[:, fc, hf * 512:(hf + 1) * 512]) for fc in range(NF)],
                       reads=[aT, wd])
                    k.op("dve", lambda e, po=po, tm=tm, hf=hf: e.tensor_tensor(
                        out=tm.t[:, :], in0=po.t[:, :], in1=gtb[1][s].t[:, hf * 512:(hf + 1) * 512], op=ALU.mult),
                        reads=[po, gtb[1][s]], writes=[tm])
                    k.op("pool", lambda e, tm=tm, t=t, hf=hf: e.tensor_tensor(
                        out=X.t[:, t, hf * 512:(hf + 1) * 512], in0=X.t[:, t, hf * 512:(hf + 1) * 512], in1=tm.t[:, :],
                        op=ALU.add), reads=[tm, X], writes=[X])
            k.dma("sp", xres_d[r0:r0 + n, :].rearrange("(t p) d -> p t d", p=P), X.t[:, 0:nt, :], X,
                  reads=[X], writes=[k.dres("xres", bi)])
        k.barrier()
        pes.close()

    def phase_final():
        pes = ExitStack()
        grow = k.tile(pes, "grow", [1, D], F32)
        gb = k.tile(pes, "gb", [P, D], F32)
        xt = [k.tile(pes, f"xt{i}", [P, 4, D], F32) for i in range(2)]
        junk = k.tile(pes, "junk", [P, D], F32)
        ss = k.tile(pes, "ss", [P, 4], F32)
        lnv = k.tile(pes, "lnv", [P, 4], F32)
        rstd = k.tile(pes, "rstd", [P, 4], F32)
        k.dma("sp", grow.t[:, :], gfin_d[:, :], grow, writes=[grow])
        for hf in range(2):
            mm(ps[hf], ps[hf].t[:, :], [(ones1.t[0:1, :], grow.t[0:1, hf * 512:(hf + 1) * 512])], reads=[ones1, grow])
            k.op("dve", lambda e, hf=hf: e.tensor_copy(out=gb.t[:, hf * 512:(hf + 1) * 512], in_=ps[hf].t[:, :]),
                 reads=[ps[hf]], writes=[gb])

        def load(bi):
            r0, n, s = BLOCKS[bi]
            k.dma("sp", xt[bi % 2].t[:, :, :], xres_d[r0:r0 + n, :].rearrange("(t p) d -> p t d", p=P), xt[bi % 2],
                  reads=[k.dres("xres", bi)], writes=[xt[bi % 2]])
        load(0)
        for bi in range(8):
            r0, n, s = BLOCKS[bi]
            if bi + 1 < 8:
                load(bi + 1)
            X = xt[bi % 2]
            for t in range(4):
                k.op("act", lambda e, t=t: e.activation(out=junk.t[:, :], in_=X.t[:, t, :], func=AF.Square),
                     reads=[X], writes=[junk])
                k.op("dve", lambda e, t=t: e.tensor_reduce(out=ss.t[:, t:t + 1], in_=junk.t[:, :], axis=mybir.AxisListType.X,
                                                           op=ALU.add), reads=[junk], writes=[ss])
            rstd_from_ss(ss, lnv, rstd, 4, 1.0 / D)
            for t in range(4):
                k.op("dve", lambda e, t=t: e.scalar_tensor_tensor(
                    out=X.t[:, t, :], in0=X.t[:, t, :], scalar=rstd.t[:, t:t + 1], in1=gb.t[:, :], op0=ALU.mult, op1=ALU.mult),
                    reads=[X, rstd, gb], writes=[X])
            k.dma("sp", out_d[r0:r0 + n, :].rearrange("(t p) d -> p t d", p=P), X.t[:, :, :], X,
                  reads=[X], writes=[k.dres("out", bi)])
        k.barrier()
        pes.close()

    k.barrier()
    for li, l in enumerate(layers):
        last = (l == L - 1)
        steps = [("mod", lambda: phase_mod(l)), ("A", lambda: phase_A(l)), ("attn0", lambda: phase_attn(l, 0, last)),
                 ("attn1", lambda: phase_attn(l, 1, last)), ("merge", lambda: phase_merge(l, last)),
                 ("ffn0", lambda: phase_ffn(l, 0, last)), ("ffn1", lambda: phase_ffn(l, 1, last))]
        stop = False
        for name, fn in steps:
            fn()
            if stop_after == name:
                stop = True
                break
        if stop:
            break
    if final_norm and stop_after is None:
        phase_final()
    k.barrier()
    es.close()
    return nc


def _rope_tables():
    t = np.arange(T)
    row = (t // GRID_W).astype(np.float32)
    col = (t % GRID_W).astype(np.float32)
    inv = (10000.0 ** (-np.arange(0, 32, 2, dtype=np.float32) / 32.0)).astype(np.float32)
    cosT = np.zeros((P, T), np.float32)
    sinT = np.zeros((P, T), np.float32)
    for p in range(P):
        j = p % 64
        axis, half, i = j // 32, (j % 32) // 16, j % 16
        ang = ((row if axis == 0 else col) * inv[i]).astype(np.float32)
        cosT[p] = np.cos(ang)
        sinT[p] = np.sin(ang) * (-1.0 if half == 0 else 1.0)
    return cosT, sinT


def _partner(j):
    return j + 16 if (j % 32) < 16 else j - 16


def _consts():
    c = np.zeros((P, 640), np.float32)
    c[:, 0:128] = np.eye(P, dtype=np.float32)
    for m in range(P):
        c[(m // 64) * 64 + _partner(m % 64), 128 + m] = 1.0
    c[0:64, 256:320] = 1.0
    c[64:128, 320:384] = 1.0
    kp = np.arange(P)[:, None]
    qp = np.arange(P)[None, :]
    c[:, 384:512] = (kp >= qp)
    c[:, 512:640] = (kp <= qp)
    sel = np.zeros((2, 256), np.float32)
    sel[0, 0:128] = 1.0
    sel[1, 128:256] = 1.0
    return c, sel


def _prep_inputs(inp):
    f = lambda a: np.ascontiguousarray(np.asarray(a, dtype=np.float32))
    w_in = f(inp["w_in"])
    perm = np.concatenate([np.arange(0, 512), np.arange(768, 1280), np.arange(512, 640), np.arange(1280, 1408),
                           np.arange(640, 768), np.arange(1408, 1536), np.arange(1536, 3584)])
    w_in_p = np.ascontiguousarray(w_in[:, :, perm])
    small = np.zeros((L, P, 32), np.float32)
    part = np.array([_partner(j) for j in range(64)])
    for l in range(L):
        small[l, :, 0:8] = f(inp["norm1_g"])[l].reshape(8, P).T
        small[l, :, 8:16] = f(inp["norm2_g"])[l].reshape(8, P).T
        qg = f(inp["q_norm_g"])[l]
        kg = f(inp["k_norm_g"])[l]
        small[l, :, 16] = np.tile(qg, 2)
        small[l, :, 17] = np.tile(qg[part], 2)
        small[l, :, 18] = np.tile(kg, 2)
        small[l, :, 19] = np.tile(kg[part], 2)
    cosT, sinT = _rope_tables()
    consts, sel = _consts()
    shared = {
        "w_ada": f(inp["w_ada"]), "b_ada": f(inp["b_ada"]), "small": small, "sink": f(inp["sink_a"]),
        "w_in": w_in_p, "w_pa": f(inp["w_proj_a"]), "w_pb": f(inp["w_proj_b"]), "w_o": f(inp["w_out"]),
        "w_g": f(inp["w_ffn_gate"]), "w_u": f(inp["w_ffn_up"]), "w_d": f(inp["w_ffn_down"]),
        "gfin": f(inp["final_norm_g"]).reshape(1, D), "consts": consts, "sel": sel, "cosT": cosT, "sinT": sinT,
    }
    return shared


_NC_CACHE = {}


def _get_nc(layers, final_norm):
    key = (tuple(layers), final_norm)
    if key not in _NC_CACHE:
        _NC_CACHE[key] = build(list(layers), final_norm)
    return _NC_CACHE[key]


FUSED = True


def kernel(**inp):
    shared = _prep_inputs(inp)
    x = np.asarray(inp["x"], dtype=np.float32)
    ctx = np.asarray(inp["ctx"], dtype=np.float32)
    c = np.asarray(inp["c"], dtype=np.float32)
    c_ctx = np.asarray(inp["c_ctx"], dtype=np.float32)
    B = x.shape[0]
    xs = [np.concatenate([x[b], ctx[b]], axis=0) for b in range(B)]
    ccs = [np.stack([c[b], c_ctx], axis=0) for b in range(B)]
    groups = [list(range(L))] if FUSED else [[l] for l in range(L)]
    out = None
    for gi, layers in enumerate(groups):
        fin = (layers[-1] == L - 1)
        nc = _get_nc(layers, fin)
        in_maps = []
        for core in range(NCORES):
            b = core % B
            m = dict(shared)
            m["x"] = np.ascontiguousarray(xs[b])
            m["cc"] = np.ascontiguousarray(ccs[b])
            in_maps.append(m)
        res = run_bass_kernel_spmd(nc, in_maps, core_ids=list(range(NCORES)))
        if fin:
            out = np.stack([res.results[b]["out"] for b in range(B)], axis=0)
        else:
            xs = [res.results[b]["out"] for b in range(B)]
    return out.astype(np.float32)
```

```python
import os
import numpy as np
from contextlib import ExitStack
DBGA = int(os.environ.get('DBGA', '99'))
import concourse.bass as bass
import concourse.mybir as mybir
from concourse.bass_utils import run_bass_kernel_spmd

F32 = mybir.dt.float32
BF16 = mybir.dt.bfloat16
ALU = mybir.AluOpType
AF = mybir.ActivationFunctionType

D = 1024
T = 4096
C = 256
NT = T + C
L = 4
HD = 64
DFF = 2816
P = 128
EPS = 1e-6
GRID_W = 64
NCORES = 8
BLOCKS = [(i * 512, 512, 0) for i in range(8)] + [(T, C, 1)]

QA0, QB0, KA0, KB0, V0, GA0, GB0, INW = 0, 512, 1024, 1152, 1280, 1536, 2560, 3584


class Res:
    __slots__ = ("name", "w", "r", "excl")

    def __init__(self, name):
        self.name = name
        self.w = None
        self.r = {}
        self.excl = False


class Tile:
    def __init__(self, t, name):
        self.t = t
        self.res = Res(name)
        self.dsem = None

    def __getitem__(self, idx):
        return self.t[idx]


class KB:
    def __init__(self, nc, es):
        self.nc = nc
        self.eng = {"pe": nc.tensor, "act": nc.scalar, "dve": nc.vector, "pool": nc.gpsimd, "sp": nc.sync}
        self.es = es
        self.sems = {k: [] for k in self.eng}
        self.cnt = {k: 0 for k in self.eng}
        self.seen = {k: {} for k in self.eng}
        self.dpool = [[es.enter_context(nc.semaphore(f"D{i}")), 0, i] for i in range(48)]
        self.dfree = list(range(48))
        self.dram_res = {}
        self.uid = 0

    def tile(self, es, name, shape, dtype):
        self.uid += 1
        t = es.enter_context(self.nc.sbuf_tensor(f"{name}_{self.uid}", list(shape), dtype))
        tl = Tile(t, name)
        es.callback(self._release, tl)
        return tl

    def _release(self, tl):
        if tl.dsem is not None:
            self.dfree.append(tl.dsem[2])
            tl.dsem = None

    def _dsem(self, tl):
        if tl.dsem is None:
            tl.dsem = self.dpool[self.dfree.pop(0)]
        return tl.dsem

    def dres(self, name, b):
        key = (name, b)
        if key not in self.dram_res:
            self.dram_res[key] = Res(f"{name}[{b}]")
        return self.dram_res[key]

    EPOCH = 1500

    def _etok(self, e):
        cnt = self.cnt[e]
        ep = (cnt - 1) // self.EPOCH
        while len(self.sems[e]) <= ep:
            self.sems[e].append(self.es.enter_context(self.nc.semaphore(f"S_{e}_{len(self.sems[e])}")))
        return (e, self.sems[e][ep], (cnt - ep * self.EPOCH, cnt))

    def _wait(self, e, tok):
        if tok is None:
            return
        key, sem, val = tok
        if key == e and e == "pe":
            return
        if isinstance(val, list):
            lval = gval = val[1]
        else:
            lval, gval = val
        if self.seen[e].get(key, 0) >= gval:
            return
        self.eng[e].wait_ge(sem, lval)
        self.seen[e][key] = gval

    def _deps(self, e, reads, writes):
        for r in reads:
            self._wait(e, r.w)
        for w in writes:
            self._wait(e, w.w)
            for t in list(w.r.values()):
                self._wait(e, t)

    def _mark(self, tok, reads, writes):
        for r in reads:
            r.r[tok[0]] = tok
        for w in writes:
            w.w = tok
            w.r = {}

    @staticmethod
    def _res(xs):
        return [x.res if isinstance(x, Tile) else x for x in xs]

    def op(self, e, fn, reads=(), writes=()):
        reads = self._res(reads)
        writes = self._res(writes)
        writes = writes + [r for r in reads if r.excl and r not in writes]
        self._deps(e, reads, writes)
        ins = fn(self.eng[e])
        self.cnt[e] += 1
        tok = self._etok(e)
        ins.then_inc(tok[1], 1)
        self._mark(tok, reads, writes)

    def dma(self, q, out, in_, sb, reads=(), writes=(), **kw):
        reads = self._res(reads)
        writes = self._res(writes)
        self._deps(q, reads, writes)
        ds = self._dsem(sb)
        ins = self.eng[q].dma_start(out=out, in_=in_, **kw)
        ds[1] += 16
        ins.then_inc(ds[0], 16)
        self._mark(("d%d" % ds[2], ds[0], ds), reads, writes)

    def barrier(self):
        toks = [self._etok(k) for k in self.eng if self.cnt[k] > 0]
        toks += [("d%d" % d[2], d[0], d) for d in self.dpool if d[1] > 0]
        for e in self.eng:
            for t in toks:
                self._wait(e, t)


def build(layers, final_norm, debug=False, stop_after=None):
    nc = bass.Bass("TRN2", target_bir_lowering=False)
    es = ExitStack()

    def din(name, shape, dt=F32):
        return nc.dram_tensor(name, list(shape), dt, kind="ExternalInput").ap()

    scr_kind = "ExternalOutput" if debug else "Internal"

    def dscr(name, shape, dt):
        return nc.dram_tensor(name, list(shape), dt, kind=scr_kind).ap()

    x_d = din("x", [NT, D])
    cc_d = din("cc", [2, D])
    wada_d = din("w_ada", [L, D, 6 * D])
    bada_d = din("b_ada", [L, 6 * D])
    small_d = din("small", [L, P, 32])
    sink_d = din("sink", [L, 8])
    win_d = din("w_in", [L, D, INW])
    wpa_d = din("w_pa", [L, 512, D])
    wpb_d = din("w_pb", [L, 512, D])
    wo_d = din("w_o", [L, D, D])
    wg_d = din("w_g", [L, D, DFF])
    wu_d = din("w_u", [L, D, DFF])
    wd_d = din("w_d", [L, DFF, D])
    gfin_d = din("gfin", [1, D])
    consts_d = din("consts", [P, 640])
    sel_d = din("sel", [2, 256])
    cos_d = din("cosT", [P, T])
    sin_d = din("sinT", [P, T])

    if final_norm:
        out_d = nc.dram_tensor("out", [T, D], F32, kind="ExternalOutput").ap()
        xres_d = dscr("xres", [NT, D], F32)
    else:
        xres_d = nc.dram_tensor("out", [NT, D], F32, kind="ExternalOutput").ap()
    qscr = dscr("qscr", [8, P, NT], BF16)
    kscr = dscr("kscr", [2, P, NT], BF16)
    vscr = dscr("vscr", [NT, 768], BF16)
    gscr = dscr("gscr", [16, P, NT], BF16)
    yscr = dscr("yscr", [8, P, NT], BF16)
    hscr = dscr("hscr", [8, P, NT], BF16)

    k = KB(nc, es)
    ps = []
    psd = []
    for i in range(4):
        t = es.enter_context(nc.psum_tensor(f"psd{i}", [P, 1024], F32))
        psd.append(Tile(t, f"psd{i}"))
        psd[-1].res.excl = True
        for hh in range(2):
            ps.append(Tile(t[:, hh * 512:(hh + 1) * 512], f"ps{2 * i + hh}"))
            ps[-1].res.excl = True

    cst = k.tile(es, "cst", [P, 640], F32)
    ident = cst.t[:, 0:128]
    bones = cst.t[:, 256:384]
    permb = k.tile(es, "permb", [P, P], BF16)
    maskb = k.tile(es, "maskb", [P, 2, P], BF16)
    sel = k.tile(es, "sel", [2, 256], F32)
    ones1 = k.tile(es, "ones1", [1, P], F32)
    silu_row = k.tile(es, "silu_row", [2, D], F32)
    siluT = k.tile(es, "siluT", [P, 8, 2], F32)
    modT = k.tile(es, "modT", [P, 48, 2], F32)
    gm = k.tile(es, "gm", [P, 16, 2], F32)
    small = k.tile(es, "small", [P, 32], F32)
    gtb = [[k.tile(es, f"gtb{g}{s}", [P, D], F32) for s in range(2)] for g in range(2)]
    sinkrow = k.tile(es, "sinkrow", [1, 8], F32)
    sinkexp = k.tile(es, "sinkexp", [P, 8], F32)
    setup_sem_tile = cst

    k.dma("sp", cst.t[:, :], consts_d[:, :], cst, writes=[cst])
    k.dma("sp", sel.t[:, :], sel_d[:, :], sel, writes=[sel])
    k.op("dve", lambda e: e.tensor_copy(out=permb.t[:, :], in_=cst.t[:, 128:256]), reads=[cst], writes=[permb])
    k.op("dve", lambda e: e.tensor_copy(out=maskb.t[:, :, :], in_=cst.t[:, 384:640].rearrange("p (a b) -> p a b", b=P)),
         reads=[cst], writes=[maskb])
    k.op("dve", lambda e: e.memset(ones1.t[:, :], 1.0), writes=[ones1])
    k.dma("sp", silu_row.t[:, :], cc_d[:, :], silu_row, writes=[silu_row])
    k.op("act", lambda e: e.activation(out=silu_row.t[:, :], in_=silu_row.t[:, :], func=AF.Silu),
         reads=[silu_row], writes=[silu_row])

    def _tr_silu(e):
        ins = None
        for kc in range(8):
            ins = e.transpose(out=ps[0].t[:, kc * 2:kc * 2 + 2], in_=silu_row.t[0:2, kc * P:(kc + 1) * P],
                              identity=cst.t[0:2, 0:2])
        return ins
    k.op("pe", _tr_silu, reads=[silu_row, cst], writes=[ps[0]])
    k.op("dve", lambda e: e.tensor_copy(out=siluT.t[:, :, :], in_=ps[0].t[:, 0:16].rearrange("p (a b) -> p a b", b=2)),
         reads=[ps[0]], writes=[siluT])

    for bi, (r0, n, s) in enumerate(BLOCKS):
        k.dma("sp", xres_d[r0:r0 + n, :], x_d[r0:r0 + n, :], setup_sem_tile, writes=[k.dres("xres", bi)])

    def wload(tl, dst_fn, src2d, ncols, maxc=2048):
        c0 = 0
        while c0 < ncols:
            c1 = min(ncols, c0 + maxc)
            k.dma("pool", dst_fn(c0, c1), src2d[:, c0:c1], tl, writes=[tl])
            c0 = c1

    def rstd_from_ss(ss, lnv, rstd, nt, inv_n):
        k.op("act", lambda e: e.activation(out=lnv.t[:, 0:nt], in_=ss.t[:, 0:nt], func=AF.Ln, scale=inv_n, bias=EPS),
             reads=[ss], writes=[lnv])
        k.op("act", lambda e: e.activation(out=rstd.t[:, 0:nt], in_=lnv.t[:, 0:nt], func=AF.Exp, scale=-0.5),
             reads=[lnv], writes=[rstd])

    def norm_to_hT(xs_tile, hT, nt, s, gmoff, shoff, psA, psB):
        for c in range(8):
            pst = psA if c % 2 == 0 else psB

            def _tr(e, c=c, pst=pst):
                ins = None
                for t in range(nt):
                    ins = e.transpose(out=pst.t[:, t * P:(t + 1) * P], in_=xs_tile.t[:, t, c * P:(c + 1) * P],
                                      identity=ident)
                return ins
            k.op("pe", _tr, reads=[xs_tile, cst], writes=[pst])
            k.op("act", lambda e, c=c, pst=pst: e.activation(
                out=hT.t[:, c, 0:nt * P], in_=pst.t[:, 0:nt * P], func=AF.Identity,
                scale=gm.t[:, gmoff + c, s:s + 1], bias=modT.t[:, shoff + c, s:s + 1]),
                reads=[pst, gm, modT], writes=[hT])

    def mm(pst, out_ap, pairs, reads, start=True, stop=True):
        def _f(e):
            ins = None
            n_ = len(pairs)
            for i, (l_, r_) in enumerate(pairs):
                ins = e.matmul(out_ap, lhsT=l_, rhs=r_, start=(start and i == 0), stop=(stop and i == n_ - 1))
            return ins
        k.op("pe", _f, reads=reads, writes=[pst])

    def phase_mod(l):
        pes = ExitStack()
        wa = [k.tile(pes, f"wada{i}", [P, 8, 512], F32) for i in range(2)]
        modrow = k.tile(pes, "modrow", [2, 6 * D], F32)
        bada2 = k.tile(pes, "bada2", [2, 6 * D], F32)
        k.dma("sp", bada2.t[0:1, :], bada_d[l:l + 1, :], bada2, writes=[bada2])
        k.dma("sp", bada2.t[1:2, :], bada_d[l:l + 1, :], bada2, writes=[bada2])
        k.dma("sp", small.t[:, :], small_d[l, :, :], small, writes=[small])
        k.dma("sp", sinkrow.t[:, :], sink_d[l:l + 1, :], sinkrow, writes=[sinkrow])
        wsrc = wada_d[l, :, :].rearrange("(kc p) n -> p kc n", p=P)

        def ld(cb):
            k.dma("sp", wa[cb % 2].t[:, :, :], wsrc[:, :, cb * 512:(cb + 1) * 512], wa[cb % 2], writes=[wa[cb % 2]])
        ld(0)
        for cb in range(12):
            if cb + 1 < 12:
                ld(cb + 1)
            w_ = wa[cb % 2]
            pst = ps[cb % 2]
            mm(pst, pst.t[0:2, :], [(siluT.t[:, kc, :], w_.t[:, kc, :]) for kc in range(8)], reads=[siluT, w_])
            k.op("dve", lambda e, cb=cb, pst=pst: e.tensor_tensor(
                out=modrow.t[:, cb * 512:(cb + 1) * 512], in0=pst.t[0:2, :], in1=bada2.t[:, cb * 512:(cb + 1) * 512],
                op=ALU.add), reads=[pst, bada2], writes=[modrow])
        for off in (D, 4 * D):
            k.op("dve", lambda e, off=off: e.tensor_scalar(out=modrow.t[:, off:off + D], in0=modrow.t[:, off:off + D],
                                                           scalar1=1.0, scalar2=None, op0=ALU.add),
                 reads=[modrow], writes=[modrow])

        def _tr(e):
            ins = None
            for j in range(48):
                ins = e.transpose(out=ps[2].t[:, j * 2:j * 2 + 2], in_=modrow.t[0:2, j * P:(j + 1) * P],
                                  identity=cst.t[0:2, 0:2])
            return ins
        k.op("pe", _tr, reads=[modrow, cst], writes=[ps[2]])
        k.op("dve", lambda e: e.tensor_copy(out=modT.t[:, :, :], in_=ps[2].t[:, 0:96].rearrange("p (a b) -> p a b", b=2)),
             reads=[ps[2]], writes=[modT])
        for s in range(2):
            k.op("dve", lambda e, s=s: e.tensor_tensor(out=gm.t[:, 0:8, s], in0=modT.t[:, 8:16, s], in1=small.t[:, 0:8],
                                                       op=ALU.mult), reads=[modT, small], writes=[gm])
            k.op("dve", lambda e, s=s: e.tensor_tensor(out=gm.t[:, 8:16, s], in0=modT.t[:, 32:40, s], in1=small.t[:, 8:16],
                                                       op=ALU.mult), reads=[modT, small], writes=[gm])
        i = 0
        for g, off in enumerate((2 * D, 5 * D)):
            for s in range(2):
                for hf in range(2):
                    pst = ps[3 + (i % 2)]
                    i += 1
                    mm(pst, pst.t[:, :], [(sel.t[0:2, s * P:(s + 1) * P], modrow.t[0:2, off + hf * 512:off + (hf + 1) * 512])],
                       reads=[sel, modrow])
                    k.op("dve", lambda e, g=g, s=s, hf=hf, pst=pst: e.tensor_copy(
                        out=gtb[g][s].t[:, hf * 512:(hf + 1) * 512], in_=pst.t[:, :]), reads=[pst], writes=[gtb[g][s]])
        mm(ps[5], ps[5].t[:, 0:8], [(ones1.t[0:1, :], sinkrow.t[0:1, :])], reads=[ones1, sinkrow])
        k.op("act", lambda e: e.activation(out=sinkexp.t[:, :], in_=ps[5].t[:, 0:8], func=AF.Exp),
             reads=[ps[5]], writes=[sinkexp])
        k.barrier()
        pes.close()

    def phase_A(l):
        pes = ExitStack()
        w = k.tile(pes, "w_in", [P, 8, INW], BF16)
        for kc in range(8):
            wload(w, lambda c0, c1, kc=kc: w.t[:, kc, c0:c1], win_d[l, kc * P:(kc + 1) * P, :], INW)
        xt = [k.tile(pes, f"xt{i}", [P, 4, D], F32) for i in range(2)]
        cs = [k.tile(pes, f"cs{i}", [P, 2, 512], F32) for i in range(2)]
        hT = k.tile(pes, "hT", [P, 8, 512], BF16)
        junk = k.tile(pes, "junk", [P, D], F32)
        ss = k.tile(pes, "ss", [P, 4], F32)
        lnv = k.tile(pes, "lnv", [P, 4], F32)
        rstd = k.tile(pes, "rstd", [P, 4], F32)
        zb = k.tile(pes, "zb", [P, 512], BF16)
        sq = k.tile(pes, "sq", [P, 512], F32)
        lnq = k.tile(pes, "lnq", [P, 512], F32)
        rq = k.tile(pes, "rq", [P, 512], F32)
        t1 = k.tile(pes, "t1", [P, 512], F32)
        t2 = k.tile(pes, "t2", [P, 512], F32)
        qout = k.tile(pes, "qout", [P, 8, 512], BF16)
        kout = k.tile(pes, "kout", [P, 2, 512], BF16)
        vout = k.tile(pes, "vout", [P, 4, 4, 192], BF16)
        gout = k.tile(pes, "gout", [P, 16, 512], BF16)
        k.op("dve", lambda e: e.memset(vout.t[:, :, :, :], 1.0), writes=[vout])

        def load(bi):
            r0, n, s = BLOCKS[bi]
            nt = n // P
            b = bi % 2
            k.dma("sp", xt[b].t[:, 0:nt, :], xres_d[r0:r0 + n, :].rearrange("(t p) d -> p t d", p=P), xt[b],
                  reads=[k.dres("xres", bi)], writes=[xt[b]])
            if s == 0:
                k.dma("sp", cs[b].t[:, 0, :], cos_d[:, r0:r0 + n], cs[b], writes=[cs[b]])
                k.dma("sp", cs[b].t[:, 1, :], sin_d[:, r0:r0 + n], cs[b], writes=[cs[b]])

        load(0)
        for bi, (r0, n, s) in enumerate(BLOCKS):
            if bi + 1 < len(BLOCKS):
                load(bi + 1)
            nt = n // P
            X = xt[bi % 2]
            CS = cs[bi % 2]
            rope = (s == 0)
            for t in range(nt):
                k.op("act", lambda e, t=t: e.activation(out=junk.t[:, :], in_=X.t[:, t, :], func=AF.Square),
                     reads=[X], writes=[junk])
                k.op("dve", lambda e, t=t: e.tensor_reduce(out=ss.t[:, t:t + 1], in_=junk.t[:, :], axis=mybir.AxisListType.X,
                                                           op=ALU.add), reads=[junk], writes=[ss])
            rstd_from_ss(ss, lnv, rstd, nt, 1.0 / D)
            for t in range(nt):
                k.op("dve", lambda e, t=t: e.tensor_scalar(out=X.t[:, t, :], in0=X.t[:, t, :], scalar1=rstd.t[:, t:t + 1],
                                                           scalar2=None, op0=ALU.mult), reads=[X, rstd], writes=[X])
            if DBGA < 2:
                break
            norm_to_hT(X, hT, nt, s, 0, 0, ps[0], ps[1])
            if DBGA < 3:
                break

            chunks = [(QA0 + j * P, qout, j, False, None) for j in range(4)]
            chunks += [(QB0 + j * P, qout, 4 + j, True, 16) for j in range(4)]
            chunks += [(KA0, kout, 0, False, None), (KB0, kout, 1, True, 18)]
            for ci, (col, dst, dj, isB, gcol) in enumerate(chunks):
                pz = ps[2 + (ci % 2)]
                mm(pz, pz.t[:, 0:n], [(w.t[:, kc, col:col + P], hT.t[:, kc, 0:n]) for kc in range(8)], reads=[w, hT])
                if isB:
                    k.op("act", lambda e, pz=pz: e.activation(out=sq.t[:, 0:n], in_=pz.t[:, 0:n], func=AF.Square),
                         reads=[pz], writes=[sq])
                    mm(ps[4], ps[4].t[:, 0:n], [(bones, sq.t[:, 0:n])], reads=[cst, sq])
                    k.op("act", lambda e: e.activation(out=lnq.t[:, 0:n], in_=ps[4].t[:, 0:n], func=AF.Ln,
                                                       scale=1.0 / HD, bias=EPS), reads=[ps[4]], writes=[lnq])
                    k.op("act", lambda e: e.activation(out=rq.t[:, 0:n], in_=lnq.t[:, 0:n], func=AF.Exp, scale=-0.5),
                         reads=[lnq], writes=[rq])
                g0 = small.t[:, gcol:gcol + 1] if isB else 1.0
                g1 = small.t[:, gcol + 1:gcol + 2] if isB else 1.0
                if rope:
                    k.op("act", lambda e, pz=pz: e.activation(out=zb.t[:, 0:n], in_=pz.t[:, 0:n], func=AF.Copy),
                         reads=[pz], writes=[zb])
                    mm(ps[5], ps[5].t[:, 0:n], [(permb.t[:, :], zb.t[:, 0:n])], reads=[permb, zb])
                    k.op("dve", lambda e, pz=pz, g0=g0: e.scalar_tensor_tensor(
                        out=t1.t[:, 0:n], in0=pz.t[:, 0:n], scalar=g0, in1=CS.t[:, 0, 0:n], op0=ALU.mult, op1=ALU.mult),
                        reads=[pz, CS, small], writes=[t1])
                    k.op("dve", lambda e, g1=g1: e.scalar_tensor_tensor(
                        out=t2.t[:, 0:n], in0=ps[5].t[:, 0:n], scalar=g1, in1=CS.t[:, 1, 0:n], op0=ALU.mult, op1=ALU.mult),
                        reads=[ps[5], CS, small], writes=[t2])
                    if isB:
                        k.op("dve", lambda e: e.tensor_tensor(out=t1.t[:, 0:n], in0=t1.t[:, 0:n], in1=t2.t[:, 0:n],
                                                               op=ALU.add), reads=[t1, t2], writes=[t1])
                        k.op("dve", lambda e, dst=dst, dj=dj: e.tensor_tensor(
                            out=dst.t[:, dj, 0:n], in0=t1.t[:, 0:n], in1=rq.t[:, 0:n], op=ALU.mult),
                            reads=[t1, rq], writes=[dst])
                    else:
                        k.op("dve", lambda e, dst=dst, dj=dj: e.tensor_tensor(
                            out=dst.t[:, dj, 0:n], in0=t1.t[:, 0:n], in1=t2.t[:, 0:n], op=ALU.add),
                            reads=[t1, t2], writes=[dst])
                else:
                    if isB:
                        k.op("dve", lambda e, pz=pz, g0=g0, dst=dst, dj=dj: e.scalar_tensor_tensor(
                            out=dst.t[:, dj, 0:n], in0=pz.t[:, 0:n], scalar=g0, in1=rq.t[:, 0:n], op0=ALU.mult,
                            op1=ALU.mult), reads=[pz, rq, small], writes=[dst])
                    else:
                        k.op("act", lambda e, pz=pz, dst=dst, dj=dj: e.activation(
                            out=dst.t[:, dj, 0:n], in_=pz.t[:, 0:n], func=AF.Copy), reads=[pz], writes=[dst])
            if DBGA < 4:
                break
            for t in range(nt):
                pv = ps[6 + (t % 2)]
                mm(pv, pv.t[:, 0:256], [(hT.t[:, kc, t * P:(t + 1) * P], w.t[:, kc, V0:V0 + 256]) for kc in range(8)],
                   reads=[w, hT])
                k.op("dve", lambda e, t=t, pv=pv: e.tensor_copy(
                    out=vout.t[:, t, :, 64:128], in_=pv.t[:, 0:256].rearrange("p (a b) -> p a b", b=64)),
                    reads=[pv], writes=[vout])
            if DBGA < 5:
                break
            for j in range(16):
                pg = ps[2 + (j % 2)]
                col = GA0 + j * P
                mm(pg, pg.t[:, 0:n], [(w.t[:, kc, col:col + P], hT.t[:, kc, 0:n]) for kc in range(8)], reads=[w, hT])
                k.op("act", lambda e, j=j, pg=pg: e.activation(out=gout.t[:, j, 0:n], in_=pg.t[:, 0:n], func=AF.Sigmoid),
                     reads=[pg], writes=[gout])
            if DBGA < 6:
                break
            k.dma("sp", qscr[:, :, r0:r0 + n].rearrange("j p n -> p j n"), qout.t[:, :, 0:n], qout,
                  reads=[qout], writes=[k.dres("q", bi)])
            k.dma("sp", kscr[:, :, r0:r0 + n].rearrange("j p n -> p j n"), kout.t[:, :, 0:n], kout,
                  reads=[kout], writes=[k.dres("k", bi)])
            k.dma("sp", vscr[r0:r0 + n, :].rearrange("(t p) f -> p t f", p=P),
                  vout.t[:, 0:nt, :, :].rearrange("p t a b -> p t (a b)"), vout, reads=[vout], writes=[k.dres("v", bi)])
            k.dma("sp", gscr[:, :, r0:r0 + n].rearrange("j p n -> p j n"), gout.t[:, :, 0:n], gout,
                  reads=[gout], writes=[k.dres("g", bi)])
        k.barrier()
        pes.close()

    def phase_attn(l, mixer, last):
        pes = ExitStack()
        Kp = k.tile(pes, "Kp", [P, 2, 2, NT], BF16)
        Va = k.tile(pes, "Va", [P, 34, 2, 192], BF16)
        qt = [k.tile(pes, f"qt{i}", [P, 4, 512], BF16) for i in range(2)]
        ptl = [k.tile(pes, f"pt{i}", [P, 2, 512], BF16) for i in range(3)]
        yout = k.tile(pes, "yout", [P, 4, 512], BF16)
        den = k.tile(pes, "den", [P, 512], F32)
        rec = k.tile(pes, "rec", [P, 512], F32)
        k.op("dve", lambda e: e.memset(Kp.t[:, :, :, :], 0.0), writes=[Kp])
        kreads = [k.dres("k", bi) for bi in range(len(BLOCKS))]
        vreads = [k.dres("v", bi) for bi in range(len(BLOCKS))]
        for kvh in range(2):
            for r in range(2):
                k.dma("sp", Kp.t[r * 64:(r + 1) * 64, r, kvh, :], kscr[mixer, kvh * 64:(kvh + 1) * 64, :], Kp,
                      reads=kreads, writes=[Kp])
        vsrc = vscr[:, mixer * 384:(mixer + 1) * 384].rearrange("(c p) f -> p c f", p=P)
        for c0 in range(0, 34, 9):
            c1 = min(34, c0 + 9)
            k.dma("sp", Va.t[:, c0:c1, :, :].rearrange("p c a b -> p c (a b)"), vsrc[:, c0:c1, :], Va,
                  reads=vreads, writes=[Va])

        blocks = list(range(len(BLOCKS)))

        def load(bi):
            r0, n, s = BLOCKS[bi]
            b = bi % 2
            k.dma("sp", qt[b].t[:, :, 0:n], qscr[mixer * 4:(mixer + 1) * 4, :, r0:r0 + n].rearrange("j p n -> p j n"),
                  qt[b], reads=[k.dres("q", bi)], writes=[qt[b]])
        load(0)
        for bi in blocks:
            r0, n, s = BLOCKS[bi]
            if bi + 1 < len(BLOCKS):
                load(bi + 1)
            Q = qt[bi % 2]
            sched = []
            if s == 1:
                sched = [(32, 0, n, []), (33, 0, n, [])]
            elif mixer == 1:
                sched = [(kc, 0, n, []) for kc in range(34)]
            else:
                sched = [(32, 0, n, []), (33, 0, n, [])]
                for kc in range(4 * bi - 1, 4 * bi + 5):
                    if kc < 0 or kc > 31:
                        continue
                    qlo = max(kc - 1, 4 * bi)
                    qhi = min(kc + 1, 4 * bi + 3)
                    masks = []
                    for qtile in range(qlo, qhi + 1):
                        if kc == qtile - 1:
                            masks.append((0, (qtile - 4 * bi) * P))
                        elif kc == qtile + 1:
                            masks.append((1, (qtile - 4 * bi) * P))
                    sched.append((kc, (qlo - 4 * bi) * P, (qhi + 1 - 4 * bi) * P, masks))
            items = []
            ii = 0
            while ii < len(sched):
                a = sched[ii]
                if ii + 1 < len(sched) and not a[3] and not sched[ii + 1][3] and a[1:3] == sched[ii + 1][1:3]:
                    items.append([a, sched[ii + 1]])
                    ii += 2
                else:
                    items.append([a])
                    ii += 1
            nit = len(items)
            for h in range(8):
                j, r, kvh = h // 2, h % 2, h // 4
                acc = ps[6 + (h % 2)]
                vcols = slice(64, 192) if r == 0 else slice(0, 128)
                SK = 2

                def s_mm(ii):
                    Dt = psd[ii % 3]

                    def _f(e, ii=ii, Dt=Dt):
                        ins = None
                        for jj, (kc, q0, q1, _) in enumerate(items[ii]):
                            ins = e.matmul(Dt.t[:, jj * 512 + q0:jj * 512 + q1], lhsT=Kp.t[:, r, kvh, kc * P:(kc + 1) * P],
                                           rhs=Q.t[:, j, q0:q1], start=True, stop=True)
                        return ins
                    k.op("pe", _f, reads=[Kp, Q], writes=[Dt])
                for ii in range(min(SK, nit)):
                    s_mm(ii)
                for ii, it in enumerate(items):
                    Dt = psd[ii % 3]
                    pt = ptl[ii % 3]
                    q0, q1 = it[0][1], it[0][2]
                    if len(it) == 2:
                        k.op("act", lambda e, Dt=Dt, pt=pt, q0=q0, q1=q1: e.activation(
                            out=pt.t[:, :, q0:q1], in_=Dt.t[:, :].rearrange("p (a b) -> p a b", b=512)[:, :, q0:q1],
                            func=AF.Exp, scale=HD ** -0.5), reads=[Dt], writes=[pt])
                    else:
                        k.op("act", lambda e, Dt=Dt, pt=pt, q0=q0, q1=q1: e.activation(
                            out=pt.t[:, 0, q0:q1], in_=Dt.t[:, q0:q1], func=AF.Exp, scale=HD ** -0.5),
                            reads=[Dt], writes=[pt])
                        for (mi, c0) in it[0][3]:
                            k.op("dve", lambda e, pt=pt, mi=mi, c0=c0: e.tensor_tensor(
                                out=pt.t[:, 0, c0:c0 + P], in0=pt.t[:, 0, c0:c0 + P], in1=maskb.t[:, mi, :], op=ALU.mult),
                                reads=[pt, maskb], writes=[pt])
                    if ii + SK < nit:
                        s_mm(ii + SK)

                    def _pv(e, ii=ii, it=it, pt=pt):
                        ins = None
                        for jj, (kc, q0_, q1_, _) in enumerate(it):
                            ins = e.matmul(acc.t[:, q0_:q1_], lhsT=Va.t[:, kc, kvh, vcols], rhs=pt.t[:, jj, q0_:q1_],
                                           start=(ii == 0 and jj == 0), stop=(ii == nit - 1 and jj == len(it) - 1))
                        return ins
                    k.op("pe", _pv, reads=[Va, pt], writes=[acc])
                drows = slice(64, 128) if r == 0 else slice(0, 64)
                nrows = slice(0, 64) if r == 0 else slice(64, 128)
                if mixer == 0:
                    k.op("dve", lambda e, acc=acc, drows=drows, h=h: e.tensor_scalar(
                        out=den.t[drows, 0:n], in0=acc.t[drows, 0:n], scalar1=sinkexp.t[drows, h:h + 1], scalar2=None,
                        op0=ALU.add), reads=[acc, sinkexp], writes=[den])
                    k.op("act", lambda e, drows=drows: e.activation(out=den.t[drows, 0:n], in_=den.t[drows, 0:n], func=AF.Ln),
                         reads=[den], writes=[den])
                else:
                    k.op("act", lambda e, acc=acc, drows=drows: e.activation(out=den.t[drows, 0:n], in_=acc.t[drows, 0:n],
                                                                             func=AF.Ln), reads=[acc], writes=[den])
                k.op("act", lambda e, drows=drows: e.activation(out=rec.t[drows, 0:n], in_=den.t[drows, 0:n], func=AF.Exp,
                                                                scale=-1.0), reads=[den], writes=[rec])
                k.op("dve", lambda e, acc=acc, drows=drows, nrows=nrows, j=j: e.tensor_tensor(
                    out=yout.t[nrows, j, 0:n], in0=acc.t[nrows, 0:n], in1=rec.t[drows, 0:n], op=ALU.mult),
                    reads=[acc, rec], writes=[yout])
            k.dma("sp", yscr[mixer * 4:(mixer + 1) * 4, :, r0:r0 + n].rearrange("j p n -> p j n"), yout.t[:, :, 0:n], yout,
                  reads=[yout], writes=[k.dres(f"y{mixer}", bi)])
        k.barrier()
        pes.close()

    def phase_merge(l, last):
        pes = ExitStack()
        wpa = k.tile(pes, "wpa", [P, 4, D], BF16)
        wpb = k.tile(pes, "wpb", [P, 4, D], BF16)
        wo = k.tile(pes, "wo", [P, 8, D], BF16)
        for kc in range(4):
            wload(wpa, lambda c0, c1, kc=kc: wpa.t[:, kc, c0:c1], wpa_d[l, kc * P:(kc + 1) * P, :], D)
            wload(wpb, lambda c0, c1, kc=kc: wpb.t[:, kc, c0:c1], wpb_d[l, kc * P:(kc + 1) * P, :], D)
        for kc in range(8):
            wload(wo, lambda c0, c1, kc=kc: wo.t[:, kc, c0:c1], wo_d[l, kc * P:(kc + 1) * P, :], D)
        yt = [k.tile(pes, f"yt{i}", [P, 8, 512], BF16) for i in range(2)]
        gt = [k.tile(pes, "gt0", [P, 16, 512], BF16)] * 2
        xt = [k.tile(pes, f"xt{i}", [P, 4, D], F32) for i in range(2)]
        xs = k.tile(pes, "xs", [P, 4, D], F32)
        mT = k.tile(pes, "mT", [P, 8, 512], BF16)
        hT = k.tile(pes, "hT", [P, 8, 512], BF16)
        ta = k.tile(pes, "ta", [P, 512], F32)
        tb = k.tile(pes, "tb", [P, 512], F32)
        tmp = [k.tile(pes, f"tmp{i}", [P, 512], F32) for i in range(2)]
        junk = k.tile(pes, "junk", [P, D], F32)
        ss = k.tile(pes, "ss", [P, 4], F32)
        lnv = k.tile(pes, "lnv", [P, 4], F32)
        rstd = k.tile(pes, "rstd", [P, 4], F32)
        nb = len(BLOCKS) - (1 if last else 0)

        def load(bi):
            r0, n, s = BLOCKS[bi]
            nt = n // P
            b = bi % 2
            k.dma("sp", yt[b].t[:, :, 0:n], yscr[:, :, r0:r0 + n].rearrange("j p n -> p j n"), yt[b],
                  reads=[k.dres("y0", bi), k.dres("y1", bi)], writes=[yt[b]])
            k.dma("sp", xt[b].t[:, 0:nt, :], xres_d[r0:r0 + n, :].rearrange("(t p) d -> p t d", p=P), xt[b],
                  reads=[k.dres("xres", bi)], writes=[xt[b]])
        load(0)
        for bi in range(nb):
            r0, n, s = BLOCKS[bi]
            nt = n // P
            if bi + 1 < nb:
                load(bi + 1)
            Y, G, X = yt[bi % 2], gt[0], xt[bi % 2]
            k.dma("sp", G.t[:, :, 0:n], gscr[:, :, r0:r0 + n].rearrange("j p n -> p j n"), G,
                  reads=[k.dres("g", bi)], writes=[G])
            for oc in range(8):
                pa, pb = ps[(oc % 2) * 2], ps[(oc % 2) * 2 + 1]
                mm(pa, pa.t[:, 0:n], [(wpa.t[:, kc, oc * P:(oc + 1) * P], Y.t[:, kc, 0:n]) for kc in range(4)], reads=[wpa, Y])
                mm(pb, pb.t[:, 0:n], [(wpb.t[:, kc, oc * P:(oc + 1) * P], Y.t[:, 4 + kc, 0:n]) for kc in range(4)],
                   reads=[wpb, Y])
                k.op("dve", lambda e, pa=pa, oc=oc: e.tensor_tensor(out=ta.t[:, 0:n], in0=pa.t[:, 0:n], in1=G.t[:, oc, 0:n],
                                                                    op=ALU.mult), reads=[pa, G], writes=[ta])
                k.op("dve", lambda e, pb=pb, oc=oc: e.tensor_tensor(out=tb.t[:, 0:n], in0=pb.t[:, 0:n], in1=G.t[:, 8 + oc, 0:n],
                                                                    op=ALU.mult), reads=[pb, G], writes=[tb])
                k.op("dve", lambda e, oc=oc: e.tensor_tensor(out=mT.t[:, oc, 0:n], in0=ta.t[:, 0:n], in1=tb.t[:, 0:n],
                                                              op=ALU.add), reads=[ta, tb], writes=[mT])
            i = 0
            for t in range(nt):
                for hf in range(2):
                    po = ps[4 + (i % 2)]
                    tm = tmp[i % 2]
                    i += 1
                    mm(po, po.t[:, :], [(mT.t[:, kc, t * P:(t + 1) * P], wo.t[:, kc, hf * 512:(hf + 1) * 512]) for kc in range(8)],
                       reads=[mT, wo])
                    k.op("dve", lambda e, po=po, tm=tm, hf=hf: e.tensor_tensor(
                        out=tm.t[:, :], in0=po.t[:, :], in1=gtb[0][s].t[:, hf * 512:(hf + 1) * 512], op=ALU.mult),
                        reads=[po, gtb[0][s]], writes=[tm])
                    k.op("dve", lambda e, tm=tm, t=t, hf=hf: e.tensor_tensor(
                        out=X.t[:, t, hf * 512:(hf + 1) * 512], in0=X.t[:, t, hf * 512:(hf + 1) * 512], in1=tm.t[:, :],
                        op=ALU.add), reads=[tm, X], writes=[X])
            k.dma("sp", xres_d[r0:r0 + n, :].rearrange("(t p) d -> p t d", p=P), X.t[:, 0:nt, :], X,
                  reads=[X], writes=[k.dres("xres", bi)])
            for t in range(nt):
                k.op("act", lambda e, t=t: e.activation(out=junk.t[:, :], in_=X.t[:, t, :], func=AF.Square),
                     reads=[X], writes=[junk])
                k.op("dve", lambda e, t=t: e.tensor_reduce(out=ss.t[:, t:t + 1], in_=junk.t[:, :], axis=mybir.AxisListType.X,
                                                           op=ALU.add), reads=[junk], writes=[ss])
            rstd_from_ss(ss, lnv, rstd, nt, 1.0 / D)
            for t in range(nt):
                k.op("dve", lambda e, t=t: e.tensor_scalar(out=xs.t[:, t, :], in0=X.t[:, t, :], scalar1=rstd.t[:, t:t + 1],
                                                           scalar2=None, op0=ALU.mult), reads=[X, rstd], writes=[xs])
            norm_to_hT(xs, hT, nt, s, 8, 24, ps[6], ps[7])
            k.dma("sp", hscr[:, :, r0:r0 + n].rearrange("j p n -> p j n"), hT.t[:, :, 0:n], hT,
                  reads=[hT], writes=[k.dres("h2", bi)])
        k.barrier()
        pes.close()

    def phase_ffn(l, half, last):
        pes = ExitStack()
        FH = DFF // 2
        NF = FH // P
        f0 = half * FH
        wg = k.tile(pes, "wg", [P, 8, FH], BF16)
        wu = k.tile(pes, "wu", [P, 8, FH], BF16)
        wd = k.tile(pes, "wd", [P, NF, D], BF16)
        for kc in range(8):
            wload(wg, lambda c0, c1, kc=kc: wg.t[:, kc, c0:c1], wg_d[l, kc * P:(kc + 1) * P, f0:f0 + FH], FH)
            wload(wu, lambda c0, c1, kc=kc: wu.t[:, kc, c0:c1], wu_d[l, kc * P:(kc + 1) * P, f0:f0 + FH], FH)
        for fc in range(NF):
            wload(wd, lambda c0, c1, fc=fc: wd.t[:, fc, c0:c1], wd_d[l, f0 + fc * P:f0 + (fc + 1) * P, :], D)
        ht = [k.tile(pes, f"ht{i}", [P, 8, 512], BF16) for i in range(2)]
        xt = [k.tile(pes, f"xt{i}", [P, 4, D], F32) for i in range(2)]
        aT = k.tile(pes, "aT", [P, NF, 512], BF16)
        sg = [k.tile(pes, f"sg{i}", [P, 512], F32) for i in range(2)]
        tmp = [k.tile(pes, f"tmp{i}", [P, 512], F32) for i in range(2)]
        nb = len(BLOCKS) - (1 if last else 0)

        def load(bi):
            r0, n, s = BLOCKS[bi]
            nt = n // P
            b = bi % 2
            k.dma("sp", ht[b].t[:, :, 0:n], hscr[:, :, r0:r0 + n].rearrange("j p n -> p j n"), ht[b],
                  reads=[k.dres("h2", bi)], writes=[ht[b]])
            k.dma("sp", xt[b].t[:, 0:nt, :], xres_d[r0:r0 + n, :].rearrange("(t p) d -> p t d", p=P), xt[b],
                  reads=[k.dres("xres", bi)], writes=[xt[b]])
        load(0)
        for bi in range(nb):
            r0, n, s = BLOCKS[bi]
            nt = n // P
            if bi + 1 < nb:
                load(bi + 1)
            H, X = ht[bi % 2], xt[bi % 2]
            for fc in range(NF):
                pg, pu = ps[(fc % 2) * 2], ps[(fc % 2) * 2 + 1]
                sgt = sg[fc % 2]
                mm(pg, pg.t[:, 0:n], [(wg.t[:, kc, fc * P:(fc + 1) * P], H.t[:, kc, 0:n]) for kc in range(8)], reads=[wg, H])
                mm(pu, pu.t[:, 0:n], [(wu.t[:, kc, fc * P:(fc + 1) * P], H.t[:, kc, 0:n]) for kc in range(8)], reads=[wu, H])
                k.op("act", lambda e, pg=pg, sgt=sgt: e.activation(out=sgt.t[:, 0:n], in_=pg.t[:, 0:n], func=AF.Silu),
                     reads=[pg], writes=[sgt])
                k.op("dve", lambda e, pu=pu, sgt=sgt, fc=fc: e.tensor_tensor(
                    out=aT.t[:, fc, 0:n], in0=pu.t[:, 0:n], in1=sgt.t[:, 0:n], op=ALU.mult), reads=[pu, sgt], writes=[aT])
            i = 0
            for t in range(nt):
                for hf in range(2):
                    po = ps[4 + (i % 2)]
                    tm = tmp[i % 2]
                    i += 1
                    mm(po, po.t[:, :], [(aT.t[:, fc, t * P:(t + 1) * P], wd.t[:, fc, hf * 512:(hf + 1) * 512]) for fc in range(NF)],
                       reads=[aT, wd])
                    k.op("dve", lambda e, po=po, tm=tm, hf=hf: e.tensor_tensor(
                        out=tm.t[:, :], in0=po.t[:, :], in1=gtb[1][s].t[:, hf * 512:(hf + 1) * 512], op=ALU.mult),
                        reads=[po, gtb[1][s]], writes=[tm])
                    k.op("dve", lambda e, tm=tm, t=t, hf=hf: e.tensor_tensor(
                        out=X.t[:, t, hf * 512:(hf + 1) * 512], in0=X.t[:, t, hf * 512:(hf + 1) * 512], in1=tm.t[:, :],
                        op=ALU.add), reads=[tm, X], writes=[X])
            k.dma("sp", xres_d[r0:r0 + n, :].rearrange("(t p) d -> p t d", p=P), X.t[:, 0:nt, :], X,
                  reads=[X], writes=[k.dres("xres", bi)])
        k.barrier()
        pes.close()

    def phase_final():
        pes = ExitStack()
        grow = k.tile(pes, "grow", [1, D], F32)
        gb = k.tile(pes, "gb", [P, D], F32)
        xt = [k.tile(pes, f"xt{i}", [P, 4, D], F32) for i in range(2)]
        junk = k.tile(pes, "junk", [P, D], F32)
        ss = k.tile(pes, "ss", [P, 4], F32)
        lnv = k.tile(pes, "lnv", [P, 4], F32)
        rstd = k.tile(pes, "rstd", [P, 4], F32)
        k.dma("sp", grow.t[:, :], gfin_d[:, :], grow, writes=[grow])
        for hf in range(2):
            mm(ps[hf], ps[hf].t[:, :], [(ones1.t[0:1, :], grow.t[0:1, hf * 512:(hf + 1) * 512])], reads=[ones1, grow])
            k.op("dve", lambda e, hf=hf: e.tensor_copy(out=gb.t[:, hf * 512:(hf + 1) * 512], in_=ps[hf].t[:, :]),
                 reads=[ps[hf]], writes=[gb])

        def load(bi):
            r0, n, s = BLOCKS[bi]
            k.dma("sp", xt[bi % 2].t[:, :, :], xres_d[r0:r0 + n, :].rearrange("(t p) d -> p t d", p=P), xt[bi % 2],
                  reads=[k.dres("xres", bi)], writes=[xt[bi % 2]])
        load(0)
        for bi in range(8):
            r0, n, s = BLOCKS[bi]
            if bi + 1 < 8:
                load(bi + 1)
            X = xt[bi % 2]
            for t in range(4):
                k.op("act", lambda e, t=t: e.activation(out=junk.t[:, :], in_=X.t[:, t, :], func=AF.Square),
                     reads=[X], writes=[junk])
                k.op("dve", lambda e, t=t: e.tensor_reduce(out=ss.t[:, t:t + 1], in_=junk.t[:, :], axis=mybir.AxisListType.X,
                                                           op=ALU.add), reads=[junk], writes=[ss])
            rstd_from_ss(ss, lnv, rstd, 4, 1.0 / D)
            for t in range(4):
                k.op("dve", lambda e, t=t: e.scalar_tensor_tensor(
                    out=X.t[:, t, :], in0=X.t[:, t, :], scalar=rstd.t[:, t:t + 1], in1=gb.t[:, :], op0=ALU.mult, op1=ALU.mult),
                    reads=[X, rstd, gb], writes=[X])
            k.dma("sp", out_d[r0:r0 + n, :].rearrange("(t p) d -> p t d", p=P), X.t[:, :, :], X,
                  reads=[X], writes=[k.dres("out", bi)])
        k.barrier()
        pes.close()

    k.barrier()
    for li, l in enumerate(layers):
        last = (l == L - 1)
        steps = [("mod", lambda: phase_mod(l)), ("A", lambda: phase_A(l)), ("attn0", lambda: phase_attn(l, 0, last)),
                 ("attn1", lambda: phase_attn(l, 1, last)), ("merge", lambda: phase_merge(l, last)),
                 ("ffn0", lambda: phase_ffn(l, 0, last)), ("ffn1", lambda: phase_ffn(l, 1, last))]
        stop = False
        for name, fn in steps:
            fn()
            if stop_after == name:
                stop = True
                break
        if stop:
            break
    if final_norm and stop_after is None:
        phase_final()
    k.barrier()
    es.close()
    return nc


def _rope_tables():
    t = np.arange(T)
    row = (t // GRID_W).astype(np.float32)
    col = (t % GRID_W).astype(np.float32)
    inv = (10000.0 ** (-np.arange(0, 32, 2, dtype=np.float32) / 32.0)).astype(np.float32)
    cosT = np.zeros((P, T), np.float32)
    sinT = np.zeros((P, T), np.float32)
    for p in range(P):
        j = p % 64
        axis, half, i = j // 32, (j % 32) // 16, j % 16
        ang = ((row if axis == 0 else col) * inv[i]).astype(np.float32)
        cosT[p] = np.cos(ang)
        sinT[p] = np.sin(ang) * (-1.0 if half == 0 else 1.0)
    return cosT, sinT


def _partner(j):
    return j + 16 if (j % 32) < 16 else j - 16


def _consts():
    c = np.zeros((P, 640), np.float32)
    c[:, 0:128] = np.eye(P, dtype=np.float32)
    for m in range(P):
        c[(m // 64) * 64 + _partner(m % 64), 128 + m] = 1.0
    c[0:64, 256:320] = 1.0
    c[64:128, 320:384] = 1.0
    kp = np.arange(P)[:, None]
    qp = np.arange(P)[None, :]
    c[:, 384:512] = (kp >= qp)
    c[:, 512:640] = (kp <= qp)
    sel = np.zeros((2, 256), np.float32)
    sel[0, 0:128] = 1.0
    sel[1, 128:256] = 1.0
    return c, sel


def _prep_inputs(inp):
    f = lambda a: np.ascontiguousarray(np.asarray(a, dtype=np.float32))
    w_in = f(inp["w_in"])
    perm = np.concatenate([np.arange(0, 512), np.arange(768, 1280), np.arange(512, 640), np.arange(1280, 1408),
                           np.arange(640, 768), np.arange(1408, 1536), np.arange(1536, 3584)])
    w_in_p = np.ascontiguousarray(w_in[:, :, perm])
    small = np.zeros((L, P, 32), np.float32)
    part = np.array([_partner(j) for j in range(64)])
    for l in range(L):
        small[l, :, 0:8] = f(inp["norm1_g"])[l].reshape(8, P).T
        small[l, :, 8:16] = f(inp["norm2_g"])[l].reshape(8, P).T
        qg = f(inp["q_norm_g"])[l]
        kg = f(inp["k_norm_g"])[l]
        small[l, :, 16] = np.tile(qg, 2)
        small[l, :, 17] = np.tile(qg[part], 2)
        small[l, :, 18] = np.tile(kg, 2)
        small[l, :, 19] = np.tile(kg[part], 2)
    cosT, sinT = _rope_tables()
    consts, sel = _consts()
    shared = {
        "w_ada": f(inp["w_ada"]), "b_ada": f(inp["b_ada"]), "small": small, "sink": f(inp["sink_a"]),
        "w_in": w_in_p, "w_pa": f(inp["w_proj_a"]), "w_pb": f(inp["w_proj_b"]), "w_o": f(inp["w_out"]),
        "w_g": f(inp["w_ffn_gate"]), "w_u": f(inp["w_ffn_up"]), "w_d": f(inp["w_ffn_down"]),
        "gfin": f(inp["final_norm_g"]).reshape(1, D), "consts": consts, "sel": sel, "cosT": cosT, "sinT": sinT,
    }
    return shared


_NC_CACHE = {}


def _get_nc(layers, final_norm):
    key = (tuple(layers), final_norm)
    if key not in _NC_CACHE:
        _NC_CACHE[key] = build(list(layers), final_norm)
    return _NC_CACHE[key]


FUSED = True


def kernel(**inp):
    shared = _prep_inputs(inp)
    x = np.asarray(inp["x"], dtype=np.float32)
    ctx = np.asarray(inp["ctx"], dtype=np.float32)
    c = np.asarray(inp["c"], dtype=np.float32)
    c_ctx = np.asarray(inp["c_ctx"], dtype=np.float32)
    B = x.shape[0]
    xs = [np.concatenate([x[b], ctx[b]], axis=0) for b in range(B)]
    ccs = [np.stack([c[b], c_ctx], axis=0) for b in range(B)]
    groups = [list(range(L))] if FUSED else [[l] for l in range(L)]
    out = None
    for gi, layers in enumerate(groups):
        fin = (layers[-1] == L - 1)
        nc = _get_nc(layers, fin)
        in_maps = []
        for core in range(NCORES):
            b = core % B
            m = dict(shared)
            m["x"] = np.ascontiguousarray(xs[b])
            m["cc"] = np.ascontiguousarray(ccs[b])
            in_maps.append(m)
        res = run_bass_kernel_spmd(nc, in_maps, core_ids=list(range(NCORES)))
        if fin:
            out = np.stack([res.results[b]["out"] for b in range(B)], axis=0)
        else:
            xs = [res.results[b]["out"] for b in range(B)]
    return out.astype(np.float32)
```

```python
import os
import numpy as np
from contextlib import ExitStack
DBGA = int(os.environ.get('DBGA', '99'))
import concourse.bass as bass
import concourse.mybir as mybir
from concourse.bass_utils import run_bass_kernel_spmd

F32 = mybir.dt.float32
BF16 = mybir.dt.bfloat16
ALU = mybir.AluOpType
AF = mybir.ActivationFunctionType

D = 1024
T = 4096
C = 256
NT = T + C
L = 4
HD = 64
DFF = 2816
P = 128
EPS = 1e-6
GRID_W = 64
NCORES = 8
BLOCKS = [(i * 512, 512, 0) for i in range(8)] + [(T, C, 1)]

QA0, QB0, KA0, KB0, V0, GA0, GB0, INW = 0, 512, 1024, 1152, 1280, 1536, 2560, 3584


class Res:
    __slots__ = ("name", "w", "r", "excl")

    def __init__(self, name):
        self.name = name
        self.w = None
        self.r = {}
        self.excl = False


class Tile:
    def __init__(self, t, name):
        self.t = t
        self.res = Res(name)
        self.dsem = None

    def __getitem__(self, idx):
        return self.t[idx]


class KB:
    def __init__(self, nc, es):
        self.nc = nc
        self.eng = {"pe": nc.tensor, "act": nc.scalar, "dve": nc.vector, "pool": nc.gpsimd, "sp": nc.sync}
        self.es = es
        self.sems = {k: [] for k in self.eng}
        self.cnt = {k: 0 for k in self.eng}
        self.seen = {k: {} for k in self.eng}
        self.dpool = [[es.enter_context(nc.semaphore(f"D{i}")), 0, i] for i in range(48)]
        self.dfree = list(range(48))
        self.dram_res = {}
        self.uid = 0

    def tile(self, es, name, shape, dtype):
        self.uid += 1
        t = es.enter_context(self.nc.sbuf_tensor(f"{name}_{self.uid}", list(shape), dtype))
        tl = Tile(t, name)
        es.callback(self._release, tl)
        return tl

    def _release(self, tl):
        if tl.dsem is not None:
            self.dfree.append(tl.dsem[2])
            tl.dsem = None

    def _dsem(self, tl):
        if tl.dsem is None:
            tl.dsem = self.dpool[self.dfree.pop(0)]
        return tl.dsem

    def dres(self, name, b):
        key = (name, b)
        if key not in self.dram_res:
            self.dram_res[key] = Res(f"{name}[{b}]")
        return self.dram_res[key]

    EPOCH = 1500

    def _etok(self, e):
        cnt = self.cnt[e]
        ep = (cnt - 1) // self.EPOCH
        while len(self.sems[e]) <= ep:
            self.sems[e].append(self.es.enter_context(self.nc.semaphore(f"S_{e}_{len(self.sems[e])}")))
        return (e, self.sems[e][ep], (cnt - ep * self.EPOCH, cnt))

    def _wait(self, e, tok):
        if tok is None:
            return
        key, sem, val = tok
        if key == e and e == "pe":
            return
        if isinstance(val, list):
            lval = gval = val[1]
        else:
            lval, gval = val
        if self.seen[e].get(key, 0) >= gval:
            return
        self.eng[e].wait_ge(sem, lval)
        self.seen[e][key] = gval

    def _deps(self, e, reads, writes):
        for r in reads:
            self._wait(e, r.w)
        for w in writes:
            self._wait(e, w.w)
            for t in list(w.r.values()):
                self._wait(e, t)

    def _mark(self, tok, reads, writes):
        for r in reads:
            r.r[tok[0]] = tok
        for w in writes:
            w.w = tok
            w.r = {}

    @staticmethod
    def _res(xs):
        return [x.res if isinstance(x, Tile) else x for x in xs]

    def op(self, e, fn, reads=(), writes=()):
        reads = self._res(reads)
        writes = self._res(writes)
        writes = writes + [r for r in reads if r.excl and r not in writes]
        self._deps(e, reads, writes)
        ins = fn(self.eng[e])
        self.cnt[e] += 1
        tok = self._etok(e)
        ins.then_inc(tok[1], 1)
        self._mark(tok, reads, writes)

    def dma(self, q, out, in_, sb, reads=(), writes=(), **kw):
        reads = self._res(reads)
        writes = self._res(writes)
        self._deps(q, reads, writes)
        ds = self._dsem(sb)
        ins = self.eng[q].dma_start(out=out, in_=in_, **kw)
        ds[1] += 16
        ins.then_inc(ds[0], 16)
        self._mark(("d%d" % ds[2], ds[0], ds), reads, writes)

    def barrier(self):
        toks = [self._etok(k) for k in self.eng if self.cnt[k] > 0]
        toks += [("d%d" % d[2], d[0], d) for d in self.dpool if d[1] > 0]
        for e in self.eng:
            for t in toks:
                self._wait(e, t)


def build(layers, final_norm, debug=False, stop_after=None):
    nc = bass.Bass("TRN2", target_bir_lowering=False)
    es = ExitStack()

    def din(name, shape, dt=F32):
        return nc.dram_tensor(name, list(shape), dt, kind="ExternalInput").ap()

    scr_kind = "ExternalOutput" if debug else "Internal"

    def dscr(name, shape, dt):
        return nc.dram_tensor(name, list(shape), dt, kind=scr_kind).ap()

    x_d = din("x", [NT, D])
    cc_d = din("cc", [2, D])
    wada_d = din("w_ada", [L, D, 6 * D])
    bada_d = din("b_ada", [L, 6 * D])
    small_d = din("small", [L, P, 32])
    sink_d = din("sink", [L, 8])
    win_d = din("w_in", [L, D, INW])
    wpa_d = din("w_pa", [L, 512, D])
    wpb_d = din("w_pb", [L, 512, D])
    wo_d = din("w_o", [L, D, D])
    wg_d = din("w_g", [L, D, DFF])
    wu_d = din("w_u", [L, D, DFF])
    wd_d = din("w_d", [L, DFF, D])
    gfin_d = din("gfin", [1, D])
    consts_d = din("consts", [P, 640])
    sel_d = din("sel", [2, 256])
    cos_d = din("cosT", [P, T])
    sin_d = din("sinT", [P, T])

    if final_norm:
        out_d = nc.dram_tensor("out", [T, D], F32, kind="ExternalOutput").ap()
        xres_d = dscr("xres", [NT, D], F32)
    else:
        xres_d = nc.dram_tensor("out", [NT, D], F32, kind="ExternalOutput").ap()
    qscr = dscr("qscr", [8, P, NT], BF16)
    kscr = dscr("kscr", [2, P, NT], BF16)
    vscr = dscr("vscr", [NT, 768], BF16)
    gscr = dscr("gscr", [16, P, NT], BF16)
    yscr = dscr("yscr", [8, P, NT], BF16)
    hscr = dscr("hscr", [8, P, NT], BF16)

    k = KB(nc, es)
    ps = []
    psd = []
    for i in range(4):
        t = es.enter_context(nc.psum_tensor(f"psd{i}", [P, 1024], F32))
        psd.append(Tile(t, f"psd{i}"))
        psd[-1].res.excl = True
        for hh in range(2):
            ps.append(Tile(t[:, hh * 512:(hh + 1) * 512], f"ps{2 * i + hh}"))
            ps[-1].res.excl = True

    cst = k.tile(es, "cst", [P, 640], F32)
    ident = cst.t[:, 0:128]
    bones = cst.t[:, 256:384]
    permb = k.tile(es, "permb", [P, P], BF16)
    maskb = k.tile(es, "maskb", [P, 2, P], BF16)
    sel = k.tile(es, "sel", [2, 256], F32)
    ones1 = k.tile(es, "ones1", [1, P], F32)
    silu_row = k.tile(es, "silu_row", [2, D], F32)
    siluT = k.tile(es, "siluT", [P, 8, 2], F32)
    modT = k.tile(es, "modT", [P, 48, 2], F32)
    gm = k.tile(es, "gm", [P, 16, 2], F32)
    small = k.tile(es, "small", [P, 32], F32)
    gtb = [[k.tile(es, f"gtb{g}{s}", [P, D], F32) for s in range(2)] for g in range(2)]
    sinkrow = k.tile(es, "sinkrow", [1, 8], F32)
    sinkexp = k.tile(es, "sinkexp", [P, 8], F32)
    setup_sem_tile = cst

    k.dma("sp", cst.t[:, :], consts_d[:, :], cst, writes=[cst])
    k.dma("sp", sel.t[:, :], sel_d[:, :], sel, writes=[sel])
    k.op("dve", lambda e: e.tensor_copy(out=permb.t[:, :], in_=cst.t[:, 128:256]), reads=[cst], writes=[permb])
    k.op("dve", lambda e: e.tensor_copy(out=maskb.t[:, :, :], in_=cst.t[:, 384:640].rearrange("p (a b) -> p a b", b=P)),
         reads=[cst], writes=[maskb])
    k.op("dve", lambda e: e.memset(ones1.t[:, :], 1.0), writes=[ones1])
    k.dma("sp", silu_row.t[:, :], cc_d[:, :], silu_row, writes=[silu_row])
    k.op("act", lambda e: e.activation(out=silu_row.t[:, :], in_=silu_row.t[:, :], func=AF.Silu),
         reads=[silu_row], writes=[silu_row])

    def _tr_silu(e):
        ins = None
        for kc in range(8):
            ins = e.transpose(out=ps[0].t[:, kc * 2:kc * 2 + 2], in_=silu_row.t[0:2, kc * P:(kc + 1) * P],
                              identity=cst.t[0:2, 0:2])
        return ins
    k.op("pe", _tr_silu, reads=[silu_row, cst], writes=[ps[0]])
    k.op("dve", lambda e: e.tensor_copy(out=siluT.t[:, :, :], in_=ps[0].t[:, 0:16].rearrange("p (a b) -> p a b", b=2)),
         reads=[ps[0]], writes=[siluT])

    for bi, (r0, n, s) in enumerate(BLOCKS):
        k.dma("sp", xres_d[r0:r0 + n, :], x_d[r0:r0 + n, :], setup_sem_tile, writes=[k.dres("xres", bi)])

    def wload(tl, dst_fn, src2d, ncols, maxc=2048):
        c0 = 0
        while c0 < ncols:
            c1 = min(ncols, c0 + maxc)
            k.dma("pool", dst_fn(c0, c1), src2d[:, c0:c1], tl, writes=[tl])
            c0 = c1

    def rstd_from_ss(ss, lnv, rstd, nt, inv_n):
        k.op("act", lambda e: e.activation(out=lnv.t[:, 0:nt], in_=ss.t[:, 0:nt], func=AF.Ln, scale=inv_n, bias=EPS),
             reads=[ss], writes=[lnv])
        k.op("act", lambda e: e.activation(out=rstd.t[:, 0:nt], in_=lnv.t[:, 0:nt], func=AF.Exp, scale=-0.5),
             reads=[lnv], writes=[rstd])

    def norm_to_hT(xs_tile, hT, nt, s, gmoff, shoff, psA, psB):
        for c in range(8):
            pst = psA if c % 2 == 0 else psB

            def _tr(e, c=c, pst=pst):
                ins = None
                for t in range(nt):
                    ins = e.transpose(out=pst.t[:, t * P:(t + 1) * P], in_=xs_tile.t[:, t, c * P:(c + 1) * P],
                                      identity=ident)
                return ins
            k.op("pe", _tr, reads=[xs_tile, cst], writes=[pst])
            k.op("act", lambda e, c=c, pst=pst: e.activation(
                out=hT.t[:, c, 0:nt * P], in_=pst.t[:, 0:nt * P], func=AF.Identity,
                scale=gm.t[:, gmoff + c, s:s + 1], bias=modT.t[:, shoff + c, s:s + 1]),
                reads=[pst, gm, modT], writes=[hT])

    def mm(pst, out_ap, pairs, reads, start=True, stop=True):
        def _f(e):
            ins = None
            n_ = len(pairs)
            for i, (l_, r_) in enumerate(pairs):
                ins = e.matmul(out_ap, lhsT=l_, rhs=r_, start=(start and i == 0), stop=(stop and i == n_ - 1))
            return ins
        k.op("pe", _f, reads=reads, writes=[pst])

    def phase_mod(l):
        pes = ExitStack()
        wa = [k.tile(pes, f"wada{i}", [P, 8, 512], F32) for i in range(2)]
        modrow = k.tile(pes, "modrow", [2, 6 * D], F32)
        bada2 = k.tile(pes, "bada2", [2, 6 * D], F32)
        k.dma("sp", bada2.t[0:1, :], bada_d[l:l + 1, :], bada2, writes=[bada2])
        k.dma("sp", bada2.t[1:2, :], bada_d[l:l + 1, :], bada2, writes=[bada2])
        k.dma("sp", small.t[:, :], small_d[l, :, :], small, writes=[small])
        k.dma("sp", sinkrow.t[:, :], sink_d[l:l + 1, :], sinkrow, writes=[sinkrow])
        wsrc = wada_d[l, :, :].rearrange("(kc p) n -> p kc n", p=P)

        def ld(cb):
            k.dma("sp", wa[cb % 2].t[:, :, :], wsrc[:, :, cb * 512:(cb + 1) * 512], wa[cb % 2], writes=[wa[cb % 2]])
        ld(0)
        for cb in range(12):
            if cb + 1 < 12:
                ld(cb + 1)
            w_ = wa[cb % 2]
            pst = ps[cb % 2]
            mm(pst, pst.t[0:2, :], [(siluT.t[:, kc, :], w_.t[:, kc, :]) for kc in range(8)], reads=[siluT, w_])
            k.op("dve", lambda e, cb=cb, pst=pst: e.tensor_tensor(
                out=modrow.t[:, cb * 512:(cb + 1) * 512], in0=pst.t[0:2, :], in1=bada2.t[:, cb * 512:(cb + 1) * 512],
                op=ALU.add), reads=[pst, bada2], writes=[modrow])
        for off in (D, 4 * D):
            k.op("dve", lambda e, off=off: e.tensor_scalar(out=modrow.t[:, off:off + D], in0=modrow.t[:, off:off + D],
                                                           scalar1=1.0, scalar2=None, op0=ALU.add),
                 reads=[modrow], writes=[modrow])

        def _tr(e):
            ins = None
            for j in range(48):
                ins = e.transpose(out=ps[2].t[:, j * 2:j * 2 + 2], in_=modrow.t[0:2, j * P:(j + 1) * P],
                                  identity=cst.t[0:2, 0:2])
            return ins
        k.op("pe", _tr, reads=[modrow, cst], writes=[ps[2]])
        k.op("dve", lambda e: e.tensor_copy(out=modT.t[:, :, :], in_=ps[2].t[:, 0:96].rearrange("p (a b) -> p a b", b=2)),
             reads=[ps[2]], writes=[modT])
        for s in range(2):
            k.op("dve", lambda e, s=s: e.tensor_tensor(out=gm.t[:, 0:8, s], in0=modT.t[:, 8:16, s], in1=small.t[:, 0:8],
                                                       op=ALU.mult), reads=[modT, small], writes=[gm])
            k.op("dve", lambda e, s=s: e.tensor_tensor(out=gm.t[:, 8:16, s], in0=modT.t[:, 32:40, s], in1=small.t[:, 8:16],
                                                       op=ALU.mult), reads=[modT, small], writes=[gm])
        i = 0
        for g, off in enumerate((2 * D, 5 * D)):
            for s in range(2):
                for hf in range(2):
                    pst = ps[3 + (i % 2)]
                    i += 1
                    mm(pst, pst.t[:, :], [(sel.t[0:2, s * P:(s + 1) * P], modrow.t[0:2, off + hf * 512:off + (hf + 1) * 512])],
                       reads=[sel, modrow])
                    k.op("dve", lambda e, g=g, s=s, hf=hf, pst=pst: e.tensor_copy(
                        out=gtb[g][s].t[:, hf * 512:(hf + 1) * 512], in_=pst.t[:, :]), reads=[pst], writes=[gtb[g][s]])
        mm(ps[5], ps[5].t[:, 0:8], [(ones1.t[0:1, :], sinkrow.t[0:1, :])], reads=[ones1, sinkrow])
        k.op("act", lambda e: e.activation(out=sinkexp.t[:, :], in_=ps[5].t[:, 0:8], func=AF.Exp),
             reads=[ps[5]], writes=[sinkexp])
        k.barrier()
        pes.close()

    def phase_A(l):
        pes = ExitStack()
        w = k.tile(pes, "w_in", [P, 8, INW], BF16)
        for kc in range(8):
            wload(w, lambda c0, c1, kc=kc: w.t[:, kc, c0:c1], win_d[l, kc * P:(kc + 1) * P, :], INW)
        xt = [k.tile(pes, f"xt{i}", [P, 4, D], F32) for i in range(2)]
        cs = [k.tile(pes, f"cs{i}", [P, 2, 512], F32) for i in range(2)]
        hTs = [k.tile(pes, f"hT{i}", [P, 8, 512], BF16) for i in range(2)]
        junk = k.tile(pes, "junk", [P, D], F32)
        ss = k.tile(pes, "ss", [P, 4], F32)
        lnv = k.tile(pes, "lnv", [P, 4], F32)
        rstd = k.tile(pes, "rstd", [P, 4], F32)
        zb = k.tile(pes, "zb", [P, 512], BF16)
        sq = k.tile(pes, "sq", [P, 512], F32)
        lnq = k.tile(pes, "lnq", [P, 512], F32)
        rq = k.tile(pes, "rq", [P, 512], F32)
        t1 = k.tile(pes, "t1", [P, 512], F32)
        t2 = k.tile(pes, "t2", [P, 512], F32)
        qout = k.tile(pes, "qout", [P, 8, 512], BF16)
        kout = k.tile(pes, "kout", [P, 2, 512], BF16)
        vout = k.tile(pes, "vout", [P, 4, 4, 192], BF16)
        gout = k.tile(pes, "gout", [P, 16, 512], BF16)
        k.op("dve", lambda e: e.memset(vout.t[:, :, :, :], 1.0), writes=[vout])

        def load(bi):
            r0, n, s = BLOCKS[bi]
            nt = n // P
            b = bi % 2
            k.dma("sp", xt[b].t[:, 0:nt, :], xres_d[r0:r0 + n, :].rearrange("(t p) d -> p t d", p=P), xt[b],
                  reads=[k.dres("xres", bi)], writes=[xt[b]])
            if s == 0:
                k.dma("sp", cs[b].t[:, 0, :], cos_d[:, r0:r0 + n], cs[b], writes=[cs[b]])
                k.dma("sp", cs[b].t[:, 1, :], sin_d[:, r0:r0 + n], cs[b], writes=[cs[b]])

        def body(bi):
            r0, n, s = BLOCKS[bi]
            nt = n // P
            X = xt[bi % 2]
            CS = cs[bi % 2]
            hT = hTs[bi % 2]
            rope = (s == 0)
            for t in range(nt):
                k.op("act", lambda e, t=t: e.activation(out=junk.t[:, :], in_=X.t[:, t, :], func=AF.Square),
                     reads=[X], writes=[junk])
                k.op("dve", lambda e, t=t: e.tensor_reduce(out=ss.t[:, t:t + 1], in_=junk.t[:, :], axis=mybir.AxisListType.X,
                                                           op=ALU.add), reads=[junk], writes=[ss])
            rstd_from_ss(ss, lnv, rstd, nt, 1.0 / D)
            for t in range(nt):
                k.op("dve", lambda e, t=t: e.tensor_scalar(out=X.t[:, t, :], in0=X.t[:, t, :], scalar1=rstd.t[:, t:t + 1],
                                                           scalar2=None, op0=ALU.mult), reads=[X, rstd], writes=[X])
            norm_to_hT(X, hT, nt, s, 0, 0, ps[0], ps[1])
            yield

            chunks = [(QA0 + j * P, qout, j, False, None) for j in range(4)]
            chunks += [(QB0 + j * P, qout, 4 + j, True, 16) for j in range(4)]
            chunks += [(KA0, kout, 0, False, None), (KB0, kout, 1, True, 18)]
            for ci, (col, dst, dj, isB, gcol) in enumerate(chunks):
                pz = ps[2 + (ci % 2)]
                mm(pz, pz.t[:, 0:n], [(w.t[:, kc, col:col + P], hT.t[:, kc, 0:n]) for kc in range(8)], reads=[w, hT])
                if isB:
                    k.op("act", lambda e, pz=pz: e.activation(out=sq.t[:, 0:n], in_=pz.t[:, 0:n], func=AF.Square),
                         reads=[pz], writes=[sq])
                    mm(ps[4], ps[4].t[:, 0:n], [(bones, sq.t[:, 0:n])], reads=[cst, sq])
                    k.op("act", lambda e: e.activation(out=lnq.t[:, 0:n], in_=ps[4].t[:, 0:n], func=AF.Ln,
                                                       scale=1.0 / HD, bias=EPS), reads=[ps[4]], writes=[lnq])
                    k.op("act", lambda e: e.activation(out=rq.t[:, 0:n], in_=lnq.t[:, 0:n], func=AF.Exp, scale=-0.5),
                         reads=[lnq], writes=[rq])
                g0 = small.t[:, gcol:gcol + 1] if isB else 1.0
                g1 = small.t[:, gcol + 1:gcol + 2] if isB else 1.0
                if rope:
                    k.op("act", lambda e, pz=pz: e.activation(out=zb.t[:, 0:n], in_=pz.t[:, 0:n], func=AF.Copy),
                         reads=[pz], writes=[zb])
                    mm(ps[5], ps[5].t[:, 0:n], [(permb.t[:, :], zb.t[:, 0:n])], reads=[permb, zb])
                    k.op("dve", lambda e, pz=pz, g0=g0: e.scalar_tensor_tensor(
                        out=t1.t[:, 0:n], in0=pz.t[:, 0:n], scalar=g0, in1=CS.t[:, 0, 0:n], op0=ALU.mult, op1=ALU.mult),
                        reads=[pz, CS, small], writes=[t1])
                    k.op("dve", lambda e, g1=g1: e.scalar_tensor_tensor(
                        out=t2.t[:, 0:n], in0=ps[5].t[:, 0:n], scalar=g1, in1=CS.t[:, 1, 0:n], op0=ALU.mult, op1=ALU.mult),
                        reads=[ps[5], CS, small], writes=[t2])
                    if isB:
                        k.op("dve", lambda e: e.tensor_tensor(out=t1.t[:, 0:n], in0=t1.t[:, 0:n], in1=t2.t[:, 0:n],
                                                               op=ALU.add), reads=[t1, t2], writes=[t1])
                        k.op("dve", lambda e, dst=dst, dj=dj: e.tensor_tensor(
                            out=dst.t[:, dj, 0:n], in0=t1.t[:, 0:n], in1=rq.t[:, 0:n], op=ALU.mult),
                            reads=[t1, rq], writes=[dst])
                    else:
                        k.op("dve", lambda e, dst=dst, dj=dj: e.tensor_tensor(
                            out=dst.t[:, dj, 0:n], in0=t1.t[:, 0:n], in1=t2.t[:, 0:n], op=ALU.add),
                            reads=[t1, t2], writes=[dst])
                else:
                    if isB:
                        k.op("dve", lambda e, pz=pz, g0=g0, dst=dst, dj=dj: e.scalar_tensor_tensor(
                            out=dst.t[:, dj, 0:n], in0=pz.t[:, 0:n], scalar=g0, in1=rq.t[:, 0:n], op0=ALU.mult,
                            op1=ALU.mult), reads=[pz, rq, small], writes=[dst])
                    else:
                        k.op("act", lambda e, pz=pz, dst=dst, dj=dj: e.activation(
                            out=dst.t[:, dj, 0:n], in_=pz.t[:, 0:n], func=AF.Copy), reads=[pz], writes=[dst])
            for t in range(nt):
                pv = ps[6 + (t % 2)]
                mm(pv, pv.t[:, 0:256], [(hT.t[:, kc, t * P:(t + 1) * P], w.t[:, kc, V0:V0 + 256]) for kc in range(8)],
                   reads=[w, hT])
                k.op("dve", lambda e, t=t, pv=pv: e.tensor_copy(
                    out=vout.t[:, t, :, 64:128], in_=pv.t[:, 0:256].rearrange("p (a b) -> p a b", b=64)),
                    reads=[pv], writes=[vout])
            yield
            for j in range(16):
                pg = ps[2 + (j % 2)]
                col = GA0 + j * P
                mm(pg, pg.t[:, 0:n], [(w.t[:, kc, col:col + P], hT.t[:, kc, 0:n]) for kc in range(8)], reads=[w, hT])
                k.op("act", lambda e, j=j, pg=pg: e.activation(out=gout.t[:, j, 0:n], in_=pg.t[:, 0:n], func=AF.Sigmoid),
                     reads=[pg], writes=[gout])
            k.dma("sp", qscr[:, :, r0:r0 + n].rearrange("j p n -> p j n"), qout.t[:, :, 0:n], qout,
                  reads=[qout], writes=[k.dres("q", bi)])
            k.dma("sp", kscr[:, :, r0:r0 + n].rearrange("j p n -> p j n"), kout.t[:, :, 0:n], kout,
                  reads=[kout], writes=[k.dres("k", bi)])
            k.dma("sp", vscr[r0:r0 + n, :].rearrange("(t p) f -> p t f", p=P),
                  vout.t[:, 0:nt, :, :].rearrange("p t a b -> p t (a b)"), vout, reads=[vout], writes=[k.dres("v", bi)])
            k.dma("sp", gscr[:, :, r0:r0 + n].rearrange("j p n -> p j n"), gout.t[:, :, 0:n], gout,
                  reads=[gout], writes=[k.dres("g", bi)])

        nbA = len(BLOCKS)
        gens = [body(bi) for bi in range(nbA)]
        load(0)
        load(1)
        next(gens[0])
        for bi in range(nbA):
            next(gens[bi])
            if bi + 1 < nbA:
                next(gens[bi + 1])
            next(gens[bi], None)
            if bi + 2 < nbA:
                load(bi + 2)
        k.barrier()
        pes.close()

    def phase_attn(l, mixer, last):
        pes = ExitStack()
        Kp = k.tile(pes, "Kp", [P, 2, 2, NT], BF16)
        Va = k.tile(pes, "Va", [P, 34, 2, 192], BF16)
        qt = [k.tile(pes, f"qt{i}", [P, 4, 512], BF16) for i in range(2)]
        ptl = [k.tile(pes, f"pt{i}", [P, 2, 512], BF16) for i in range(3)]
        yout = k.tile(pes, "yout", [P, 4, 512], BF16)
        den = k.tile(pes, "den", [P, 512], F32)
        rec = k.tile(pes, "rec", [P, 512], F32)
        k.op("dve", lambda e: e.memset(Kp.t[:, :, :, :], 0.0), writes=[Kp])
        kreads = [k.dres("k", bi) for bi in range(len(BLOCKS))]
        vreads = [k.dres("v", bi) for bi in range(len(BLOCKS))]
        for kvh in range(2):
            for r in range(2):
                k.dma("sp", Kp.t[r * 64:(r + 1) * 64, r, kvh, :], kscr[mixer, kvh * 64:(kvh + 1) * 64, :], Kp,
                      reads=kreads, writes=[Kp])
        vsrc = vscr[:, mixer * 384:(mixer + 1) * 384].rearrange("(c p) f -> p c f", p=P)
        for c0 in range(0, 34, 9):
            c1 = min(34, c0 + 9)
            k.dma("sp", Va.t[:, c0:c1, :, :].rearrange("p c a b -> p c (a b)"), vsrc[:, c0:c1, :], Va,
                  reads=vreads, writes=[Va])

        blocks = list(range(len(BLOCKS)))

        def load(bi):
            r0, n, s = BLOCKS[bi]
            b = bi % 2
            k.dma("sp", qt[b].t[:, :, 0:n], qscr[mixer * 4:(mixer + 1) * 4, :, r0:r0 + n].rearrange("j p n -> p j n"),
                  qt[b], reads=[k.dres("q", bi)], writes=[qt[b]])
        load(0)
        for bi in blocks:
            r0, n, s = BLOCKS[bi]
            if bi + 1 < len(BLOCKS):
                load(bi + 1)
            Q = qt[bi % 2]
            sched = []
            if s == 1:
                sched = [(32, 0, n, []), (33, 0, n, [])]
            elif mixer == 1:
                sched = [(kc, 0, n, []) for kc in range(34)]
            else:
                sched = [(32, 0, n, []), (33, 0, n, [])]
                for kc in range(4 * bi - 1, 4 * bi + 5):
                    if kc < 0 or kc > 31:
                        continue
                    qlo = max(kc - 1, 4 * bi)
                    qhi = min(kc + 1, 4 * bi + 3)
                    masks = []
                    for qtile in range(qlo, qhi + 1):
                        if kc == qtile - 1:
                            masks.append((0, (qtile - 4 * bi) * P))
                        elif kc == qtile + 1:
                            masks.append((1, (qtile - 4 * bi) * P))
                    sched.append((kc, (qlo - 4 * bi) * P, (qhi + 1 - 4 * bi) * P, masks))
            items = []
            ii = 0
            while ii < len(sched):
                a = sched[ii]
                if ii + 1 < len(sched) and not a[3] and not sched[ii + 1][3] and a[1:3] == sched[ii + 1][1:3]:
                    items.append([a, sched[ii + 1]])
                    ii += 2
                else:
                    items.append([a])
                    ii += 1
            nit = len(items)
            for h in range(8):
                j, r, kvh = h // 2, h % 2, h // 4
                acc = ps[6 + (h % 2)]
                vcols = slice(64, 192) if r == 0 else slice(0, 128)
                SK = 2

                def s_mm(ii):
                    Dt = psd[ii % 3]

                    def _f(e, ii=ii, Dt=Dt):
                        ins = None
                        for jj, (kc, q0, q1, _) in enumerate(items[ii]):
                            ins = e.matmul(Dt.t[:, jj * 512 + q0:jj * 512 + q1], lhsT=Kp.t[:, r, kvh, kc * P:(kc + 1) * P],
                                           rhs=Q.t[:, j, q0:q1], start=True, stop=True)
                        return ins
                    k.op("pe", _f, reads=[Kp, Q], writes=[Dt])
                for ii in range(min(SK, nit)):
                    s_mm(ii)
                for ii, it in enumerate(items):
                    Dt = psd[ii % 3]
                    pt = ptl[ii % 3]
                    q0, q1 = it[0][1], it[0][2]
                    if len(it) == 2:
                        k.op("act", lambda e, Dt=Dt, pt=pt, q0=q0, q1=q1: e.activation(
                            out=pt.t[:, :, q0:q1], in_=Dt.t[:, :].rearrange("p (a b) -> p a b", b=512)[:, :, q0:q1],
                            func=AF.Exp, scale=HD ** -0.5), reads=[Dt], writes=[pt])
                    else:
                        k.op("act", lambda e, Dt=Dt, pt=pt, q0=q0, q1=q1: e.activation(
                            out=pt.t[:, 0, q0:q1], in_=Dt.t[:, q0:q1], func=AF.Exp, scale=HD ** -0.5),
                            reads=[Dt], writes=[pt])
                        for (mi, c0) in it[0][3]:
                            k.op("dve", lambda e, pt=pt, mi=mi, c0=c0: e.tensor_tensor(
                                out=pt.t[:, 0, c0:c0 + P], in0=pt.t[:, 0, c0:c0 + P], in1=maskb.t[:, mi, :], op=ALU.mult),
                                reads=[pt, maskb], writes=[pt])
                    if ii + SK < nit:
                        s_mm(ii + SK)

                    def _pv(e, ii=ii, it=it, pt=pt):
                        ins = None
                        for jj, (kc, q0_, q1_, _) in enumerate(it):
                            ins = e.matmul(acc.t[:, q0_:q1_], lhsT=Va.t[:, kc, kvh, vcols], rhs=pt.t[:, jj, q0_:q1_],
                                           start=(ii == 0 and jj == 0), stop=(ii == nit - 1 and jj == len(it) - 1))
                        return ins
                    k.op("pe", _pv, reads=[Va, pt], writes=[acc])
                drows = slice(64, 128) if r == 0 else slice(0, 64)
                nrows = slice(0, 64) if r == 0 else slice(64, 128)
                if mixer == 0:
                    k.op("dve", lambda e, acc=acc, drows=drows, h=h: e.tensor_scalar(
                        out=den.t[drows, 0:n], in0=acc.t[drows, 0:n], scalar1=sinkexp.t[drows, h:h + 1], scalar2=None,
                        op0=ALU.add), reads=[acc, sinkexp], writes=[den])
                    k.op("act", lambda e, drows=drows: e.activation(out=den.t[drows, 0:n], in_=den.t[drows, 0:n], func=AF.Ln),
                         reads=[den], writes=[den])
                else:
                    k.op("act", lambda e, acc=acc, drows=drows: e.activation(out=den.t[drows, 0:n], in_=acc.t[drows, 0:n],
                                                                             func=AF.Ln), reads=[acc], writes=[den])
                k.op("act", lambda e, drows=drows: e.activation(out=rec.t[drows, 0:n], in_=den.t[drows, 0:n], func=AF.Exp,
                                                                scale=-1.0), reads=[den], writes=[rec])
                k.op("dve", lambda e, acc=acc, drows=drows, nrows=nrows, j=j: e.tensor_tensor(
                    out=yout.t[nrows, j, 0:n], in0=acc.t[nrows, 0:n], in1=rec.t[drows, 0:n], op=ALU.mult),
                    reads=[acc, rec], writes=[yout])
            k.dma("sp", yscr[mixer * 4:(mixer + 1) * 4, :, r0:r0 + n].rearrange("j p n -> p j n"), yout.t[:, :, 0:n], yout,
                  reads=[yout], writes=[k.dres(f"y{mixer}", bi)])
        k.barrier()
        pes.close()

    def phase_merge(l, last):
        pes = ExitStack()
        wpa = k.tile(pes, "wpa", [P, 4, D], BF16)
        wpb = k.tile(pes, "wpb", [P, 4, D], BF16)
        wo = k.tile(pes, "wo", [P, 8, D], BF16)
        for kc in range(4):
            wload(wpa, lambda c0, c1, kc=kc: wpa.t[:, kc, c0:c1], wpa_d[l, kc * P:(kc + 1) * P, :], D)
            wload(wpb, lambda c0, c1, kc=kc: wpb.t[:, kc, c0:c1], wpb_d[l, kc * P:(kc + 1) * P, :], D)
        for kc in range(8):
            wload(wo, lambda c0, c1, kc=kc: wo.t[:, kc, c0:c1], wo_d[l, kc * P:(kc + 1) * P, :], D)
        yt = [k.tile(pes, f"yt{i}", [P, 8, 512], BF16) for i in range(2)]
        gt = [k.tile(pes, "gt0", [P, 16, 512], BF16)] * 2
        xt = [k.tile(pes, f"xt{i}", [P, 4, D], F32) for i in range(3)]
        xs = k.tile(pes, "xs", [P, 4, D], F32)
        mT = k.tile(pes, "mT", [P, 8, 512], BF16)
        hT = k.tile(pes, "hT", [P, 8, 512], BF16)
        ta = k.tile(pes, "ta", [P, 512], F32)
        tb = k.tile(pes, "tb", [P, 512], F32)
        tmp = [k.tile(pes, f"tmp{i}", [P, 512], F32) for i in range(2)]
        junk = k.tile(pes, "junk", [P, D], F32)
        ss = k.tile(pes, "ss", [P, 4], F32)
        lnv = k.tile(pes, "lnv", [P, 4], F32)
        rstd = k.tile(pes, "rstd", [P, 4], F32)
        nb = len(BLOCKS) - (1 if last else 0)

        def load(bi):
            r0, n, s = BLOCKS[bi]
            nt = n // P
            b = bi % 2
            k.dma("sp", yt[b].t[:, :, 0:n], yscr[:, :, r0:r0 + n].rearrange("j p n -> p j n"), yt[b],
                  reads=[k.dres("y0", bi), k.dres("y1", bi)], writes=[yt[b]])
            k.dma("sp", xt[bi % 3].t[:, 0:nt, :], xres_d[r0:r0 + n, :].rearrange("(t p) d -> p t d", p=P), xt[bi % 3],
                  reads=[k.dres("xres", bi)], writes=[xt[bi % 3]])

        def body(bi):
            r0, n, s = BLOCKS[bi]
            nt = n // P
            Y, G, X = yt[bi % 2], gt[0], xt[bi % 3]
            k.dma("sp", G.t[:, :, 0:n], gscr[:, :, r0:r0 + n].rearrange("j p n -> p j n"), G,
                  reads=[k.dres("g", bi)], writes=[G])
            for oc in range(8):
                pa, pb = ps[(oc % 2) * 2], ps[(oc % 2) * 2 + 1]
                mm(pa, pa.t[:, 0:n], [(wpa.t[:, kc, oc * P:(oc + 1) * P], Y.t[:, kc, 0:n]) for kc in range(4)], reads=[wpa, Y])
                mm(pb, pb.t[:, 0:n], [(wpb.t[:, kc, oc * P:(oc + 1) * P], Y.t[:, 4 + kc, 0:n]) for kc in range(4)],
                   reads=[wpb, Y])
                k.op("dve", lambda e, pa=pa, oc=oc: e.tensor_tensor(out=ta.t[:, 0:n], in0=pa.t[:, 0:n], in1=G.t[:, oc, 0:n],
                                                                    op=ALU.mult), reads=[pa, G], writes=[ta])
                k.op("dve", lambda e, pb=pb, oc=oc: e.tensor_tensor(out=tb.t[:, 0:n], in0=pb.t[:, 0:n], in1=G.t[:, 8 + oc, 0:n],
                                                                    op=ALU.mult), reads=[pb, G], writes=[tb])
                k.op("dve", lambda e, oc=oc: e.tensor_tensor(out=mT.t[:, oc, 0:n], in0=ta.t[:, 0:n], in1=tb.t[:, 0:n],
                                                              op=ALU.add), reads=[ta, tb], writes=[mT])
            i = 0
            for t in range(nt):
                for hf in range(2):
                    po = ps[4 + (i % 2)]
                    tm = tmp[i % 2]
                    i += 1
                    mm(po, po.t[:, :], [(mT.t[:, kc, t * P:(t + 1) * P], wo.t[:, kc, hf * 512:(hf + 1) * 512]) for kc in range(8)],
                       reads=[mT, wo])
                    k.op("dve", lambda e, po=po, tm=tm, hf=hf: e.tensor_tensor(
                        out=tm.t[:, :], in0=po.t[:, :], in1=gtb[0][s].t[:, hf * 512:(hf + 1) * 512], op=ALU.mult),
                        reads=[po, gtb[0][s]], writes=[tm])
                    k.op("dve", lambda e, tm=tm, t=t, hf=hf: e.tensor_tensor(
                        out=X.t[:, t, hf * 512:(hf + 1) * 512], in0=X.t[:, t, hf * 512:(hf + 1) * 512], in1=tm.t[:, :],
                        op=ALU.add), reads=[tm, X], writes=[X])
            k.dma("sp", xres_d[r0:r0 + n, :].rearrange("(t p) d -> p t d", p=P), X.t[:, 0:nt, :], X,
                  reads=[X], writes=[k.dres("xres", bi)])
            yield
            for t in range(nt):
                k.op("act", lambda e, t=t: e.activation(out=junk.t[:, :], in_=X.t[:, t, :], func=AF.Square),
                     reads=[X], writes=[junk])
                k.op("dve", lambda e, t=t: e.tensor_reduce(out=ss.t[:, t:t + 1], in_=junk.t[:, :], axis=mybir.AxisListType.X,
                                                           op=ALU.add), reads=[junk], writes=[ss])
            rstd_from_ss(ss, lnv, rstd, nt, 1.0 / D)
            for t in range(nt):
                k.op("dve", lambda e, t=t: e.tensor_scalar(out=xs.t[:, t, :], in0=X.t[:, t, :], scalar1=rstd.t[:, t:t + 1],
                                                           scalar2=None, op0=ALU.mult), reads=[X, rstd], writes=[xs])
            norm_to_hT(xs, hT, nt, s, 8, 24, ps[6], ps[7])
            k.dma("sp", hscr[:, :, r0:r0 + n].rearrange("j p n -> p j n"), hT.t[:, :, 0:n], hT,
                  reads=[hT], writes=[k.dres("h2", bi)])

        gens = [body(bi) for bi in range(nb)]
        load(0)
        load(1)
        for bi in range(nb):
            next(gens[bi])
            if bi >= 1:
                next(gens[bi - 1], None)
            if bi + 2 < nb:
                load(bi + 2)
        next(gens[nb - 1], None)
        k.barrier()
        pes.close()

    def phase_ffn(l, half, last):
        pes = ExitStack()
        FH = DFF // 2
        NF = FH // P
        f0 = half * FH
        wg = k.tile(pes, "wg", [P, 8, FH], BF16)
        wu = k.tile(pes, "wu", [P, 8, FH], BF16)
        wd = k.tile(pes, "wd", [P, NF, D], BF16)
        for kc in range(8):
            wload(wg, lambda c0, c1, kc=kc: wg.t[:, kc, c0:c1], wg_d[l, kc * P:(kc + 1) * P, f0:f0 + FH], FH)
            wload(wu, lambda c0, c1, kc=kc: wu.t[:, kc, c0:c1], wu_d[l, kc * P:(kc + 1) * P, f0:f0 + FH], FH)
        for fc in range(NF):
            wload(wd, lambda c0, c1, fc=fc: wd.t[:, fc, c0:c1], wd_d[l, f0 + fc * P:f0 + (fc + 1) * P, :], D)
        ht = [k.tile(pes, f"ht{i}", [P, 8, 512], BF16) for i in range(2)]
        xt = [k.tile(pes, f"xt{i}", [P, 4, D], F32) for i in range(2)]
        aT = k.tile(pes, "aT", [P, NF, 512], BF16)
        sg = [k.tile(pes, f"sg{i}", [P, 512], F32) for i in range(2)]
        tmp = [k.tile(pes, f"tmp{i}", [P, 512], F32) for i in range(2)]
        nb = len(BLOCKS) - (1 if last else 0)

        def load(bi):
            r0, n, s = BLOCKS[bi]
            nt = n // P
            b = bi % 2
            k.dma("sp", ht[b].t[:, :, 0:n], hscr[:, :, r0:r0 + n].rearrange("j p n -> p j n"), ht[b],
                  reads=[k.dres("h2", bi)], writes=[ht[b]])
            k.dma("sp", xt[b].t[:, 0:nt, :], xres_d[r0:r0 + n, :].rearrange("(t p) d -> p t d", p=P), xt[b],
                  reads=[k.dres("xres", bi)], writes=[xt[b]])
        load(0)
        for bi in range(nb):
            r0, n, s = BLOCKS[bi]
            nt = n // P
            if bi + 1 < nb:
                load(bi + 1)
            H, X = ht[bi % 2], xt[bi % 2]
            for fc in range(NF):
                pg, pu = ps[(fc % 2) * 2], ps[(fc % 2) * 2 + 1]
                sgt = sg[fc % 2]
                mm(pg, pg.t[:, 0:n], [(wg.t[:, kc, fc * P:(fc + 1) * P], H.t[:, kc, 0:n]) for kc in range(8)], reads=[wg, H])
                mm(pu, pu.t[:, 0:n], [(wu.t[:, kc, fc * P:(fc + 1) * P], H.t[:, kc, 0:n]) for kc in range(8)], reads=[wu, H])
                k.op("act", lambda e, pg=pg, sgt=sgt: e.activation(out=sgt.t[:, 0:n], in_=pg.t[:, 0:n], func=AF.Silu),
                     reads=[pg], writes=[sgt])
                k.op("dve", lambda e, pu=pu, sgt=sgt, fc=fc: e.tensor_tensor(
                    out=aT.t[:, fc, 0:n], in0=pu.t[:, 0:n], in1=sgt.t[:, 0:n], op=ALU.mult), reads=[pu, sgt], writes=[aT])
            i = 0
            for t in range(nt):
                for hf in range(2):
                    po = ps[4 + (i % 2)]
                    tm = tmp[i % 2]
                    i += 1
                    mm(po, po.t[:, :], [(aT.t[:, fc, t * P:(t + 1) * P], wd.t[:, fc, hf * 512:(hf + 1) * 512]) for fc in range(NF)],
                       reads=[aT, wd])
                    k.op("dve", lambda e, po=po, tm=tm, hf=hf: e.tensor_tensor(
                        out=tm.t[:, :], in0=po.t[:, :], in1=gtb[1][s].t[:, hf * 512:(hf + 1) * 512], op=ALU.mult),
                        reads=[po, gtb[1][s]], writes=[tm])
                    k.op("dve", lambda e, tm=tm, t=t, hf=hf: e.tensor_tensor(
                        out=X.t[:, t, hf * 512:(hf + 1) * 512], in0=X.t[:, t, hf * 512:(hf + 1) * 512], in1=tm.t[:, :],
                        op=ALU.add), reads=[tm, X], writes=[X])
            k.dma("sp", xres_d[r0:r0 + n, :].rearrange("(t p) d -> p t d", p=P), X.t[:, 0:nt, :], X,
                  reads=[X], writes=[k.dres("xres", bi)])
        k.barrier()
        pes.close()

    def phase_final():
        pes = ExitStack()
        grow = k.tile(pes, "grow", [1, D], F32)
        gb = k.tile(pes, "gb", [P, D], F32)
        xt = [k.tile(pes, f"xt{i}", [P, 4, D], F32) for i in range(2)]
        junk = k.tile(pes, "junk", [P, D], F32)
        ss = k.tile(pes, "ss", [P, 4], F32)
        lnv = k.tile(pes, "lnv", [P, 4], F32)
        rstd = k.tile(pes, "rstd", [P, 4], F32)
        k.dma("sp", grow.t[:, :], gfin_d[:, :], grow, writes=[grow])
        for hf in range(2):
            mm(ps[hf], ps[hf].t[:, :], [(ones1.t[0:1, :], grow.t[0:1, hf * 512:(hf + 1) * 512])], reads=[ones1, grow])
            k.op("dve", lambda e, hf=hf: e.tensor_copy(out=gb.t[:, hf * 512:(hf + 1) * 512], in_=ps[hf].t[:, :]),
                 reads=[ps[hf]], writes=[gb])

        def load(bi):
            r0, n, s = BLOCKS[bi]
            k.dma("sp", xt[bi % 2].t[:, :, :], xres_d[r0:r0 + n, :].rearrange("(t p) d -> p t d", p=P), xt[bi % 2],
                  reads=[k.dres("xres", bi)], writes=[xt[bi % 2]])
        load(0)
        for bi in range(8):
            r0, n, s = BLOCKS[bi]
            if bi + 1 < 8:
                load(bi + 1)
            X = xt[bi % 2]
            for t in range(4):
                k.op("act", lambda e, t=t: e.activation(out=junk.t[:, :], in_=X.t[:, t, :], func=AF.Square),
                     reads=[X], writes=[junk])
                k.op("dve", lambda e, t=t: e.tensor_reduce(out=ss.t[:, t:t + 1], in_=junk.t[:, :], axis=mybir.AxisListType.X,
                                                           op=ALU.add), reads=[junk], writes=[ss])
            rstd_from_ss(ss, lnv, rstd, 4, 1.0 / D)
            for t in range(4):
                k.op("dve", lambda e, t=t: e.scalar_tensor_tensor(
                    out=X.t[:, t, :], in0=X.t[:, t, :], scalar=rstd.t[:, t:t + 1], in1=gb.t[:, :], op0=ALU.mult, op1=ALU.mult),
                    reads=[X, rstd, gb], writes=[X])
            k.dma("sp", out_d[r0:r0 + n, :].rearrange("(t p) d -> p t d", p=P), X.t[:, :, :], X,
                  reads=[X], writes=[k.dres("out", bi)])
        k.barrier()
        pes.close()

    k.barrier()
    for li, l in enumerate(layers):
        last = (l == L - 1)
        steps = [("mod", lambda: phase_mod(l)), ("A", lambda: phase_A(l)), ("attn0", lambda: phase_attn(l, 0, last)),
                 ("attn1", lambda: phase_attn(l, 1, last)), ("merge", lambda: phase_merge(l, last)),
                 ("ffn0", lambda: phase_ffn(l, 0, last)), ("ffn1", lambda: phase_ffn(l, 1, last))]
        stop = False
        for name, fn in steps:
            fn()
            if stop_after == name:
                stop = True
                break
        if stop:
            break
    if final_norm and stop_after is None:
        phase_final()
    k.barrier()
    es.close()
    return nc


def _rope_tables():
    t = np.arange(T)
    row = (t // GRID_W).astype(np.float32)
    col = (t % GRID_W).astype(np.float32)
    inv = (10000.0 ** (-np.arange(0, 32, 2, dtype=np.float32) / 32.0)).astype(np.float32)
    cosT = np.zeros((P, T), np.float32)
    sinT = np.zeros((P, T), np.float32)
    for p in range(P):
        j = p % 64
        axis, half, i = j // 32, (j % 32) // 16, j % 16
        ang = ((row if axis == 0 else col) * inv[i]).astype(np.float32)
        cosT[p] = np.cos(ang)
        sinT[p] = np.sin(ang) * (-1.0 if half == 0 else 1.0)
    return cosT, sinT


def _partner(j):
    return j + 16 if (j % 32) < 16 else j - 16


def _consts():
    c = np.zeros((P, 640), np.float32)
    c[:, 0:128] = np.eye(P, dtype=np.float32)
    for m in range(P):
        c[(m // 64) * 64 + _partner(m % 64), 128 + m] = 1.0
    c[0:64, 256:320] = 1.0
    c[64:128, 320:384] = 1.0
    kp = np.arange(P)[:, None]
    qp = np.arange(P)[None, :]
    c[:, 384:512] = (kp >= qp)
    c[:, 512:640] = (kp <= qp)
    sel = np.zeros((2, 256), np.float32)
    sel[0, 0:128] = 1.0
    sel[1, 128:256] = 1.0
    return c, sel


def _prep_inputs(inp):
    f = lambda a: np.ascontiguousarray(np.asarray(a, dtype=np.float32))
    w_in = f(inp["w_in"])
    perm = np.concatenate([np.arange(0, 512), np.arange(768, 1280), np.arange(512, 640), np.arange(1280, 1408),
                           np.arange(640, 768), np.arange(1408, 1536), np.arange(1536, 3584)])
    w_in_p = np.ascontiguousarray(w_in[:, :, perm])
    small = np.zeros((L, P, 32), np.float32)
    part = np.array([_partner(j) for j in range(64)])
    for l in range(L):
        small[l, :, 0:8] = f(inp["norm1_g"])[l].reshape(8, P).T
        small[l, :, 8:16] = f(inp["norm2_g"])[l].reshape(8, P).T
        qg = f(inp["q_norm_g"])[l]
        kg = f(inp["k_norm_g"])[l]
        small[l, :, 16] = np.tile(qg, 2)
        small[l, :, 17] = np.tile(qg[part], 2)
        small[l, :, 18] = np.tile(kg, 2)
        small[l, :, 19] = np.tile(kg[part], 2)
    cosT, sinT = _rope_tables()
    consts, sel = _consts()
    shared = {
        "w_ada": f(inp["w_ada"]), "b_ada": f(inp["b_ada"]), "small": small, "sink": f(inp["sink_a"]),
        "w_in": w_in_p, "w_pa": f(inp["w_proj_a"]), "w_pb": f(inp["w_proj_b"]), "w_o": f(inp["w_out"]),
        "w_g": f(inp["w_ffn_gate"]), "w_u": f(inp["w_ffn_up"]), "w_d": f(inp["w_ffn_down"]),
        "gfin": f(inp["final_norm_g"]).reshape(1, D), "consts": consts, "sel": sel, "cosT": cosT, "sinT": sinT,
    }
    return shared


_NC_CACHE = {}


def _get_nc(layers, final_norm):
    key = (tuple(layers), final_norm)
    if key not in _NC_CACHE:
        _NC_CACHE[key] = build(list(layers), final_norm)
    return _NC_CACHE[key]


FUSED = True


def kernel(**inp):
    shared = _prep_inputs(inp)
    x = np.asarray(inp["x"], dtype=np.float32)
    ctx = np.asarray(inp["ctx"], dtype=np.float32)
    c = np.asarray(inp["c"], dtype=np.float32)
    c_ctx = np.asarray(inp["c_ctx"], dtype=np.float32)
    B = x.shape[0]
    xs = [np.concatenate([x[b], ctx[b]], axis=0) for b in range(B)]
    ccs = [np.stack([c[b], c_ctx], axis=0) for b in range(B)]
    groups = [list(range(L))] if FUSED else [[l] for l in range(L)]
    out = None
    for gi, layers in enumerate(groups):
        fin = (layers[-1] == L - 1)
        nc = _get_nc(layers, fin)
        in_maps = []
        for core in range(NCORES):
            b = core % B
            m = dict(shared)
            m["x"] = np.ascontiguousarray(xs[b])
            m["cc"] = np.ascontiguousarray(ccs[b])
            in_maps.append(m)
        res = run_bass_kernel_spmd(nc, in_maps, core_ids=list(range(NCORES)))
        if fin:
            out = np.stack([res.results[b]["out"] for b in range(B)], axis=0)
        else:
            xs = [res.results[b]["out"] for b in range(B)]
    return out.astype(np.float32)
```
